# Optimizing a Trainium2 kernel written in Bass

```python
import jax, jax.numpy as jnp
from jax import lax
import numpy as np

D_MODEL = 1024
BATCH = 8
SEQ = 2048
DEPTH = 4

MIX_WIDTH = 2 * D_MODEL
CHUNK = 128
GMLP_WIDTH = MIX_WIDTH // 2
GMLP_HEADS = 8
GMLP_HEAD_DIM = GMLP_WIDTH // GMLP_HEADS
SSD_WIDTH = MIX_WIDTH - GMLP_WIDTH
SSD_HEAD_DIM = 64
SSD_HEADS = SSD_WIDTH // SSD_HEAD_DIM
SSD_GROUPS = 4
HEADS_PER_GROUP = SSD_HEADS // SSD_GROUPS
D_STATE = 128
CONV_WIDTH = 4
CONV_DIM = SSD_WIDTH + 2 * SSD_GROUPS * D_STATE
IN_COLS = 3 * GMLP_WIDTH + SSD_WIDTH + CONV_DIM + SSD_HEADS
SPLITS = (GMLP_WIDTH, 2 * GMLP_WIDTH, 3 * GMLP_WIDTH,
          3 * GMLP_WIDTH + SSD_WIDTH, 3 * GMLP_WIDTH + SSD_WIDTH + CONV_DIM)
EPS = 1e-6
DT_MIN = 1e-3
DT_MAX = 1e-1

kernel_name = "hymba_gmlp_ssd_hybrid"


def rms_norm(x, w):
    xf = x.astype(jnp.float32)
    y = xf * lax.rsqrt(jnp.mean(xf * xf, axis=-1, keepdims=True) + EPS)
    return (y * w.astype(jnp.float32)).astype(x.dtype)


def grouped_rms_norm(x, w, group_size):
    shp = x.shape
    xf = x.astype(jnp.float32).reshape(*shp[:-1], shp[-1] // group_size, group_size)
    y = xf * lax.rsqrt(jnp.mean(xf * xf, axis=-1, keepdims=True) + EPS)
    return (y.reshape(shp) * w.astype(jnp.float32)).astype(x.dtype)


def layer_norm(x, w, b):
    xf = x.astype(jnp.float32)
    mu = jnp.mean(xf, axis=-1, keepdims=True)
    xc = xf - mu
    y = xc * lax.rsqrt(jnp.mean(xc * xc, axis=-1, keepdims=True) + EPS)
    return (y * w.astype(jnp.float32) + b.astype(jnp.float32)).astype(x.dtype)


def causal_depthwise_conv(x, w, b):
    k, c = w.shape
    out = lax.conv_general_dilated(
        x, w[:, None, :].astype(x.dtype), window_strides=(1,), padding=[(k - 1, 0)],
        dimension_numbers=('NWC', 'WIO', 'NWC'), feature_group_count=c)
    return out + b.astype(x.dtype)


def spatial_gating(u, v, zg, v_norm_w, v_norm_b, ws, bs, out_norm_w):
    bsz, seq, _ = u.shape
    nc = seq // CHUNK
    vn = layer_norm(v, v_norm_w, v_norm_b).reshape(bsz, nc, CHUNK, GMLP_HEADS, GMLP_HEAD_DIM)
    causal = jnp.tril(jnp.ones((CHUNK, CHUNK), dtype=bool))
    ws_c = jnp.where(causal[None], ws, 0.0)
    s = jnp.einsum('hts,bcshd->bcthd', ws_c.astype(vn.dtype), vn) + bs.T[None, None, :, :, None]
    s = s.reshape(bsz, seq, GMLP_WIDTH)
    y = u * s * jax.nn.silu(zg)
    return grouped_rms_norm(y, out_norm_w, GMLP_HEAD_DIM)


def ssd_mixer(xbc, z, dt_raw, conv_w, conv_b, dt_bias, a_log, d_skip, norm_w):
    bsz, seq, _ = xbc.shape
    nc = seq // CHUNK
    f32 = jnp.float32
    xbc = jax.nn.silu(causal_depthwise_conv(xbc, conv_w, conv_b))
    xs, bm, cm = jnp.split(xbc, [SSD_WIDTH, SSD_WIDTH + SSD_GROUPS * D_STATE], axis=-1)
    xs = xs.astype(f32).reshape(bsz, nc, CHUNK, SSD_GROUPS, HEADS_PER_GROUP, SSD_HEAD_DIM)
    bm = bm.astype(f32).reshape(bsz, nc, CHUNK, SSD_GROUPS, D_STATE)
    cm = cm.astype(f32).reshape(bsz, nc, CHUNK, SSD_GROUPS, D_STATE)
    dt = jax.nn.softplus(dt_raw.astype(f32) + dt_bias.astype(f32))
    dt = dt.reshape(bsz, nc, CHUNK, SSD_GROUPS, HEADS_PER_GROUP)
    a = -jnp.exp(a_log.astype(f32)).reshape(SSD_GROUPS, HEADS_PER_GROUP)
    da_cs = jnp.cumsum(dt * a, axis=2)
    xdt = xs * dt[..., None]
    causal = jnp.tril(jnp.ones((CHUNK, CHUNK), dtype=bool))[None, None, :, :, None, None]
    seg = da_cs[:, :, :, None] - da_cs[:, :, None, :]
    decay = jnp.exp(jnp.where(causal, seg, -jnp.inf))
    cb = jnp.einsum('bclgn,bcsgn->bclsg', cm, bm)
    y_diag = jnp.einsum('bclsg,bclsgr,bcsgrp->bclgrp', cb, decay, xdt)
    decay_to_end = jnp.exp(da_cs[:, :, -1:] - da_cs)
    states = jnp.einsum('bclgn,bclgr,bclgrp->bcgrpn', bm, decay_to_end, xdt)
    chunk_decay = jnp.exp(da_cs[:, :, -1])

    def step(carry, inp):
        st, dec = inp
        return carry * dec[..., None, None] + st, carry

    init = jnp.zeros_like(states[:, 0])
    _, prev_states = lax.scan(step, init, (jnp.moveaxis(states, 1, 0), jnp.moveaxis(chunk_decay, 1, 0)))
    prev_states = jnp.moveaxis(prev_states, 0, 1)
    y_off = jnp.einsum('bclgn,bcgrpn,bclgr->bclgrp', cm, prev_states, jnp.exp(da_cs))
    d = d_skip.astype(f32).reshape(SSD_GROUPS, HEADS_PER_GROUP)[:, :, None]
    y = (y_diag + y_off + xs * d).reshape(bsz, seq, SSD_WIDTH).astype(z.dtype)
    return grouped_rms_norm(y * jax.nn.silu(z), norm_w, SSD_WIDTH // SSD_GROUPS)


def hybrid_layer(x, pre_w, w_in, v_norm_w, v_norm_b, ws, bs, gmlp_norm_w,
                 conv_w, conv_b, dt_bias, a_log, d_skip, ssd_norm_w, w_out, post_w):
    h = rms_norm(x, pre_w)
    proj = jnp.einsum('bld,dc->blc', h, w_in)
    u, v, zg, z, xbc, dt_raw = jnp.split(proj, SPLITS, axis=-1)
    y_a = spatial_gating(u, v, zg, v_norm_w, v_norm_b, ws, bs, gmlp_norm_w)
    y_b = ssd_mixer(xbc, z, dt_raw, conv_w, conv_b, dt_bias, a_log, d_skip, ssd_norm_w)
    mixed = jnp.einsum('blc,cd->bld', jnp.concatenate([y_a, y_b], axis=-1), w_out)
    return x + rms_norm(mixed, post_w)


def setup_inputs(seed: int = 0) -> dict:
    key = jax.random.key(seed)
    ks = jax.random.split(key, 18)
    nrm = jax.random.normal
    dt = jnp.exp(jax.random.uniform(ks[10], (DEPTH, SSD_HEADS)) * (np.log(DT_MAX) - np.log(DT_MIN)) + np.log(DT_MIN))
    return {
        "x": nrm(ks[0], (BATCH, SEQ, D_MODEL), jnp.float32),
        "pre_norm_w": 1.0 + 0.02 * nrm(ks[1], (DEPTH, D_MODEL)),
        "w_in": nrm(ks[2], (DEPTH, D_MODEL, IN_COLS)) * D_MODEL ** -0.5,
        "gmlp_v_norm_w": 1.0 + 0.02 * nrm(ks[3], (DEPTH, GMLP_WIDTH)),
        "gmlp_v_norm_b": 0.02 * nrm(ks[4], (DEPTH, GMLP_WIDTH)),
        "gmlp_ws": nrm(ks[5], (DEPTH, GMLP_HEADS, CHUNK, CHUNK)) * CHUNK ** -0.5,
        "gmlp_bs": 1.0 + 0.02 * nrm(ks[6], (DEPTH, GMLP_HEADS, CHUNK)),
        "gmlp_norm_w": 1.0 + 0.02 * nrm(ks[7], (DEPTH, GMLP_WIDTH)),
        "conv_w": nrm(ks[8], (DEPTH, CONV_WIDTH, CONV_DIM)) * CONV_WIDTH ** -0.5,
        "conv_b": 0.02 * nrm(ks[9], (DEPTH, CONV_DIM)),
        "dt_bias": dt + jnp.log(-jnp.expm1(-dt)),
        "a_log": jnp.log(jax.random.uniform(ks[11], (DEPTH, SSD_HEADS), minval=1.0, maxval=16.0)),
        "d_skip": 1.0 + 0.02 * nrm(ks[12], (DEPTH, SSD_HEADS)),
        "ssd_norm_w": 1.0 + 0.02 * nrm(ks[13], (DEPTH, SSD_WIDTH)),
        "w_out": nrm(ks[14], (DEPTH, MIX_WIDTH, D_MODEL)) * MIX_WIDTH ** -0.5,
        "post_norm_w": 1.0 + 0.02 * nrm(ks[15], (DEPTH, D_MODEL)),
    }


def reference(x, pre_norm_w, w_in, gmlp_v_norm_w, gmlp_v_norm_b, gmlp_ws, gmlp_bs, gmlp_norm_w,
              conv_w, conv_b, dt_bias, a_log, d_skip, ssd_norm_w, w_out, post_norm_w):
    h = x
    for i in range(DEPTH):
        h = hybrid_layer(h, pre_norm_w[i], w_in[i], gmlp_v_norm_w[i], gmlp_v_norm_b[i],
                         gmlp_ws[i], gmlp_bs[i], gmlp_norm_w[i], conv_w[i], conv_b[i],
                         dt_bias[i], a_log[i], d_skip[i], ssd_norm_w[i], w_out[i], post_norm_w[i])
    return h
```

```python
import numpy as np
from contextlib import ExitStack
import concourse.bass as bass
import concourse.mybir as mybir
from concourse.bass_utils import run_bass_kernel_spmd

F32 = mybir.dt.float32
BF16 = mybir.dt.bfloat16
AF = mybir.ActivationFunctionType
ALU = mybir.AluOpType
AX = mybir.AxisListType

D = 1024
T = 128
INC = 6160
EPS = 1e-6
NEG = -60000.0
import os
DBG = int(os.environ.get("KDBG", "99"))
ENGS = ("tensor", "vector", "scalar", "gpsimd", "sync")


class _Fake:
    def __init__(self):
        self.name = None
        self.kw = {}

    def __getattr__(self, name):
        def f(*a, **kw):
            self.name = name
            self.kw = kw
            return None
        return f


def _fsize(ap):
    n = 1
    for d in list(ap.shape)[1:]:
        n *= int(d)
    return n


def op_cost(eng, fn, kind):
    if kind == "dma":
        return 2.5
    fk = _Fake()
    try:
        fn(fk)
    except Exception:
        return 0.5
    kw = fk.kw
    n = 128
    for key in ("in_", "in0", "rhs", "out", "data1"):
        if key in kw and hasattr(kw[key], "shape"):
            n = _fsize(kw[key])
            break
    if eng == "tensor":
        if fk.name == "transpose":
            return 0.12
        return 0.07 + n / 2400.0 + 0.05
    if eng == "vector":
        return 0.06 + (150 + n) / 960.0
    if eng == "scalar":
        return 0.17 + n / 1200.0 + (0.15 if "accum_out" in kw else 0.0)
    if eng == "gpsimd":
        return 0.8 + n / 800.0
    return 0.1


class Buf:
    __slots__ = ("name", "writer", "readers", "excl", "wreal")

    def __init__(self, name, excl=False):
        self.name = name
        self.writer = None
        self.readers = []
        self.wreal = True
        self.excl = excl


class Item:
    __slots__ = ("fn", "waits", "kind", "idx", "dkey")

    def __init__(self, fn, waits, kind, idx, dkey=None):
        self.fn, self.waits, self.kind, self.idx, self.dkey = fn, waits, kind, idx, dkey


class Sched:
    def __init__(self):
        self.items = {e: [] for e in ENGS}
        self.cnt = {e: 0 for e in ENGS}
        self.seen = {e: {} for e in ENGS}
        self.dma_cnt = {}
        self.signal = {e: set() for e in ENGS}
        self.rec = None
        self.efree = {e: 0.0 for e in ENGS}
        self.done = {}

    def peek(self, reads, writes):
        deps = []
        for b in reads:
            if b.writer is not None:
                deps.append(b.writer)
            if b.excl:
                deps.extend(b.readers)
        for b in writes:
            if b.writer is not None:
                deps.append(b.writer)
            deps.extend(b.readers)
        return deps

    def est_start(self, eng, reads, writes):
        t = self.efree[eng]
        for tok in self.peek(reads, writes):
            d = self.done.get(tok, 0.0)
            if not (tok[0] == "e" and tok[1] == eng):
                d += 0.15
            t = max(t, d)
        return t

    def record(self, fns):
        self.rec = []
        for fn in fns:
            fn()
        ops, self.rec = self.rec, None
        return ops

    def play(self, o):
        kind, eng, fn, key, r, w = o
        st = self.est_start(eng, r, w)
        c = op_cost(eng, fn, kind)
        if kind == "op":
            tok = self.op(eng, fn, r, w)
            self.efree[eng] = st + c
            self.done[tok] = st + c
        else:
            tok = self.dma(eng, fn, key, r, w)
            self.efree[eng] = st + 0.1
            self.done[tok] = st + c
        return tok

    def merge(self, X, Y, ybar, xbar):
        i = j = 0
        while i < len(X) or j < len(Y):
            if i >= len(X):
                self.play(Y[j]); j += 1
                continue
            if j >= len(Y):
                self.play(X[i]); i += 1
                continue
            if i >= xbar and j < ybar:
                self.play(Y[j]); j += 1
                continue
            sx = self.est_start(X[i][1], X[i][4], X[i][5])
            sy = self.est_start(Y[j][1], Y[j][4], Y[j][5])
            if sx <= sy:
                self.play(X[i]); i += 1
            else:
                self.play(Y[j]); j += 1

    def _collect(self, eng, reads, writes, is_dma):
        deps = []
        for b in reads:
            if b.writer is not None:
                deps.append(b.writer)
            if b.excl:
                deps.extend(b.readers)
        for b in writes:
            if b.writer is not None:
                deps.append(b.writer)
            deps.extend(b.readers)
        waits = []
        for tok in deps:
            if tok[0] == "e":
                _, src, idx = tok
                if src == eng and not is_dma:
                    if eng == "tensor" or not any((b.writer == tok and b.wreal) for b in reads):
                        continue
                if self.seen[eng].get(src, -1) >= idx:
                    continue
                self.seen[eng][src] = idx
                self.signal[src].add(idx)
                waits.append(tok)
            else:
                _, key, val = tok
                if self.seen[eng].get(("d", key), 0) >= val:
                    continue
                self.seen[eng][("d", key)] = val
                waits.append(tok)
        best = {}
        for tok in waits:
            k = tok[1]
            if k not in best or tok[2] > best[k][2]:
                best[k] = tok
        return list(best.values())

    def op(self, eng, fn, r=(), w=()):
        if self.rec is not None:
            self.rec.append(("op", eng, fn, None, tuple(r), tuple(w)))
            return None
        waits = self._collect(eng, r, w, False)
        idx = self.cnt[eng]
        self.cnt[eng] += 1
        tok = ("e", eng, idx)
        self.items[eng].append(Item(fn, waits, "c", idx))
        for b in r:
            if b.excl:
                b.writer = tok
                b.wreal = False
                b.readers = []
            else:
                b.readers.append(tok)
        for b in w:
            b.writer = tok
            b.wreal = True
            b.readers = []
        return tok

    def dma(self, eng, fn, key, r=(), w=()):
        if self.rec is not None:
            self.rec.append(("dma", eng, fn, key, tuple(r), tuple(w)))
            return None
        waits = self._collect(eng, r, w, True)
        n = self.dma_cnt.get(key, 0) + 1
        self.dma_cnt[key] = n
        tok = ("d", key, 16 * n)
        self.items[eng].append(Item(fn, waits, "d", None, key))
        for b in r:
            b.readers.append(tok)
        for b in w:
            b.writer = tok
            b.readers = []
        return tok

    def emit(self, nc, es):
        esem = {e: es.enter_context(nc.semaphore("es_" + e)) for e in ENGS if e != "sync"}
        dsem = {k: es.enter_context(nc.semaphore("ds_" + str(k))) for k in self.dma_cnt}
        rank = {}
        for e in ENGS:
            for i, idx in enumerate(sorted(self.signal[e])):
                rank[(e, idx)] = i + 1
        final_dma = [(dsem[k], 16 * n) for k, n in self.dma_cnt.items()]

        def run(ename, e):
            for it in self.items[ename]:
                for tok in it.waits:
                    if tok[0] == "e":
                        e.wait_ge(esem[tok[1]], rank[(tok[1], tok[2])])
                    else:
                        e.wait_ge(dsem[tok[1]], tok[2])
                ins = it.fn(e)
                if it.kind == "c":
                    if it.idx in self.signal[ename]:
                        ins.then_inc(esem[ename], 1)
                else:
                    ins.then_inc(dsem[it.dkey], 16)
            if ename == "sync":
                for s, v in final_dma:
                    e.wait_ge(s, v)

        with nc.Block() as block:
            @block.tensor
            def _(e):
                run("tensor", e)

            @block.vector
            def _(e):
                run("vector", e)

            @block.scalar
            def _(e):
                run("scalar", e)

            @block.gpsimd
            def _(e):
                run("gpsimd", e)

            @block.sync
            def _(e):
                run("sync", e)


def AP(t, off, dims):
    return bass.AP(t, off, [list(d) for d in dims])


def build_program(NL, NCH):
    nc = bass.Bass("TRN2", target_bir_lowering=False)
    L = NCH * T

    def din(name, shape):
        return nc.dram_tensor(name, shape, F32, kind="ExternalInput").ap()

    x_d = din("x", [L, D])
    pre_d = din("pre_norm_w", [NL, D])
    win_d = din("w_in", [NL, D, INC])
    vw_d = din("gmlp_v_norm_w", [NL, D])
    vb_d = din("gmlp_v_norm_b", [NL, D])
    ws_d = din("gmlp_ws", [NL, 8, T, T])
    bs_d = din("gmlp_bs", [NL, 8, T])
    gnw_d = din("gmlp_norm_w", [NL, D])
    cw_d = din("conv_w", [NL, 4, 2048])
    cb_d = din("conv_b", [NL, 2048])
    dtb_d = din("dt_bias", [NL, 16])
    alog_d = din("a_log", [NL, 16])
    dsk_d = din("d_skip", [NL, 16])
    snw_d = din("ssd_norm_w", [NL, D])
    wout_d = din("w_out", [NL, 2 * D, D])
    post_d = din("post_norm_w", [NL, D])
    c_ident = din("c_ident", [T, T])
    c_tril = din("c_tril", [T, T])
    c_negm = din("c_negm", [T, 512])
    c_sel = din("c_sel", [T, 16 * T])
    c_one16 = din("c_one16", [T, T])
    out_d = nc.dram_tensor("out", [L, D], F32, kind="ExternalOutput").ap()
    xbuf_d = nc.dram_tensor("xbuf", [L, D], F32, kind="Internal").ap() if NL > 1 else None

    es = ExitStack()
    S = Sched()
    bufs = {}

    def B(name):
        if name not in bufs:
            bufs[name] = Buf(name)
        return bufs[name]

    def sb(name, shape, dt):
        return es.enter_context(nc.sbuf_tensor(name, shape, dt))

    win = sb("win", [128, 8, INC], BF16)
    wout = sb("wout", [128, 16, D], BF16)
    ident_bf = sb("ident_bf", [128, 128], BF16)
    ident_f = sb("ident_f", [128, 16], F32)
    negm = sb("negm", [128, 128], BF16)
    one16 = sb("one16", [128, 1], F32)
    mhalf = sb("mhalf", [128, 8], F32)
    post_bc = sb("post_bc", [128, D], F32)
    vw_bc = sb("vw_bc", [128, D], F32)
    bias_t = sb("bias_t", [128, D], F32)
    wsT = sb("wsT", [128, 8, T], BF16)
    prm = sb("prm", [128, 112], F32)
    dtb = sb("dtb", [128, 1], F32)
    alog = sb("alog", [128, 1], F32)
    aneg = sb("aneg", [128, 1], F32)
    d_bc = sb("d_bc", [128, 16], F32)
    rs = sb("rs", [128, 8], F32)

    x_sb = sb("x_sb", [128, D], F32)
    xbf = sb("xbf", [128, D], BF16)
    hT = [sb("hT%d" % i, [128, 8, 131], BF16) for i in range(2)]
    halo = sb("halo", [128, 8, 3], BF16)
    g_sb = sb("g_sb", [128, D], F32)
    sz = [sb("sz%d" % i, [128, D], F32) for i in range(2)]
    dtx = [sb("dtx%d" % i, [128, T], F32) for i in range(2)]
    y_sb = sb("y_sb", [128, D], F32)
    yn_bf = sb("yn_bf", [128, D], BF16)
    vn_bf = yn_bf
    yT = sb("yT", [128, 16, T], BF16)
    acc = [sb("acc%d" % i, [128, T], F32) for i in range(2)]
    xbc_fs = [sb("xbc_f%d" % i, [128, D], F32) for i in range(2)]
    xs_tm = sb("xs_tm", [128, D], BF16)
    B_tm = sb("B_tm", [128, 512], BF16)
    fm = sb("fm", [128, 5, T], F32)
    dacs_hi = sb("dacs_hi", [128, T], BF16)
    dacs_lo = sb("dacs_lo", [128, T], BF16)
    tm48 = sb("tm48", [128, 48], F32)
    tms = sb("tms", [128, 4, 16], F32)
    diagcd = sb("diagcd", [128, 16], F32)
    cbT_bf = sb("cbT_bf", [128, 512], BF16)
    E_f = sb("E_f", [128, D], F32)
    xdt_bf = sb("xdt_bf", [128, D], BF16)
    xdte_bf = sb("xdte_bf", [128, D], BF16)
    St = sb("St", [128, D], F32)
    St_bf = sb("St_bf", [128, D], BF16)
    st = sb("st", [128, 64], F32)
    bnst = sb("bnst", [128, 2, 6], F32)

    pb = [es.enter_context(nc.psum_tensor("pb%d" % i, [128, 512], F32)) for i in range(8)]
    pbB = [B("pb%d" % i) for i in range(8)]
    for b_ in pbB:
        b_.excl = True

    def pbf(i):
        return pb[i][:, :].bitcast(BF16)

    SSQ, MS, RSTD, RSTDV, NMR, MSV, RSTDM, MSM, SCR0, SCR1 = range(10)
    SSQG, MSG, RSTDG = 16, 24, 32
    SSQB, MSB, RSTDB = 40, 44, 48
    SSQM = 52
    MV = 56

    def stc(c, n=1):
        return st[:, c:c + n]

    S.dma("sync", lambda e: e.dma_start(out=ident_f[:, :], in_=c_ident[:, 0:16]), "c0", w=[B("ident_f")])
    S.op("vector", lambda e: e.memset(one16[:, :], 1.0), w=[B("one16")])
    S.dma("gpsimd", lambda e: e.dma_start(out=ident_bf[:, :], in_=c_ident), "c3", w=[B("ident_bf")])
    S.dma("gpsimd", lambda e: e.dma_start(out=negm[:, :], in_=c_negm[:, 0:128]), "c4", w=[B("negm")])
    S.op("vector", lambda e: e.memset(mhalf[:, :], -0.5), w=[B("mhalf")])
    S.op("vector", lambda e: e.memset(dacs_hi[:, :], 0.0), w=[B("dacs_hi")])
    S.op("vector", lambda e: e.memset(dacs_lo[:, :], 0.0), w=[B("dacs_lo")])
    S.op("vector", lambda e: e.memset(diagcd[:, :], 0.0), w=[B("diagcd")])
    S.op("vector", lambda e: e.memset(st[:, SCR0:SCR1 + 1], 0.0), w=[B("st_scr")])

    def rsqrt(src_c, dst_c, n, scale, rbuf, wbuf, tmp_c):
        S.op("vector", lambda e: e.tensor_scalar(out=stc(tmp_c, n), in0=stc(src_c, n), scalar1=scale,
                                                 scalar2=EPS, op0=ALU.mult, op1=ALU.add),
             r=[rbuf], w=[B("tmp%d" % tmp_c)])
        S.op("gpsimd", lambda e: e.tensor_tensor(out=stc(dst_c, n), in0=stc(tmp_c, n), in1=mhalf[:, 0:n],
                                                 op=ALU.pow),
             r=[B("tmp%d" % tmp_c), B("mhalf")], w=[wbuf])

    def do_layer(l, src_d, dst_d):
        winB = [B("win_k%d" % k) for k in range(8)]

        def load_win(ll):
            for k in range(8):
                S.dma("gpsimd", (lambda k: lambda e: e.dma_start(out=win[:, k, :], in_=win_d[ll, k * 128:(k + 1) * 128, :]))(k),
                      "win%d" % k, w=[winB[k]])
        if l == 0:
            load_win(0)
        if True:
            stgp = g_sb[0:16, 0:896].rearrange("p (g t) -> p g t", g=7)
            for kk in range(4):
                S.dma("sync", (lambda kk: lambda e: e.dma_start(out=stgp[:, kk, :], in_=cw_d[l, kk].rearrange("(j p) -> j p", p=128)))(kk),
                      "q%d" % kk, w=[B("g_sb")])
            S.dma("sync", lambda e: e.dma_start(out=stgp[:, 4, :], in_=cb_d[l].rearrange("(j p) -> j p", p=128)), "q4", w=[B("g_sb")])
            S.dma("sync", lambda e: e.dma_start(out=stgp[0:8, 5, :], in_=pre_d[l].rearrange("(j p) -> j p", p=128)), "q5", w=[B("g_sb")])
            S.dma("sync", lambda e: e.dma_start(out=g_sb[8:16, 640:768], in_=gnw_d[l].rearrange("(j p) -> j p", p=128)), "q6", w=[B("g_sb")])
            S.dma("sync", lambda e: e.dma_start(out=stgp[0:8, 6, :], in_=snw_d[l].rearrange("(j p) -> j p", p=128)), "q7", w=[B("g_sb")])
            S.dma("sync", lambda e: e.dma_start(out=g_sb[8:16, 768:896], in_=bs_d[l]), "q8", w=[B("g_sb")])
            for gi in range(7):
                S.op("tensor", (lambda gi: lambda e: e.transpose(out=pb[6][:, gi * 16:(gi + 1) * 16], in_=g_sb[0:16, gi * 128:(gi + 1) * 128],
                                                                identity=ident_f[0:16, 0:16]))(gi),
                     r=[B("g_sb"), B("ident_f")], w=[pbB[6]])
            S.op("vector", lambda e: e.tensor_copy(out=prm[:, :], in_=pb[6][:, 0:112]), r=[pbB[6]], w=[B("prm")])
            S.dma("sync", lambda e: e.dma_start(out=dtb[0:16, :], in_=dtb_d[l].rearrange("(h o) -> h o", o=1), allow_slow_non_contiguous=True), "p6", w=[B("dtb")])
            S.dma("sync", lambda e: e.dma_start(out=alog[0:16, :], in_=alog_d[l].rearrange("(h o) -> h o", o=1), allow_slow_non_contiguous=True), "p7", w=[B("alog")])
            S.dma("sync", lambda e: e.dma_start(out=d_bc[:, :], in_=dsk_d[l].partition_broadcast(128), allow_slow_non_contiguous=True), "p8", w=[B("d_bc")])
            S.dma("sync", lambda e: e.dma_start(out=post_bc[:, :], in_=post_d[l].partition_broadcast(128), allow_slow_non_contiguous=True), "p9", w=[B("post_bc")])
            S.dma("sync", lambda e: e.dma_start(out=vw_bc[:, :], in_=vw_d[l].partition_broadcast(128), allow_slow_non_contiguous=True), "p10", w=[B("vw_bc")])
            S.dma("sync", lambda e: e.dma_start(out=sz[0][:, :], in_=vb_d[l].partition_broadcast(128), allow_slow_non_contiguous=True), "p11", w=[B("sz0")])
            S.dma("sync", lambda e: e.dma_start(out=y_sb[:, :].rearrange("t (h s) -> t h s", h=8),
                                               in_=ws_d[l].rearrange("h t s -> t h s"), allow_slow_non_contiguous=True), "p12", w=[B("y_sb")])
        S.op("scalar", lambda e: e.activation(out=aneg[0:16, :], in_=alog[0:16, :], func=AF.Exp), r=[B("alog")], w=[B("aneg")])
        S.op("vector", lambda e: e.tensor_scalar(out=aneg[0:16, :], in0=aneg[0:16, :], scalar1=-1.0, scalar2=None, op0=ALU.mult),
             r=[B("aneg")], w=[B("aneg")])
        S.dma("sync", lambda e: e.dma_start(out=St[:, 0:128], in_=c_tril), "c1", w=[B("St")])
        ws3 = y_sb[:, :].rearrange("t (h s) -> t h s", h=8)
        S.op("vector", lambda e: e.tensor_tensor(out=ws3, in0=ws3, in1=AP(St, 0, [[1024, 128], [0, 8], [1, 128]]), op=ALU.mult),
             r=[B("y_sb"), B("St")], w=[B("y_sb")])
        S.op("vector", lambda e: e.tensor_reduce(out=rs[:, :], in_=ws3, axis=AX.X, op=ALU.add), r=[B("y_sb")], w=[B("rs")])
        S.op("vector", lambda e: e.tensor_copy(out=yn_bf[:, :], in_=y_sb[:, :]), r=[B("y_sb")], w=[B("yn_bf")])
        for h in range(8):
            bk = 4 + h // 4
            S.op("tensor", (lambda h, bk: lambda e: e.transpose(out=pbf(bk)[:, (h % 4) * 128:(h % 4 + 1) * 128],
                                                               in_=yn_bf[:, h * 128:(h + 1) * 128], identity=ident_bf[:, :]))(h, bk),
                 r=[B("yn_bf"), B("ident_bf")], w=[pbB[bk]])
        for q in range(2):
            S.op("vector", (lambda q: lambda e: e.tensor_copy(out=wsT[:, 4 * q:4 * q + 4, :],
                                                              in_=pbf(4 + q)[:, 0:512].rearrange("p (h t) -> p h t", h=4)))(q),
                 r=[pbB[4 + q]], w=[B("wsT")])
        for h in range(8):
            S.op("vector", (lambda h: lambda e: e.tensor_scalar(out=bias_t[:, h * 128:(h + 1) * 128], in0=sz[0][:, h * 128:(h + 1) * 128],
                                                                scalar1=rs[:, h:h + 1], scalar2=prm[:, 104 + h:105 + h],
                                                                op0=ALU.mult, op1=ALU.add))(h),
                 r=[B("sz0"), B("rs"), B("prm")], w=[B("bias_t")])
        stg = [(g_sb, B("g_sb")), (y_sb, B("y_sb"))]
        for k in range(16):
            tt, tb = stg[k % 2]
            S.dma("sync", (lambda k, tt: lambda e: e.dma_start(out=tt[:, :], in_=wout_d[l, k * 128:(k + 1) * 128, :]))(k, tt),
                  "wo%d" % (k % 2), w=[tb])
            if k % 2 == 0:
                S.op("vector", (lambda k, tt: lambda e: e.tensor_scalar(out=wout[:, k, :], in0=tt[:, :], scalar1=prm[:, 88 + k:89 + k],
                                                                        scalar2=None, op0=ALU.mult))(k, tt),
                     r=[tb, B("prm")], w=[B("wout")])
            else:
                S.op("scalar", (lambda k, tt: lambda e: e.activation(out=wout[:, k, :], in_=tt[:, :], func=AF.Copy,
                                                                     scale=prm[:, 88 + k:89 + k]))(k, tt),
                     r=[tb, B("prm")], w=[B("wout")])
        S.op("gpsimd", lambda e: e.memset(St[:, :], 0.0), w=[B("St")])
        S.op("gpsimd", lambda e: e.memset(St_bf[:, :], 0.0), w=[B("St_bf")])
        S.op("gpsimd", lambda e: e.memset(halo[:, :, :], 0.0), w=[B("halo")])

        XR_, AB, E1, L1, DT, DACS = None, 0, 1, 2, 3, 4
        DTA = AB
        Ebf = E_f[:, :].bitcast(BF16).rearrange("p (h l) -> p h l", h=16)
        EB = B("E_f")

        def f(i):
            return fm[0:16, i, :]

        def stageA(c):
            par = c % 2
            hTc, hB = hT[par], B("hT%d" % par)
            szc, szB = sz[par], B("sz%d" % par)
            dtxc, dtxB = dtx[par], B("dtx%d" % par)
            dramB = B("dram%d" % c)
            r0 = c * T
            xB = B("x_sb")
            xbc_f = xbc_fs[par]
            xbcT = xbc_f[:, :].bitcast(BF16).rearrange("p (j t) -> p j t", j=16)
            XB = B("xbc_f%d" % par)

            def A1():
                S.dma("sync", lambda e: e.dma_start(out=x_sb[:, :], in_=src_d[r0:r0 + T, :]), "xl", r=[dramB], w=[xB])
                S.op("scalar", lambda e: e.activation(out=xbf[:, :], in_=x_sb[:, :], func=AF.Square, accum_out=stc(SSQ)),
                     r=[xB], w=[B("xbf"), B("ssq")])
                S.op("scalar", lambda e: e.copy(out=stc(SCR0), in_=stc(SCR1)), r=[B("st_scr")], w=[B("ssq")])
                rsqrt(SSQ, RSTD, 1, 1.0 / D, B("ssq"), B("rstd"), MS)
                S.op("vector", lambda e: e.tensor_scalar(out=xbf[:, :], in0=x_sb[:, :], scalar1=stc(RSTD), scalar2=None, op0=ALU.mult),
                     r=[xB, B("rstd")], w=[B("xbf")])
                for k in range(8):
                    S.op("tensor", (lambda k: lambda e: e.transpose(out=pbf(0)[:, k * 128:(k + 1) * 128],
                                                                   in_=xbf[:, k * 128:(k + 1) * 128], identity=ident_bf[:, :]))(k),
                         r=[B("xbf"), B("ident_bf")], w=[pbB[0]])
                S.op("gpsimd", lambda e: e.tensor_copy(out=hTc[:, :, 0:3], in_=halo[:, :, :]), r=[B("halo")], w=[hB])
                S.op("vector", lambda e: e.tensor_tensor(out=hTc[:, :, 3:131], in0=pbf(0).rearrange("p (k t) -> p k t", k=8),
                                                         in1=AP(prm, 80, [[112, 128], [1, 8], [0, 128]]), op=ALU.mult),
                     r=[pbB[0], B("prm")], w=[hB])
                S.op("gpsimd", lambda e: e.tensor_copy(out=halo[:, :, :], in_=hTc[:, :, 128:131]), r=[hB], w=[B("halo")])

            def proj(bank, col0):
                for k in range(8):
                    S.op("tensor", (lambda k: lambda e: e.matmul(out=pb[bank][:, :], lhsT=hTc[:, k, 3:131],
                                                                 rhs=win[:, k, col0:col0 + 512], start=(k == 0), stop=(k == 7)))(k),
                         r=[hB, winB[k]], w=[pbB[bank]])

            def A2():
                proj(2, 1024)
                proj(3, 1536)
                S.op("vector", lambda e: e.bn_stats(out=bnst[:, 0, :], in_=pb[2][:, :]), r=[pbB[2]], w=[B("bnst")])
                S.op("vector", lambda e: e.bn_stats(out=bnst[:, 1, :], in_=pb[3][:, :]), r=[pbB[3]], w=[B("bnst")])
                S.op("vector", lambda e: e.bn_aggr(out=stc(MV, 2), in_=bnst[:, :, :].rearrange("p a b -> p (a b)")),
                     r=[B("bnst")], w=[B("mv")])
                S.op("vector", lambda e: e.tensor_scalar(out=stc(MSV), in0=stc(MV + 1), scalar1=EPS, scalar2=None, op0=ALU.add),
                     r=[B("mv")], w=[B("msv")])
                S.op("gpsimd", lambda e: e.tensor_tensor(out=stc(RSTDV), in0=stc(MSV), in1=mhalf[:, 0:1], op=ALU.pow),
                     r=[B("msv"), B("mhalf")], w=[B("rstdv")])
                S.op("vector", lambda e: e.scalar_tensor_tensor(out=stc(NMR), in0=stc(MV), scalar=-1.0, in1=stc(RSTDV),
                                                                op0=ALU.mult, op1=ALU.mult),
                     r=[B("mv"), B("rstdv")], w=[B("nmr")])
                for i, bank in enumerate((2, 3)):
                    S.op("scalar", (lambda i, bank: lambda e: e.activation(out=y_sb[:, i * 512:(i + 1) * 512], in_=pb[bank][:, :],
                                                                           func=AF.Identity, bias=stc(NMR), scale=stc(RSTDV)))(i, bank),
                         r=[pbB[bank], B("nmr"), B("rstdv")], w=[B("y_sb")])
                S.op("vector", lambda e: e.tensor_tensor(out=vn_bf[:, :], in0=y_sb[:, :], in1=vw_bc[:, :], op=ALU.mult),
                     r=[B("y_sb"), B("vw_bc")], w=[B("yn_bf")])

            def A3():
                proj(2, 2048)
                proj(3, 2560)
                for i, bank in enumerate((2, 3)):
                    S.op("scalar", (lambda i, bank: lambda e: e.activation(out=g_sb[:, i * 512:(i + 1) * 512], in_=pb[bank][:, :],
                                                                           func=AF.Silu))(i, bank),
                         r=[pbB[bank]], w=[B("g_sb")])
                proj(2, 0)
                proj(3, 512)
                for i, bank in enumerate((2, 3)):
                    S.op("vector", (lambda i, bank: lambda e: e.tensor_tensor(out=g_sb[:, i * 512:(i + 1) * 512], in0=g_sb[:, i * 512:(i + 1) * 512],
                                                                              in1=pb[bank][:, :], op=ALU.mult))(i, bank),
                         r=[pbB[bank], B("g_sb")], w=[B("g_sb")])

            def A4():
                for h in range(8):
                    bk = 2 + h // 4
                    S.op("tensor", (lambda h, bk: lambda e: e.matmul(out=pb[bk][:, (h % 4) * 128:(h % 4 + 1) * 128], lhsT=wsT[:, h, :],
                                                                     rhs=vn_bf[:, h * 128:(h + 1) * 128], start=True, stop=True))(h, bk),
                         r=[B("wsT"), B("yn_bf")], w=[pbB[bk]])
                for q in range(2):
                    S.op("vector", (lambda q: lambda e: e.tensor_tensor(out=y_sb[:, q * 512:(q + 1) * 512], in0=pb[2 + q][:, :],
                                                                        in1=bias_t[:, q * 512:(q + 1) * 512], op=ALU.add))(q),
                         r=[pbB[2 + q], B("bias_t")], w=[B("y_sb")])
                S.op("vector", lambda e: e.tensor_tensor(out=y_sb[:, :], in0=y_sb[:, :], in1=g_sb[:, :], op=ALU.mult),
                     r=[B("y_sb"), B("g_sb")], w=[B("y_sb")])
                for h in range(8):
                    S.op("scalar", (lambda h: lambda e: e.activation(out=xbf[:, h * 128:(h + 1) * 128], in_=y_sb[:, h * 128:(h + 1) * 128],
                                                                     func=AF.Square, accum_out=stc(SSQG + h)))(h),
                         r=[B("y_sb")], w=[B("xbf"), B("ssqg")])
                S.op("scalar", lambda e: e.copy(out=stc(SCR0), in_=stc(SCR1)), r=[B("st_scr")], w=[B("ssqg")])
                rsqrt(SSQG, RSTDG, 8, 1.0 / 128, B("ssqg"), B("rstdg"), MSG)
                S.op("vector", lambda e: e.tensor_tensor(out=yn_bf[:, :].rearrange("p (h d) -> p h d", h=8),
                                                         in0=y_sb[:, :].rearrange("p (h d) -> p h d", h=8),
                                                         in1=AP(st, RSTDG, [[64, 128], [1, 8], [0, 128]]), op=ALU.mult),
                     r=[B("y_sb"), B("rstdg")], w=[B("yn_bf")])

            def A4b():
                for h in range(8):
                    S.op("tensor", (lambda h: lambda e: e.transpose(out=pbf(0)[:, h * 128:(h + 1) * 128],
                                                                   in_=yn_bf[:, h * 128:(h + 1) * 128], identity=ident_bf[:, :]))(h),
                         r=[B("yn_bf"), B("ident_bf")], w=[pbB[0]])
                S.op("scalar", lambda e: e.copy(out=yT[:, 0:8, :], in_=pbf(0).rearrange("p (k t) -> p k t", k=8)),
                     r=[pbB[0]], w=[B("yTa")])

            def A5():
                for k in range(8):
                    S.op("tensor", (lambda k: lambda e: e.matmul(out=pb[0][0:16, 0:128], lhsT=win[:, k, 6144:6160],
                                                                 rhs=hTc[:, k, 3:131], start=(k == 0), stop=(k == 7)))(k),
                         r=[hB, winB[k]], w=[pbB[0]])
                S.op("vector", lambda e: e.tensor_scalar(out=dtxc[0:16, :], in0=pb[0][0:16, 0:128], scalar1=dtb[0:16, :], scalar2=None,
                                                         op0=ALU.add),
                     r=[pbB[0], B("dtb")], w=[dtxB])
                proj(2, 3072)
                proj(3, 3584)
                for i, bank in enumerate((2, 3)):
                    S.op("scalar", (lambda i, bank: lambda e: e.activation(out=szc[:, i * 512:(i + 1) * 512], in_=pb[bank][:, :],
                                                                           func=AF.Silu))(i, bank),
                         r=[pbB[bank]], w=[szB])
            def B1(j0, j1):
                for j in range(j0, j1):
                    bank = 2 + (j % 2)
                    a_t, aB = acc[j % 2], B("acc%d" % (j % 2))
                    c0 = 4096 + j * 128
                    for k in range(8):
                        S.op("tensor", (lambda k, bank, c0: lambda e: e.matmul(out=pb[bank][:, 0:131], lhsT=win[:, k, c0:c0 + 128],
                                                                             rhs=hTc[:, k, 0:131], start=(k == 0), stop=(k == 7)))(k, bank, c0),
                             r=[hB, winB[k]], w=[pbB[bank]])
                    rb = j % 2
                    S.op("scalar", (lambda j, bank, a_t: lambda e: e.activation(out=a_t[:, :], in_=pb[bank][:, 3:131], func=AF.Identity,
                                                                              scale=prm[:, 48 + j:49 + j], bias=prm[:, 64 + j:65 + j]))(j, bank, a_t),
                         r=[pbB[bank], B("prm")], w=[aB])
                    S.op("scalar", (lambda bank, rb: lambda e: e.copy(out=pb[rb][:, 256:387], in_=pb[bank][:, 0:131]))(bank, rb),
                         r=[pbB[bank]], w=[pbB[rb]])
                    for kk in (2, 1, 0):
                        S.op("vector", (lambda j, rb, a_t, kk: lambda e: e.scalar_tensor_tensor(
                            out=a_t[:, :], in0=pb[rb][:, 256 + kk:256 + kk + 128], scalar=prm[:, 16 * kk + j:16 * kk + j + 1], in1=a_t[:, :],
                            op0=ALU.mult, op1=ALU.add))(j, rb, a_t, kk),
                             r=[pbB[rb], B("prm"), aB], w=[aB])
                    S.op("scalar", (lambda j, a_t: lambda e: e.activation(out=xbcT[:, j, :], in_=a_t[:, :], func=AF.Silu))(j, a_t),
                         r=[aB], w=[XB])

            return dict(A1=A1, A2=A2, A3=A3, A4a=A4, A5=A5, B1=B1, A4b=A4b)

        def stageB(c):
            par = c % 2
            hTc, hB = hT[par], B("hT%d" % par)
            szc, szB = sz[par], B("sz%d" % par)
            dtxc, dtxB = dtx[par], B("dtx%d" % par)
            dramB = B("dram%d" % c)
            r0 = c * T
            XR = dtxc[0:16, :]
            DTE = XR
            M1 = pbB[7]
            xbc_f = xbc_fs[par]
            xbcT = xbc_f[:, :].bitcast(BF16).rearrange("p (j t) -> p j t", j=16)
            XB = B("xbc_f%d" % par)
            t1 = E_f
            t2 = xbc_f

            def B2():
                S.op("vector", lambda e: e.scalar_tensor_tensor(out=f(AB), in0=XR, scalar=-1.0, in1=XR, op0=ALU.mult, op1=ALU.max),
                     r=[dtxB], w=[B("fAB")])
                S.op("scalar", lambda e: e.activation(out=f(E1), in_=f(AB), func=AF.Exp, scale=-1.0), r=[B("fAB")], w=[B("fE1")])
                S.op("scalar", lambda e: e.activation(out=f(L1), in_=f(E1), func=AF.Ln, bias=1.0), r=[B("fE1")], w=[B("fL1")])
                S.op("vector", lambda e: e.scalar_tensor_tensor(out=f(DT), in0=XR, scalar=0.0, in1=f(L1), op0=ALU.max, op1=ALU.add),
                     r=[dtxB, B("fL1")], w=[B("fDT")])
                S.op("vector", lambda e: e.tensor_scalar(out=f(DTA), in0=f(DT), scalar1=aneg[0:16, :], scalar2=None, op0=ALU.mult),
                     r=[B("fDT"), B("aneg")], w=[B("fAB")])
                S.op("vector", lambda e: e.memset(f(E1), 1.0), r=[B("fE1")], w=[B("fE1")])
                S.op("vector", lambda e: e.tensor_tensor_scan(out=f(DACS), data0=f(E1), data1=f(DTA), initial=0.0,
                                                              op0=ALU.mult, op1=ALU.add),
                     r=[B("fE1"), B("fAB")], w=[B("fDACS")])
                S.op("scalar", lambda e: e.activation(out=DTE, in_=f(DACS), func=AF.Exp, scale=-1.0, bias=fm[0:16, DACS, 127:128]),
                     r=[B("fDACS")], w=[dtxB])
                S.op("vector", lambda e: e.tensor_copy(out=dacs_hi[0:16, :], in_=f(DACS)), r=[B("fDACS")], w=[B("dacs_hi")])
                S.op("vector", lambda e: e.tensor_tensor(out=dacs_lo[0:16, :], in0=f(DACS), in1=dacs_hi[0:16, :], op=ALU.subtract),
                     r=[B("fDACS"), B("dacs_hi")], w=[B("dacs_lo")])
                S.op("scalar", lambda e: e.activation(out=fm[0:16, L1, 0:1], in_=fm[0:16, DACS, 127:128], func=AF.Exp),
                     r=[B("fDACS"), B("fL1")], w=[B("fL1")])
                S.op("vector", lambda e: e.tensor_scalar(out=diagcd[0:16, :], in0=ident_f[0:16, 0:16], scalar1=fm[0:16, L1, 0:1],
                                                         scalar2=None, op0=ALU.mult),
                     r=[B("fL1"), B("ident_f")], w=[B("diagcd")])
                srcs = (f(DT), f(DACS), DTE)
                for i in range(3):
                    S.op("tensor", (lambda i: lambda e: e.transpose(out=pb[7][:, 128 + 16 * i:128 + 16 * (i + 1)], in_=srcs[i],
                                                                   identity=ident_f[0:16, 0:16]))(i),
                         r=[B("fDT"), B("fDACS"), dtxB, B("ident_f")], w=[M1])
                S.op("tensor", lambda e: e.matmul(out=pb[7][:, 176:192], lhsT=AP(one16, 0, [[1, 128], [0, 128]]), rhs=diagcd[:, :], start=True, stop=True),
                     r=[B("one16"), B("diagcd")], w=[M1])
                S.op("vector", lambda e: e.tensor_copy(out=tm48[:, :], in_=pb[7][:, 128:176]), r=[M1], w=[B("tm48")])
                S.op("vector", lambda e: e.tensor_copy(out=tms[:, 3, :], in_=pb[7][:, 176:192]), r=[M1], w=[B("cd_bc")])
                S.op("vector", lambda e: e.tensor_scalar(out=tms[:, 0, :], in0=tm48[:, 16:32], scalar1=-1.0, scalar2=None, op0=ALU.mult),
                     r=[B("tm48")], w=[B("negdacs")])
                S.op("scalar", lambda e: e.activation(out=tms[:, 1, :], in_=tm48[:, 16:32], func=AF.Exp), r=[B("tm48")], w=[B("edacs")])
                S.op("vector", lambda e: e.tensor_tensor(out=tms[:, 2, :], in0=tm48[:, 0:16], in1=tm48[:, 32:48], op=ALU.mult),
                     r=[B("tm48")], w=[B("wdte")])

            def B3():
                for j in range(8):
                    S.op("tensor", (lambda j: lambda e: e.transpose(out=pbf(4)[:, j * 128:(j + 1) * 128], in_=xbcT[:, j, :],
                                                                   identity=ident_bf[:, :]))(j),
                         r=[XB, B("ident_bf")], w=[pbB[4]])
                S.op("vector", lambda e: e.tensor_copy(out=xs_tm[:, :], in_=pbf(4)), r=[pbB[4]], w=[B("xs_tm")])
                for g in range(4):
                    S.op("tensor", (lambda g: lambda e: e.transpose(out=pbf(7)[:, 512 + g * 128:512 + (g + 1) * 128], in_=xbcT[:, 8 + g, :],
                                                                   identity=ident_bf[:, :]))(g),
                         r=[XB, B("ident_bf")], w=[M1])
                S.op("scalar", lambda e: e.copy(out=B_tm[:, :], in_=pbf(7)[:, 512:1024]), r=[M1], w=[B("B_tm")])
                S.op("gpsimd", lambda e: e.tensor_tensor(out=xdt_bf[:, :].rearrange("p (h d) -> p h d", h=16),
                                                         in0=xs_tm[:, :].rearrange("p (h d) -> p h d", h=16),
                                                         in1=AP(tm48, 0, [[48, 128], [1, 16], [0, 64]]), op=ALU.mult),
                     r=[B("xs_tm"), B("tm48")], w=[B("xdt_bf")])
                S.op("gpsimd", lambda e: e.tensor_tensor(out=xdte_bf[:, :].rearrange("p (h d) -> p h d", h=16),
                                                         in0=xs_tm[:, :].rearrange("p (h d) -> p h d", h=16),
                                                         in1=AP(tms, 32, [[64, 128], [1, 16], [0, 64]]), op=ALU.mult),
                     r=[B("xs_tm"), B("wdte")], w=[B("xdte_bf")])

            def B4():
                for g in range(4):
                    S.op("tensor", (lambda g: lambda e: e.matmul(out=pb[6][:, g * 128:(g + 1) * 128], lhsT=xbcT[:, 8 + g, :],
                                                                 rhs=xbcT[:, 12 + g, :], start=True, stop=True))(g),
                         r=[XB], w=[pbB[6]])
                S.op("scalar", lambda e: e.copy(out=cbT_bf[:, :], in_=pb[6][:, :]), r=[pbB[6]], w=[B("cbT_bf")])
                for q in range(4):
                    bk = 4 + q % 2
                    S.op("tensor", (lambda bk: lambda e: e.matmul(out=pb[bk][:, :], lhsT=ident_bf[:, :], rhs=AP(negm, 0, [[128, 128], [0, 4], [1, 128]]),
                                                                  start=True, stop=False))(bk),
                         r=[B("ident_bf"), B("negm")], w=[pbB[bk]])
                    for r_ in range(4):
                        h = 4 * q + r_
                        for nm, src in (("dacs_hi", dacs_hi), ("dacs_lo", dacs_lo)):
                            S.op("tensor", (lambda bk, r_, h, src, nm: lambda e: e.matmul(
                                out=pb[bk][:, r_ * 128:(r_ + 1) * 128], lhsT=AP(ident_bf, h, [[128, 128], [0, 128]]), rhs=src[:, :],
                                start=False, stop=(nm == "dacs_lo" and r_ == 3)))(bk, r_, h, src, nm),
                                 r=[B("ident_bf"), B(nm)], w=[pbB[bk]])
                    for r_ in range(4):
                        h = 4 * q + r_
                        S.op("scalar", (lambda bk, r_, h: lambda e: e.activation(out=Ebf[:, h, :], in_=pb[bk][:, r_ * 128:(r_ + 1) * 128],
                                                                                func=AF.Exp, bias=tms[:, 0, h:h + 1]))(bk, r_, h),
                             r=[pbB[bk], B("negdacs")], w=[EB])
                S.op("vector", lambda e: e.tensor_tensor(out=Ebf.rearrange("p (g r) l -> p g r l", g=4),
                                                         in0=Ebf.rearrange("p (g r) l -> p g r l", g=4),
                                                         in1=AP(cbT_bf, 0, [[512, 128], [128, 4], [0, 4], [1, 128]]), op=ALU.mult),
                     r=[EB, B("cbT_bf")], w=[EB])

            def B5():
                for h in range(16):
                    bk = 4 + h // 8
                    S.op("tensor", (lambda h, bk: lambda e: e.matmul(out=pb[bk][:, (h % 8) * 64:(h % 8 + 1) * 64], lhsT=Ebf[:, h, :],
                                                                     rhs=xdt_bf[:, h * 64:(h + 1) * 64], start=True, stop=True))(h, bk),
                         r=[EB, B("xdt_bf")], w=[pbB[bk]])
                for g in range(4):
                    bk = 6 + g // 2
                    S.op("tensor", (lambda g, bk: lambda e: e.matmul(out=pb[bk][:, (g % 2) * 256:(g % 2 + 1) * 256], lhsT=xbcT[:, 12 + g, :],
                                                                     rhs=St_bf[:, g * 256:(g + 1) * 256], start=True, stop=True))(g, bk),
                         r=[XB, B("St_bf")], w=[pbB[bk]])
                for q in range(2):
                    S.op("vector", (lambda q: lambda e: e.tensor_tensor(out=t1[:, q * 512:(q + 1) * 512].rearrange("p (h d) -> p h d", h=8),
                                                                        in0=pb[6 + q][:, :].rearrange("p (h d) -> p h d", h=8),
                                                                        in1=AP(tms, 16 + 8 * q, [[64, 128], [1, 8], [0, 64]]), op=ALU.mult))(q),
                         r=[pbB[6 + q], B("edacs")], w=[EB])
                    S.op("vector", (lambda q: lambda e: e.tensor_tensor(out=t1[:, q * 512:(q + 1) * 512], in0=t1[:, q * 512:(q + 1) * 512],
                                                                        in1=pb[4 + q][:, :], op=ALU.add))(q),
                         r=[pbB[4 + q], EB], w=[EB])
                for g in range(4):
                    bk = 6 + g // 2
                    S.op("tensor", (lambda g, bk: lambda e: e.matmul(out=pb[bk][:, (g % 2) * 256:(g % 2 + 1) * 256],
                                                                     lhsT=B_tm[:, g * 128:(g + 1) * 128],
                                                                     rhs=xdte_bf[:, g * 256:(g + 1) * 256], start=True, stop=True))(g, bk),
                         r=[B("B_tm"), B("xdte_bf")], w=[pbB[bk]])
                S.op("gpsimd", lambda e: e.tensor_tensor(out=t2[:, :].rearrange("p (h d) -> p h d", h=16),
                                                         in0=xs_tm[:, :].rearrange("p (h d) -> p h d", h=16),
                                                         in1=AP(d_bc, 0, [[16, 128], [1, 16], [0, 64]]), op=ALU.mult),
                     r=[B("xs_tm"), B("d_bc"), XB], w=[XB])
                S.op("vector", lambda e: e.tensor_tensor(out=t1[:, :], in0=t1[:, :], in1=t2[:, :], op=ALU.add), r=[EB, XB], w=[EB])
                S.op("vector", lambda e: e.tensor_tensor(out=t1[:, :], in0=t1[:, :], in1=szc[:, :], op=ALU.mult), r=[EB, szB], w=[EB])
                S.op("vector", lambda e: e.tensor_tensor(out=St[:, :].rearrange("p (h d) -> p h d", h=16),
                                                         in0=St[:, :].rearrange("p (h d) -> p h d", h=16),
                                                         in1=AP(tms, 48, [[64, 128], [1, 16], [0, 64]]), op=ALU.mult),
                     r=[B("St"), B("cd_bc")], w=[B("St")])
                for q in range(2):
                    S.op("vector", (lambda q: lambda e: e.tensor_tensor(out=St[:, q * 512:(q + 1) * 512], in0=St[:, q * 512:(q + 1) * 512],
                                                                        in1=pb[6 + q][:, :], op=ALU.add))(q),
                         r=[B("St"), pbB[6 + q]], w=[B("St")])
                S.op("scalar", lambda e: e.copy(out=St_bf[:, :], in_=St[:, :]), r=[B("St")], w=[B("St_bf")])

            def B6():
                for g in range(4):
                    S.op("scalar", (lambda g: lambda e: e.activation(out=xdt_bf[:, g * 256:(g + 1) * 256], in_=t1[:, g * 256:(g + 1) * 256],
                                                                     func=AF.Square, accum_out=stc(SSQB + g)))(g),
                         r=[EB], w=[B("xdt_bf"), B("ssqb")])
                S.op("scalar", lambda e: e.copy(out=stc(SCR0), in_=stc(SCR1)), r=[B("st_scr")], w=[B("ssqb")])
                rsqrt(SSQB, RSTDB, 4, 1.0 / 256, B("ssqb"), B("rstdb"), MSB)
                S.op("vector", lambda e: e.tensor_tensor(out=xs_tm[:, :].rearrange("p (g d) -> p g d", g=4),
                                                         in0=t1[:, :].rearrange("p (g d) -> p g d", g=4),
                                                         in1=AP(st, RSTDB, [[64, 128], [1, 4], [0, 256]]), op=ALU.mult),
                     r=[EB, B("rstdb")], w=[B("xs_tm")])
                for h in range(8):
                    S.op("tensor", (lambda h: lambda e: e.transpose(out=pbf(5)[:, h * 128:(h + 1) * 128],
                                                                   in_=xs_tm[:, h * 128:(h + 1) * 128], identity=ident_bf[:, :]))(h),
                         r=[B("xs_tm"), B("ident_bf")], w=[pbB[5]])
                S.op("scalar", lambda e: e.copy(out=yT[:, 8:16, :], in_=pbf(5).rearrange("p (k t) -> p k t", k=8)),
                     r=[pbB[5]], w=[B("yTb")])

            def B7():
                S.dma("sync", lambda e: e.dma_start(out=t2[:, :], in_=src_d[r0:r0 + T, :]), "xr", r=[dramB], w=[XB])
                for b in range(2):
                    for k in range(16):
                        S.op("tensor", (lambda b, k: lambda e: e.matmul(out=pb[4 + b][:, :], lhsT=yT[:, k, :],
                                                                        rhs=wout[:, k, b * 512:(b + 1) * 512], start=(k == 0), stop=(k == 15)))(b, k),
                             r=[B("yTa"), B("yTb"), B("wout")], w=[pbB[4 + b]])
                for b in range(2):
                    S.op("scalar", (lambda b: lambda e: e.activation(out=xdt_bf[:, b * 512:(b + 1) * 512], in_=pb[4 + b][:, :],
                                                                     func=AF.Square, accum_out=stc(SSQM + b)))(b),
                         r=[pbB[4 + b]], w=[B("xdt_bf"), B("ssqm")])
                S.op("scalar", lambda e: e.copy(out=stc(SCR0), in_=stc(SCR1)), r=[B("st_scr")], w=[B("ssqm")])
                S.op("vector", lambda e: e.tensor_tensor(out=stc(SSQM), in0=stc(SSQM), in1=stc(SSQM + 1), op=ALU.add),
                     r=[B("ssqm")], w=[B("ssqm")])
                rsqrt(SSQM, RSTDM, 1, 1.0 / D, B("ssqm"), B("rstdm"), MSM)
                for b in range(2):
                    S.op("vector", (lambda b: lambda e: e.scalar_tensor_tensor(out=t1[:, b * 512:(b + 1) * 512], in0=pb[4 + b][:, :],
                                                                               scalar=stc(RSTDM), in1=post_bc[:, b * 512:(b + 1) * 512],
                                                                               op0=ALU.mult, op1=ALU.mult))(b),
                         r=[pbB[4 + b], B("rstdm"), B("post_bc")], w=[EB])
                S.op("vector", lambda e: e.tensor_tensor(out=t1[:, :], in0=t1[:, :], in1=t2[:, :], op=ALU.add), r=[EB, XB], w=[EB])
                S.dma("sync", lambda e: e.dma_start(out=dst_d[r0:r0 + T, :], in_=t1[:, :]), "xs", r=[EB], w=[dramB])
            return [B2, B3, B4, B5, B6, B7]

        def Aprime(As):
            b1 = As["B1"]
            return S.record([As["A1"], lambda: b1(0, 4), As["A2"], lambda: b1(4, 8), As["A3"], lambda: b1(8, 12), As["A5"],
                             lambda: b1(12, 16), As["A4a"]])

        As = stageA(0)
        for o in Aprime(As):
            S.play(o)
        for c in range(NCH):
            Bs = stageB(c)
            X = S.record(Bs[:5])
            xbar = len(X)
            X += S.record(Bs[5:])
            Y = S.record([As["A4b"]])
            ybar = len(Y)
            An = stageA(c + 1) if c + 1 < NCH else None
            if An is not None:
                Y += Aprime(An)
            elif l + 1 < NL:
                load_win(l + 1)
            S.merge(X, Y, ybar, xbar)
            As = An

    for l_ in range(NL):
        do_layer(l_, x_d if l_ == 0 else xbuf_d, out_d if l_ == NL - 1 else xbuf_d)
    global LAST_S
    LAST_S = S
    S.emit(nc, es)
    es.close()
    return nc


_CONSTS = None


def _consts():
    global _CONSTS
    if _CONSTS is None:
        ident = np.eye(T, dtype=np.float32)
        tril = np.tril(np.ones((T, T), np.float32))
        sidx = np.arange(T)[:, None]
        lidx = np.arange(T)[None, :]
        negm1 = np.where(sidx <= lidx, 0.0, NEG).astype(np.float32)
        negm = np.tile(negm1, (1, 4))
        sel = np.zeros((T, 16, T), np.float32)
        for h in range(16):
            sel[h, h, :] = 1.0
        one16 = np.zeros((T, T), np.float32)
        one16[0:16, :] = 1.0
        _CONSTS = dict(c_ident=ident, c_tril=tril, c_negm=negm, c_sel=sel.reshape(T, 16 * T), c_one16=one16)
    return _CONSTS


_PROG = {}


def _get_prog(NL, NCH):
    key = (NL, NCH)
    if key not in _PROG:
        _PROG[key] = build_program(NL, NCH)
    return _PROG[key]


PARAMS = ["pre_norm_w", "w_in", "gmlp_v_norm_w", "gmlp_v_norm_b", "gmlp_ws", "gmlp_bs", "gmlp_norm_w",
          "conv_w", "conv_b", "dt_bias", "a_log", "d_skip", "ssd_norm_w", "w_out", "post_norm_w"]

FUSED = True


def kernel(**inputs):
    x = np.ascontiguousarray(np.asarray(inputs["x"], dtype=np.float32))
    Bn, Ls, _ = x.shape
    NCH = Ls // T
    depth = inputs["w_in"].shape[0]
    par = {k: np.ascontiguousarray(np.asarray(inputs[k], dtype=np.float32)) for k in PARAMS}
    cs = _consts()
    if FUSED:
        nc = _get_prog(depth, NCH)
        in_maps = [dict(x=x[b], **par, **cs) for b in range(Bn)]
        res = run_bass_kernel_spmd(nc, in_maps, core_ids=list(range(Bn)))
        return np.stack([np.asarray(r["out"]) for r in res.results], axis=0).astype(np.float32)
    cur = [x[b] for b in range(Bn)]
    nc = _get_prog(1, NCH)
    for l in range(depth):
        pl = {k: np.ascontiguousarray(v[l:l + 1]) for k, v in par.items()}
        in_maps = [dict(x=cur[b], **pl, **cs) for b in range(Bn)]
        res = run_bass_kernel_spmd(nc, in_maps, core_ids=list(range(Bn)))
        cur = [np.ascontiguousarray(np.asarray(r["out"], dtype=np.float32)) for r in res.results]
    return np.stack(cur, axis=0).astype(np.float32)
```

```python
import numpy as np
from contextlib import ExitStack
import concourse.bass as bass
import concourse.mybir as mybir
from concourse.bass_utils import run_bass_kernel_spmd

F32 = mybir.dt.float32
BF16 = mybir.dt.bfloat16
AF = mybir.ActivationFunctionType
ALU = mybir.AluOpType
AX = mybir.AxisListType

D = 1024
T = 128
INC = 6160
EPS = 1e-6
NEG = -60000.0
import os
DBG = int(os.environ.get("KDBG", "99"))
ENGS = ("tensor", "vector", "scalar", "gpsimd", "sync")


class _Fake:
    def __init__(self):
        self.name = None
        self.kw = {}

    def __getattr__(self, name):
        def f(*a, **kw):
            self.name = name
            self.kw = kw
            return None
        return f


def _fsize(ap):
    n = 1
    for d in list(ap.shape)[1:]:
        n *= int(d)
    return n


def op_cost(eng, fn, kind):
    if kind == "dma":
        return 2.5
    fk = _Fake()
    try:
        fn(fk)
    except Exception:
        return 0.5
    kw = fk.kw
    n = 128
    for key in ("in_", "in0", "rhs", "out", "data1"):
        if key in kw and hasattr(kw[key], "shape"):
            n = _fsize(kw[key])
            break
    if eng == "tensor":
        if fk.name == "transpose":
            return 0.12
        return 0.07 + n / 2400.0 + 0.05
    if eng == "vector":
        return 0.06 + (150 + n) / 960.0
    if eng == "scalar":
        return 0.17 + n / 1200.0 + (0.15 if "accum_out" in kw else 0.0)
    if eng == "gpsimd":
        return 0.8 + n / 800.0
    return 0.1


class Buf:
    __slots__ = ("name", "writer", "readers", "excl", "wreal")

    def __init__(self, name, excl=False):
        self.name = name
        self.writer = None
        self.readers = []
        self.wreal = True
        self.excl = excl


class Item:
    __slots__ = ("fn", "waits", "kind", "idx", "dkey")

    def __init__(self, fn, waits, kind, idx, dkey=None):
        self.fn, self.waits, self.kind, self.idx, self.dkey = fn, waits, kind, idx, dkey


class Sched:
    def __init__(self):
        self.items = {e: [] for e in ENGS}
        self.cnt = {e: 0 for e in ENGS}
        self.seen = {e: {} for e in ENGS}
        self.dma_cnt = {}
        self.signal = {e: set() for e in ENGS}
        self.rec = None
        self.efree = {e: 0.0 for e in ENGS}
        self.done = {}

    def peek(self, reads, writes):
        deps = []
        for b in reads:
            if b.writer is not None:
                deps.append(b.writer)
            if b.excl:
                deps.extend(b.readers)
        for b in writes:
            if b.writer is not None:
                deps.append(b.writer)
            deps.extend(b.readers)
        return deps

    def est_start(self, eng, reads, writes):
        t = self.efree[eng]
        for tok in self.peek(reads, writes):
            d = self.done.get(tok, 0.0)
            if not (tok[0] == "e" and tok[1] == eng):
                d += 0.35
            else:
                d += 0.1
            t = max(t, d)
        return t

    def record(self, fns):
        self.rec = []
        for fn in fns:
            fn()
        ops, self.rec = self.rec, None
        return ops

    def play(self, o):
        kind, eng, fn, key, r, w = o
        st = self.est_start(eng, r, w)
        c = op_cost(eng, fn, kind)
        if kind == "op":
            tok = self.op(eng, fn, r, w)
            self.efree[eng] = st + c
            self.done[tok] = st + c
        else:
            tok = self.dma(eng, fn, key, r, w)
            self.efree[eng] = st + 0.1
            self.done[tok] = st + c
        return tok

    def merge(self, X, Y, ybar, xbar):
        i = j = 0
        while i < len(X) or j < len(Y):
            if i >= len(X):
                self.play(Y[j]); j += 1
                continue
            if j >= len(Y):
                self.play(X[i]); i += 1
                continue
            if i >= xbar and j < ybar:
                self.play(Y[j]); j += 1
                continue
            sx = self.est_start(X[i][1], X[i][4], X[i][5])
            sy = self.est_start(Y[j][1], Y[j][4], Y[j][5])
            if sx <= sy:
                self.play(X[i]); i += 1
            else:
                self.play(Y[j]); j += 1

    def _collect(self, eng, reads, writes, is_dma):
        deps = []
        for b in reads:
            if b.writer is not None:
                deps.append(b.writer)
            if b.excl:
                deps.extend(b.readers)
        for b in writes:
            if b.writer is not None:
                deps.append(b.writer)
            deps.extend(b.readers)
        waits = []
        for tok in deps:
            if tok[0] == "e":
                _, src, idx = tok
                if src == eng and not is_dma:
                    if eng == "tensor" or not any((b.writer == tok and b.wreal) for b in reads):
                        continue
                if self.seen[eng].get(src, -1) >= idx:
                    continue
                self.seen[eng][src] = idx
                self.signal[src].add(idx)
                waits.append(tok)
            else:
                _, key, val = tok
                if self.seen[eng].get(("d", key), 0) >= val:
                    continue
                self.seen[eng][("d", key)] = val
                waits.append(tok)
        best = {}
        for tok in waits:
            k = tok[1]
            if k not in best or tok[2] > best[k][2]:
                best[k] = tok
        return list(best.values())

    def op(self, eng, fn, r=(), w=()):
        if self.rec is not None:
            self.rec.append(("op", eng, fn, None, tuple(r), tuple(w)))
            return None
        waits = self._collect(eng, r, w, False)
        idx = self.cnt[eng]
        self.cnt[eng] += 1
        tok = ("e", eng, idx)
        self.items[eng].append(Item(fn, waits, "c", idx))
        for b in r:
            if b.excl:
                b.writer = tok
                b.wreal = False
                b.readers = []
            else:
                b.readers.append(tok)
        for b in w:
            b.writer = tok
            b.wreal = True
            b.readers = []
        return tok

    def dma(self, eng, fn, key, r=(), w=()):
        if self.rec is not None:
            self.rec.append(("dma", eng, fn, key, tuple(r), tuple(w)))
            return None
        waits = self._collect(eng, r, w, True)
        n = self.dma_cnt.get(key, 0) + 1
        self.dma_cnt[key] = n
        tok = ("d", key, 16 * n)
        self.items[eng].append(Item(fn, waits, "d", None, key))
        for b in r:
            b.readers.append(tok)
        for b in w:
            b.writer = tok
            b.readers = []
        return tok

    def emit(self, nc, es):
        esem = {e: es.enter_context(nc.semaphore("es_" + e)) for e in ENGS if e != "sync"}
        dsem = {k: es.enter_context(nc.semaphore("ds_" + str(k))) for k in self.dma_cnt}
        rank = {}
        for e in ENGS:
            for i, idx in enumerate(sorted(self.signal[e])):
                rank[(e, idx)] = i + 1
        final_dma = [(dsem[k], 16 * n) for k, n in self.dma_cnt.items()]

        def run(ename, e):
            for it in self.items[ename]:
                for tok in it.waits:
                    if tok[0] == "e":
                        e.wait_ge(esem[tok[1]], rank[(tok[1], tok[2])])
                    else:
                        e.wait_ge(dsem[tok[1]], tok[2])
                ins = it.fn(e)
                if it.kind == "c":
                    if it.idx in self.signal[ename]:
                        ins.then_inc(esem[ename], 1)
                else:
                    ins.then_inc(dsem[it.dkey], 16)
            if ename == "sync":
                for s, v in final_dma:
                    e.wait_ge(s, v)

        with nc.Block() as block:
            @block.tensor
            def _(e):
                run("tensor", e)

            @block.vector
            def _(e):
                run("vector", e)

            @block.scalar
            def _(e):
                run("scalar", e)

            @block.gpsimd
            def _(e):
                run("gpsimd", e)

            @block.sync
            def _(e):
                run("sync", e)


def AP(t, off, dims):
    return bass.AP(t, off, [list(d) for d in dims])


def build_program(NL, NCH):
    nc = bass.Bass("TRN2", target_bir_lowering=False)
    L = NCH * T

    def din(name, shape):
        return nc.dram_tensor(name, shape, F32, kind="ExternalInput").ap()

    x_d = din("x", [L, D])
    pre_d = din("pre_norm_w", [NL, D])
    win_d = din("w_in", [NL, D, INC])
    vw_d = din("gmlp_v_norm_w", [NL, D])
    vb_d = din("gmlp_v_norm_b", [NL, D])
    ws_d = din("gmlp_ws", [NL, 8, T, T])
    bs_d = din("gmlp_bs", [NL, 8, T])
    gnw_d = din("gmlp_norm_w", [NL, D])
    cw_d = din("conv_w", [NL, 4, 2048])
    cb_d = din("conv_b", [NL, 2048])
    dtb_d = din("dt_bias", [NL, 16])
    alog_d = din("a_log", [NL, 16])
    dsk_d = din("d_skip", [NL, 16])
    snw_d = din("ssd_norm_w", [NL, D])
    wout_d = din("w_out", [NL, 2 * D, D])
    post_d = din("post_norm_w", [NL, D])
    c_ident = din("c_ident", [T, T])
    c_tril = din("c_tril", [T, T])
    c_negm = din("c_negm", [T, 512])
    c_sel = din("c_sel", [T, 16 * T])
    c_one16 = din("c_one16", [T, T])
    out_d = nc.dram_tensor("out", [L, D], F32, kind="ExternalOutput").ap()
    xbuf_d = nc.dram_tensor("xbuf", [L, D], F32, kind="Internal").ap() if NL > 1 else None

    es = ExitStack()
    S = Sched()
    bufs = {}

    def B(name):
        if name not in bufs:
            bufs[name] = Buf(name)
        return bufs[name]

    def sb(name, shape, dt):
        return es.enter_context(nc.sbuf_tensor(name, shape, dt))

    win = sb("win", [128, 8, INC], BF16)
    wout = sb("wout", [128, 16, D], BF16)
    ident_bf = sb("ident_bf", [128, 128], BF16)
    ident_f = sb("ident_f", [128, 16], F32)
    negm = sb("negm", [128, 128], BF16)
    one16 = sb("one16", [128, 1], F32)
    mhalf = sb("mhalf", [128, 8], F32)
    post_bc = sb("post_bc", [128, D], F32)
    vw_bc = sb("vw_bc", [128, D], F32)
    bias_t = sb("bias_t", [128, D], F32)
    wsT = sb("wsT", [128, 8, T], BF16)
    prm = sb("prm", [128, 112], F32)
    dtb = sb("dtb", [128, 1], F32)
    alog = sb("alog", [128, 1], F32)
    aneg = sb("aneg", [128, 1], F32)
    d_bc = sb("d_bc", [128, 16], F32)
    rs = sb("rs", [128, 8], F32)

    x_sb = sb("x_sb", [128, D], F32)
    xbf = sb("xbf", [128, D], BF16)
    hT = [sb("hT%d" % i, [128, 8, 131], BF16) for i in range(2)]
    halo = sb("halo", [128, 8, 3], BF16)
    g_sb = sb("g_sb", [128, D], F32)
    sz = [sb("sz%d" % i, [128, D], F32) for i in range(2)]
    dtx = [sb("dtx%d" % i, [128, T], F32) for i in range(2)]
    y_sb = sb("y_sb", [128, D], F32)
    yn_bf = sb("yn_bf", [128, D], BF16)
    vn_bf = yn_bf
    yT = sb("yT", [128, 16, T], BF16)
    acc = [sb("acc%d" % i, [128, T], F32) for i in range(2)]
    xbc_fs = [sb("xbc_f%d" % i, [128, D], F32) for i in range(2)]
    xs_tm = sb("xs_tm", [128, D], BF16)
    B_tm = sb("B_tm", [128, 512], BF16)
    fm = sb("fm", [128, 5, T], F32)
    dacs_hi = sb("dacs_hi", [128, T], BF16)
    dacs_lo = sb("dacs_lo", [128, T], BF16)
    tm48 = sb("tm48", [128, 48], F32)
    tms = sb("tms", [128, 4, 16], F32)
    diagcd = sb("diagcd", [128, 16], F32)
    cbT_bf = sb("cbT_bf", [128, 512], BF16)
    E_f = sb("E_f", [128, D], F32)
    xdt_bf = sb("xdt_bf", [128, D], BF16)
    xdte_bf = sb("xdte_bf", [128, D], BF16)
    St = sb("St", [128, D], F32)
    St_bf = sb("St_bf", [128, D], BF16)
    st = sb("st", [128, 64], F32)
    bnst = sb("bnst", [128, 2, 6], F32)

    pb = [es.enter_context(nc.psum_tensor("pb%d" % i, [128, 512], F32)) for i in range(8)]
    pbB = [B("pb%d" % i) for i in range(8)]
    for b_ in pbB:
        b_.excl = True

    def pbf(i):
        return pb[i][:, :].bitcast(BF16)

    SSQ, MS, RSTD, RSTDV, NMR, MSV, RSTDM, MSM, SCR0, SCR1 = range(10)
    SSQG, MSG, RSTDG = 16, 24, 32
    SSQB, MSB, RSTDB = 40, 44, 48
    SSQM = 52
    MV = 56

    def stc(c, n=1):
        return st[:, c:c + n]

    S.dma("sync", lambda e: e.dma_start(out=ident_f[:, :], in_=c_ident[:, 0:16]), "c0", w=[B("ident_f")])
    S.op("vector", lambda e: e.memset(one16[:, :], 1.0), w=[B("one16")])
    S.dma("gpsimd", lambda e: e.dma_start(out=ident_bf[:, :], in_=c_ident), "c3", w=[B("ident_bf")])
    S.dma("gpsimd", lambda e: e.dma_start(out=negm[:, :], in_=c_negm[:, 0:128]), "c4", w=[B("negm")])
    S.op("vector", lambda e: e.memset(mhalf[:, :], -0.5), w=[B("mhalf")])
    S.op("vector", lambda e: e.memset(dacs_hi[:, :], 0.0), w=[B("dacs_hi")])
    S.op("vector", lambda e: e.memset(dacs_lo[:, :], 0.0), w=[B("dacs_lo")])
    S.op("vector", lambda e: e.memset(diagcd[:, :], 0.0), w=[B("diagcd")])
    S.op("vector", lambda e: e.memset(st[:, SCR0:SCR1 + 1], 0.0), w=[B("st_scr")])

    def rsqrt(src_c, dst_c, n, scale, rbuf, wbuf, tmp_c):
        S.op("vector", lambda e: e.tensor_scalar(out=stc(tmp_c, n), in0=stc(src_c, n), scalar1=scale,
                                                 scalar2=EPS, op0=ALU.mult, op1=ALU.add),
             r=[rbuf], w=[B("tmp%d" % tmp_c)])
        S.op("gpsimd", lambda e: e.tensor_tensor(out=stc(dst_c, n), in0=stc(tmp_c, n), in1=mhalf[:, 0:n],
                                                 op=ALU.pow),
             r=[B("tmp%d" % tmp_c), B("mhalf")], w=[wbuf])

    def do_layer(l, src_d, dst_d):
        winB = [B("win_k%d" % k) for k in range(8)]

        def load_win(ll):
            for k in range(8):
                S.dma("gpsimd", (lambda k: lambda e: e.dma_start(out=win[:, k, :], in_=win_d[ll, k * 128:(k + 1) * 128, :]))(k),
                      "win%d" % k, w=[winB[k]])
        if l == 0:
            load_win(0)
        if True:
            stgp = g_sb[0:16, 0:896].rearrange("p (g t) -> p g t", g=7)
            for kk in range(4):
                S.dma("sync", (lambda kk: lambda e: e.dma_start(out=stgp[:, kk, :], in_=cw_d[l, kk].rearrange("(j p) -> j p", p=128)))(kk),
                      "q%d" % kk, w=[B("g_sb")])
            S.dma("sync", lambda e: e.dma_start(out=stgp[:, 4, :], in_=cb_d[l].rearrange("(j p) -> j p", p=128)), "q4", w=[B("g_sb")])
            S.dma("sync", lambda e: e.dma_start(out=stgp[0:8, 5, :], in_=pre_d[l].rearrange("(j p) -> j p", p=128)), "q5", w=[B("g_sb")])
            S.dma("sync", lambda e: e.dma_start(out=g_sb[8:16, 640:768], in_=gnw_d[l].rearrange("(j p) -> j p", p=128)), "q6", w=[B("g_sb")])
            S.dma("sync", lambda e: e.dma_start(out=stgp[0:8, 6, :], in_=snw_d[l].rearrange("(j p) -> j p", p=128)), "q7", w=[B("g_sb")])
            S.dma("sync", lambda e: e.dma_start(out=g_sb[8:16, 768:896], in_=bs_d[l]), "q8", w=[B("g_sb")])
            for gi in range(7):
                S.op("tensor", (lambda gi: lambda e: e.transpose(out=pb[6][:, gi * 16:(gi + 1) * 16], in_=g_sb[0:16, gi * 128:(gi + 1) * 128],
                                                                identity=ident_f[0:16, 0:16]))(gi),
                     r=[B("g_sb"), B("ident_f")], w=[pbB[6]])
            S.op("vector", lambda e: e.tensor_copy(out=prm[:, :], in_=pb[6][:, 0:112]), r=[pbB[6]], w=[B("prm")])
            S.dma("sync", lambda e: e.dma_start(out=dtb[0:16, :], in_=dtb_d[l].rearrange("(h o) -> h o", o=1), allow_slow_non_contiguous=True), "p6", w=[B("dtb")])
            S.dma("sync", lambda e: e.dma_start(out=alog[0:16, :], in_=alog_d[l].rearrange("(h o) -> h o", o=1), allow_slow_non_contiguous=True), "p7", w=[B("alog")])
            S.dma("sync", lambda e: e.dma_start(out=d_bc[:, :], in_=dsk_d[l].partition_broadcast(128), allow_slow_non_contiguous=True), "p8", w=[B("d_bc")])
            S.dma("sync", lambda e: e.dma_start(out=post_bc[:, :], in_=post_d[l].partition_broadcast(128), allow_slow_non_contiguous=True), "p9", w=[B("post_bc")])
            S.dma("sync", lambda e: e.dma_start(out=vw_bc[:, :], in_=vw_d[l].partition_broadcast(128), allow_slow_non_contiguous=True), "p10", w=[B("vw_bc")])
            S.dma("sync", lambda e: e.dma_start(out=sz[0][:, :], in_=vb_d[l].partition_broadcast(128), allow_slow_non_contiguous=True), "p11", w=[B("sz0")])
            S.dma("sync", lambda e: e.dma_start(out=y_sb[:, :].rearrange("t (h s) -> t h s", h=8),
                                               in_=ws_d[l].rearrange("h t s -> t h s"), allow_slow_non_contiguous=True), "p12", w=[B("y_sb")])
        S.op("scalar", lambda e: e.activation(out=aneg[0:16, :], in_=alog[0:16, :], func=AF.Exp), r=[B("alog")], w=[B("aneg")])
        S.op("vector", lambda e: e.tensor_scalar(out=aneg[0:16, :], in0=aneg[0:16, :], scalar1=-1.0, scalar2=None, op0=ALU.mult),
             r=[B("aneg")], w=[B("aneg")])
        S.dma("sync", lambda e: e.dma_start(out=St[:, 0:128], in_=c_tril), "c1", w=[B("St")])
        ws3 = y_sb[:, :].rearrange("t (h s) -> t h s", h=8)
        S.op("vector", lambda e: e.tensor_tensor(out=ws3, in0=ws3, in1=AP(St, 0, [[1024, 128], [0, 8], [1, 128]]), op=ALU.mult),
             r=[B("y_sb"), B("St")], w=[B("y_sb")])
        S.op("vector", lambda e: e.tensor_reduce(out=rs[:, :], in_=ws3, axis=AX.X, op=ALU.add), r=[B("y_sb")], w=[B("rs")])
        S.op("vector", lambda e: e.tensor_copy(out=yn_bf[:, :], in_=y_sb[:, :]), r=[B("y_sb")], w=[B("yn_bf")])
        for h in range(8):
            bk = 4 + h // 4
            S.op("tensor", (lambda h, bk: lambda e: e.transpose(out=pbf(bk)[:, (h % 4) * 128:(h % 4 + 1) * 128],
                                                               in_=yn_bf[:, h * 128:(h + 1) * 128], identity=ident_bf[:, :]))(h, bk),
                 r=[B("yn_bf"), B("ident_bf")], w=[pbB[bk]])
        for q in range(2):
            S.op("vector", (lambda q: lambda e: e.tensor_copy(out=wsT[:, 4 * q:4 * q + 4, :],
                                                              in_=pbf(4 + q)[:, 0:512].rearrange("p (h t) -> p h t", h=4)))(q),
                 r=[pbB[4 + q]], w=[B("wsT")])
        for h in range(8):
            S.op("vector", (lambda h: lambda e: e.tensor_scalar(out=bias_t[:, h * 128:(h + 1) * 128], in0=sz[0][:, h * 128:(h + 1) * 128],
                                                                scalar1=rs[:, h:h + 1], scalar2=prm[:, 104 + h:105 + h],
                                                                op0=ALU.mult, op1=ALU.add))(h),
                 r=[B("sz0"), B("rs"), B("prm")], w=[B("bias_t")])
        stg = [(g_sb, B("g_sb")), (y_sb, B("y_sb"))]
        for k in range(16):
            tt, tb = stg[k % 2]
            S.dma("sync", (lambda k, tt: lambda e: e.dma_start(out=tt[:, :], in_=wout_d[l, k * 128:(k + 1) * 128, :]))(k, tt),
                  "wo%d" % (k % 2), w=[tb])
            if k % 2 == 0:
                S.op("vector", (lambda k, tt: lambda e: e.tensor_scalar(out=wout[:, k, :], in0=tt[:, :], scalar1=prm[:, 88 + k:89 + k],
                                                                        scalar2=None, op0=ALU.mult))(k, tt),
                     r=[tb, B("prm")], w=[B("wout")])
            else:
                S.op("scalar", (lambda k, tt: lambda e: e.activation(out=wout[:, k, :], in_=tt[:, :], func=AF.Copy,
                                                                     scale=prm[:, 88 + k:89 + k]))(k, tt),
                     r=[tb, B("prm")], w=[B("wout")])
        S.op("gpsimd", lambda e: e.memset(St[:, :], 0.0), w=[B("St")])
        S.op("gpsimd", lambda e: e.memset(St_bf[:, :], 0.0), w=[B("St_bf")])
        S.op("gpsimd", lambda e: e.memset(halo[:, :, :], 0.0), w=[B("halo")])

        XR_, AB, E1, L1, DT, DACS = None, 0, 1, 2, 3, 4
        DTA = AB
        Ebf = E_f[:, :].bitcast(BF16).rearrange("p (h l) -> p h l", h=16)
        EB = B("E_f")

        def f(i):
            return fm[0:16, i, :]

        def stageA(c):
            par = c % 2
            hTc, hB = hT[par], B("hT%d" % par)
            szc, szB = sz[par], B("sz%d" % par)
            dtxc, dtxB = dtx[par], B("dtx%d" % par)
            dramB = B("dram%d" % c)
            r0 = c * T
            xB = B("x_sb")
            xbc_f = xbc_fs[par]
            xbcT = xbc_f[:, :].bitcast(BF16).rearrange("p (j t) -> p j t", j=16)
            XB = B("xbc_f%d" % par)

            def A1():
                S.dma("sync", lambda e: e.dma_start(out=x_sb[:, :], in_=src_d[r0:r0 + T, :]), "xl", r=[dramB], w=[xB])
                S.op("scalar", lambda e: e.activation(out=xbf[:, :], in_=x_sb[:, :], func=AF.Square, accum_out=stc(SSQ)),
                     r=[xB], w=[B("xbf"), B("ssq")])
                S.op("scalar", lambda e: e.copy(out=stc(SCR0), in_=stc(SCR1)), r=[B("st_scr")], w=[B("ssq")])
                rsqrt(SSQ, RSTD, 1, 1.0 / D, B("ssq"), B("rstd"), MS)
                S.op("vector", lambda e: e.tensor_scalar(out=xbf[:, :], in0=x_sb[:, :], scalar1=stc(RSTD), scalar2=None, op0=ALU.mult),
                     r=[xB, B("rstd")], w=[B("xbf")])
                for k in range(8):
                    S.op("tensor", (lambda k: lambda e: e.transpose(out=pbf(0)[:, k * 128:(k + 1) * 128],
                                                                   in_=xbf[:, k * 128:(k + 1) * 128], identity=ident_bf[:, :]))(k),
                         r=[B("xbf"), B("ident_bf")], w=[pbB[0]])
                S.op("gpsimd", lambda e: e.tensor_copy(out=hTc[:, :, 0:3], in_=halo[:, :, :]), r=[B("halo")], w=[hB])
                S.op("vector", lambda e: e.tensor_tensor(out=hTc[:, :, 3:131], in0=pbf(0).rearrange("p (k t) -> p k t", k=8),
                                                         in1=AP(prm, 80, [[112, 128], [1, 8], [0, 128]]), op=ALU.mult),
                     r=[pbB[0], B("prm")], w=[hB])
                S.op("gpsimd", lambda e: e.tensor_copy(out=halo[:, :, :], in_=hTc[:, :, 128:131]), r=[hB], w=[B("halo")])

            def proj(bank, col0):
                for k in range(8):
                    S.op("tensor", (lambda k: lambda e: e.matmul(out=pb[bank][:, :], lhsT=hTc[:, k, 3:131],
                                                                 rhs=win[:, k, col0:col0 + 512], start=(k == 0), stop=(k == 7)))(k),
                         r=[hB, winB[k]], w=[pbB[bank]])

            def A2():
                proj(2, 1024)
                proj(3, 1536)
                S.op("vector", lambda e: e.bn_stats(out=bnst[:, 0, :], in_=pb[2][:, :]), r=[pbB[2]], w=[B("bnst")])
                S.op("vector", lambda e: e.bn_stats(out=bnst[:, 1, :], in_=pb[3][:, :]), r=[pbB[3]], w=[B("bnst")])
                S.op("vector", lambda e: e.bn_aggr(out=stc(MV, 2), in_=bnst[:, :, :].rearrange("p a b -> p (a b)")),
                     r=[B("bnst")], w=[B("mv")])
                S.op("vector", lambda e: e.tensor_scalar(out=stc(MSV), in0=stc(MV + 1), scalar1=EPS, scalar2=None, op0=ALU.add),
                     r=[B("mv")], w=[B("msv")])
                S.op("gpsimd", lambda e: e.tensor_tensor(out=stc(RSTDV), in0=stc(MSV), in1=mhalf[:, 0:1], op=ALU.pow),
                     r=[B("msv"), B("mhalf")], w=[B("rstdv")])
                S.op("vector", lambda e: e.scalar_tensor_tensor(out=stc(NMR), in0=stc(MV), scalar=-1.0, in1=stc(RSTDV),
                                                                op0=ALU.mult, op1=ALU.mult),
                     r=[B("mv"), B("rstdv")], w=[B("nmr")])
                for i, bank in enumerate((2, 3)):
                    S.op("scalar", (lambda i, bank: lambda e: e.activation(out=y_sb[:, i * 512:(i + 1) * 512], in_=pb[bank][:, :],
                                                                           func=AF.Identity, bias=stc(NMR), scale=stc(RSTDV)))(i, bank),
                         r=[pbB[bank], B("nmr"), B("rstdv")], w=[B("y_sb")])
                S.op("gpsimd", lambda e: e.tensor_tensor(out=vn_bf[:, :], in0=y_sb[:, :], in1=vw_bc[:, :], op=ALU.mult),
                     r=[B("y_sb"), B("vw_bc")], w=[B("yn_bf")])

            def A3():
                proj(2, 2048)
                proj(3, 2560)
                for i, bank in enumerate((2, 3)):
                    S.op("scalar", (lambda i, bank: lambda e: e.activation(out=g_sb[:, i * 512:(i + 1) * 512], in_=pb[bank][:, :],
                                                                           func=AF.Silu))(i, bank),
                         r=[pbB[bank]], w=[B("g_sb")])
                proj(2, 0)
                proj(3, 512)
                for i, bank in enumerate((2, 3)):
                    S.op("vector", (lambda i, bank: lambda e: e.tensor_tensor(out=g_sb[:, i * 512:(i + 1) * 512], in0=g_sb[:, i * 512:(i + 1) * 512],
                                                                              in1=pb[bank][:, :], op=ALU.mult))(i, bank),
                         r=[pbB[bank], B("g_sb")], w=[B("g_sb")])

            def A4():
                for h in range(8):
                    bk = 2 + h // 4
                    S.op("tensor", (lambda h, bk: lambda e: e.matmul(out=pb[bk][:, (h % 4) * 128:(h % 4 + 1) * 128], lhsT=wsT[:, h, :],
                                                                     rhs=vn_bf[:, h * 128:(h + 1) * 128], start=True, stop=True))(h, bk),
                         r=[B("wsT"), B("yn_bf")], w=[pbB[bk]])
                for q in range(2):
                    S.op("vector", (lambda q: lambda e: e.tensor_tensor(out=y_sb[:, q * 512:(q + 1) * 512], in0=pb[2 + q][:, :],
                                                                        in1=bias_t[:, q * 512:(q + 1) * 512], op=ALU.add))(q),
                         r=[pbB[2 + q], B("bias_t")], w=[B("y_sb")])
                S.op("vector", lambda e: e.tensor_tensor(out=y_sb[:, :], in0=y_sb[:, :], in1=g_sb[:, :], op=ALU.mult),
                     r=[B("y_sb"), B("g_sb")], w=[B("y_sb")])
                for h in range(8):
                    S.op("scalar", (lambda h: lambda e: e.activation(out=xbf[:, h * 128:(h + 1) * 128], in_=y_sb[:, h * 128:(h + 1) * 128],
                                                                     func=AF.Square, accum_out=stc(SSQG + h)))(h),
                         r=[B("y_sb")], w=[B("xbf"), B("ssqg")])
                S.op("scalar", lambda e: e.copy(out=stc(SCR0), in_=stc(SCR1)), r=[B("st_scr")], w=[B("ssqg")])
                rsqrt(SSQG, RSTDG, 8, 1.0 / 128, B("ssqg"), B("rstdg"), MSG)
                S.op("vector", lambda e: e.tensor_tensor(out=yn_bf[:, :].rearrange("p (h d) -> p h d", h=8),
                                                         in0=y_sb[:, :].rearrange("p (h d) -> p h d", h=8),
                                                         in1=AP(st, RSTDG, [[64, 128], [1, 8], [0, 128]]), op=ALU.mult),
                     r=[B("y_sb"), B("rstdg")], w=[B("yn_bf")])

            def A4b():
                for h in range(8):
                    S.op("tensor", (lambda h: lambda e: e.transpose(out=pbf(0)[:, h * 128:(h + 1) * 128],
                                                                   in_=yn_bf[:, h * 128:(h + 1) * 128], identity=ident_bf[:, :]))(h),
                         r=[B("yn_bf"), B("ident_bf")], w=[pbB[0]])
                S.op("scalar", lambda e: e.copy(out=yT[:, 0:8, :], in_=pbf(0).rearrange("p (k t) -> p k t", k=8)),
                     r=[pbB[0]], w=[B("yTa")])

            def A5():
                for k in range(8):
                    S.op("tensor", (lambda k: lambda e: e.matmul(out=pb[0][0:16, 0:128], lhsT=win[:, k, 6144:6160],
                                                                 rhs=hTc[:, k, 3:131], start=(k == 0), stop=(k == 7)))(k),
                         r=[hB, winB[k]], w=[pbB[0]])
                S.op("vector", lambda e: e.tensor_scalar(out=dtxc[0:16, :], in0=pb[0][0:16, 0:128], scalar1=dtb[0:16, :], scalar2=None,
                                                         op0=ALU.add),
                     r=[pbB[0], B("dtb")], w=[dtxB])
                proj(2, 3072)
                proj(3, 3584)
                for i, bank in enumerate((2, 3)):
                    S.op("scalar", (lambda i, bank: lambda e: e.activation(out=szc[:, i * 512:(i + 1) * 512], in_=pb[bank][:, :],
                                                                           func=AF.Silu))(i, bank),
                         r=[pbB[bank]], w=[szB])
            def B1(j0, j1):
                for j in range(j0, j1):
                    bank = 2 + (j % 2)
                    a_t, aB = acc[j % 2], B("acc%d" % (j % 2))
                    c0 = 4096 + j * 128
                    for k in range(8):
                        S.op("tensor", (lambda k, bank, c0: lambda e: e.matmul(out=pb[bank][:, 0:131], lhsT=win[:, k, c0:c0 + 128],
                                                                             rhs=hTc[:, k, 0:131], start=(k == 0), stop=(k == 7)))(k, bank, c0),
                             r=[hB, winB[k]], w=[pbB[bank]])
                    rb = j % 2
                    S.op("scalar", (lambda j, bank, a_t: lambda e: e.activation(out=a_t[:, :], in_=pb[bank][:, 3:131], func=AF.Identity,
                                                                              scale=prm[:, 48 + j:49 + j], bias=prm[:, 64 + j:65 + j]))(j, bank, a_t),
                         r=[pbB[bank], B("prm")], w=[aB])
                    S.op("scalar", (lambda bank, rb: lambda e: e.copy(out=pb[rb][:, 256:387], in_=pb[bank][:, 0:131]))(bank, rb),
                         r=[pbB[bank]], w=[pbB[rb]])
                    for kk in (2, 1, 0):
                        S.op("vector", (lambda j, rb, a_t, kk: lambda e: e.scalar_tensor_tensor(
                            out=a_t[:, :], in0=pb[rb][:, 256 + kk:256 + kk + 128], scalar=prm[:, 16 * kk + j:16 * kk + j + 1], in1=a_t[:, :],
                            op0=ALU.mult, op1=ALU.add))(j, rb, a_t, kk),
                             r=[pbB[rb], B("prm"), aB], w=[aB])
                    S.op("scalar", (lambda j, a_t: lambda e: e.activation(out=xbcT[:, j, :], in_=a_t[:, :], func=AF.Silu))(j, a_t),
                         r=[aB], w=[XB])

            return dict(A1=A1, A2=A2, A3=A3, A4a=A4, A5=A5, B1=B1, A4b=A4b)

        def stageB(c):
            par = c % 2
            hTc, hB = hT[par], B("hT%d" % par)
            szc, szB = sz[par], B("sz%d" % par)
            dtxc, dtxB = dtx[par], B("dtx%d" % par)
            dramB = B("dram%d" % c)
            r0 = c * T
            XR = dtxc[0:16, :]
            DTE = XR
            M1 = pbB[7]
            xbc_f = xbc_fs[par]
            xbcT = xbc_f[:, :].bitcast(BF16).rearrange("p (j t) -> p j t", j=16)
            XB = B("xbc_f%d" % par)
            t1 = E_f
            t2 = xbc_f

            def B2():
                S.op("vector", lambda e: e.scalar_tensor_tensor(out=f(AB), in0=XR, scalar=-1.0, in1=XR, op0=ALU.mult, op1=ALU.max),
                     r=[dtxB], w=[B("fAB")])
                S.op("scalar", lambda e: e.activation(out=f(E1), in_=f(AB), func=AF.Exp, scale=-1.0), r=[B("fAB")], w=[B("fE1")])
                S.op("scalar", lambda e: e.activation(out=f(L1), in_=f(E1), func=AF.Ln, bias=1.0), r=[B("fE1")], w=[B("fL1")])
                S.op("vector", lambda e: e.scalar_tensor_tensor(out=f(DT), in0=XR, scalar=0.0, in1=f(L1), op0=ALU.max, op1=ALU.add),
                     r=[dtxB, B("fL1")], w=[B("fDT")])
                S.op("vector", lambda e: e.tensor_scalar(out=f(DTA), in0=f(DT), scalar1=aneg[0:16, :], scalar2=None, op0=ALU.mult),
                     r=[B("fDT"), B("aneg")], w=[B("fAB")])
                S.op("vector", lambda e: e.memset(f(E1), 1.0), r=[B("fE1")], w=[B("fE1")])
                S.op("vector", lambda e: e.tensor_tensor_scan(out=f(DACS), data0=f(E1), data1=f(DTA), initial=0.0,
                                                              op0=ALU.mult, op1=ALU.add),
                     r=[B("fE1"), B("fAB")], w=[B("fDACS")])
                S.op("scalar", lambda e: e.activation(out=DTE, in_=f(DACS), func=AF.Exp, scale=-1.0, bias=fm[0:16, DACS, 127:128]),
                     r=[B("fDACS")], w=[dtxB])
                S.op("vector", lambda e: e.tensor_copy(out=dacs_hi[0:16, :], in_=f(DACS)), r=[B("fDACS")], w=[B("dacs_hi")])
                S.op("vector", lambda e: e.tensor_tensor(out=dacs_lo[0:16, :], in0=f(DACS), in1=dacs_hi[0:16, :], op=ALU.subtract),
                     r=[B("fDACS"), B("dacs_hi")], w=[B("dacs_lo")])
                S.op("scalar", lambda e: e.activation(out=fm[0:16, L1, 0:1], in_=fm[0:16, DACS, 127:128], func=AF.Exp),
                     r=[B("fDACS"), B("fL1")], w=[B("fL1")])
                S.op("vector", lambda e: e.tensor_scalar(out=diagcd[0:16, :], in0=ident_f[0:16, 0:16], scalar1=fm[0:16, L1, 0:1],
                                                         scalar2=None, op0=ALU.mult),
                     r=[B("fL1"), B("ident_f")], w=[B("diagcd")])
                srcs = (f(DT), f(DACS), DTE)
                for i in range(3):
                    S.op("tensor", (lambda i: lambda e: e.transpose(out=pb[7][:, 128 + 16 * i:128 + 16 * (i + 1)], in_=srcs[i],
                                                                   identity=ident_f[0:16, 0:16]))(i),
                         r=[B("fDT"), B("fDACS"), dtxB, B("ident_f")], w=[M1])
                S.op("tensor", lambda e: e.matmul(out=pb[7][:, 176:192], lhsT=AP(one16, 0, [[1, 128], [0, 128]]), rhs=diagcd[:, :], start=True, stop=True),
                     r=[B("one16"), B("diagcd")], w=[M1])
                S.op("vector", lambda e: e.tensor_copy(out=tm48[:, :], in_=pb[7][:, 128:176]), r=[M1], w=[B("tm48")])
                S.op("vector", lambda e: e.tensor_copy(out=tms[:, 3, :], in_=pb[7][:, 176:192]), r=[M1], w=[B("cd_bc")])
                S.op("vector", lambda e: e.tensor_scalar(out=tms[:, 0, :], in0=tm48[:, 16:32], scalar1=-1.0, scalar2=None, op0=ALU.mult),
                     r=[B("tm48")], w=[B("negdacs")])
                S.op("scalar", lambda e: e.activation(out=tms[:, 1, :], in_=tm48[:, 16:32], func=AF.Exp), r=[B("tm48")], w=[B("edacs")])
                S.op("vector", lambda e: e.tensor_tensor(out=tms[:, 2, :], in0=tm48[:, 0:16], in1=tm48[:, 32:48], op=ALU.mult),
                     r=[B("tm48")], w=[B("wdte")])

            def B3():
                for j in range(8):
                    S.op("tensor", (lambda j: lambda e: e.transpose(out=pbf(4)[:, j * 128:(j + 1) * 128], in_=xbcT[:, j, :],
                                                                   identity=ident_bf[:, :]))(j),
                         r=[XB, B("ident_bf")], w=[pbB[4]])
                S.op("vector", lambda e: e.tensor_copy(out=xs_tm[:, :], in_=pbf(4)), r=[pbB[4]], w=[B("xs_tm")])
                for g in range(4):
                    S.op("tensor", (lambda g: lambda e: e.transpose(out=pbf(7)[:, 512 + g * 128:512 + (g + 1) * 128], in_=xbcT[:, 8 + g, :],
                                                                   identity=ident_bf[:, :]))(g),
                         r=[XB, B("ident_bf")], w=[M1])
                S.op("scalar", lambda e: e.copy(out=B_tm[:, :], in_=pbf(7)[:, 512:1024]), r=[M1], w=[B("B_tm")])
                S.op("gpsimd", lambda e: e.tensor_tensor(out=xdt_bf[:, :].rearrange("p (h d) -> p h d", h=16),
                                                         in0=xs_tm[:, :].rearrange("p (h d) -> p h d", h=16),
                                                         in1=AP(tm48, 0, [[48, 128], [1, 16], [0, 64]]), op=ALU.mult),
                     r=[B("xs_tm"), B("tm48")], w=[B("xdt_bf")])
                S.op("gpsimd", lambda e: e.tensor_tensor(out=xdte_bf[:, :].rearrange("p (h d) -> p h d", h=16),
                                                         in0=xs_tm[:, :].rearrange("p (h d) -> p h d", h=16),
                                                         in1=AP(tms, 32, [[64, 128], [1, 16], [0, 64]]), op=ALU.mult),
                     r=[B("xs_tm"), B("wdte")], w=[B("xdte_bf")])

            def B4():
                for g in range(4):
                    S.op("tensor", (lambda g: lambda e: e.matmul(out=pb[6][:, g * 128:(g + 1) * 128], lhsT=xbcT[:, 8 + g, :],
                                                                 rhs=xbcT[:, 12 + g, :], start=True, stop=True))(g),
                         r=[XB], w=[pbB[6]])
                S.op("scalar", lambda e: e.copy(out=cbT_bf[:, :], in_=pb[6][:, :]), r=[pbB[6]], w=[B("cbT_bf")])
                for q in range(4):
                    bk = 4 + q % 2
                    S.op("tensor", (lambda bk: lambda e: e.matmul(out=pb[bk][:, :], lhsT=ident_bf[:, :], rhs=AP(negm, 0, [[128, 128], [0, 4], [1, 128]]),
                                                                  start=True, stop=False))(bk),
                         r=[B("ident_bf"), B("negm")], w=[pbB[bk]])
                    for r_ in range(4):
                        h = 4 * q + r_
                        for nm, src in (("dacs_hi", dacs_hi), ("dacs_lo", dacs_lo)):
                            S.op("tensor", (lambda bk, r_, h, src, nm: lambda e: e.matmul(
                                out=pb[bk][:, r_ * 128:(r_ + 1) * 128], lhsT=AP(ident_bf, h, [[128, 128], [0, 128]]), rhs=src[:, :],
                                start=False, stop=(nm == "dacs_lo" and r_ == 3)))(bk, r_, h, src, nm),
                                 r=[B("ident_bf"), B(nm)], w=[pbB[bk]])
                    for r_ in range(4):
                        h = 4 * q + r_
                        S.op("scalar", (lambda bk, r_, h: lambda e: e.activation(out=Ebf[:, h, :], in_=pb[bk][:, r_ * 128:(r_ + 1) * 128],
                                                                                func=AF.Exp, bias=tms[:, 0, h:h + 1]))(bk, r_, h),
                             r=[pbB[bk], B("negdacs")], w=[EB])
                S.op("vector", lambda e: e.tensor_tensor(out=Ebf.rearrange("p (g r) l -> p g r l", g=4),
                                                         in0=Ebf.rearrange("p (g r) l -> p g r l", g=4),
                                                         in1=AP(cbT_bf, 0, [[512, 128], [128, 4], [0, 4], [1, 128]]), op=ALU.mult),
                     r=[EB, B("cbT_bf")], w=[EB])

            def B5():
                for h in range(16):
                    bk = 4 + h // 8
                    S.op("tensor", (lambda h, bk: lambda e: e.matmul(out=pb[bk][:, (h % 8) * 64:(h % 8 + 1) * 64], lhsT=Ebf[:, h, :],
                                                                     rhs=xdt_bf[:, h * 64:(h + 1) * 64], start=True, stop=True))(h, bk),
                         r=[EB, B("xdt_bf")], w=[pbB[bk]])
                for g in range(4):
                    bk = 6 + g // 2
                    S.op("tensor", (lambda g, bk: lambda e: e.matmul(out=pb[bk][:, (g % 2) * 256:(g % 2 + 1) * 256], lhsT=xbcT[:, 12 + g, :],
                                                                     rhs=St_bf[:, g * 256:(g + 1) * 256], start=True, stop=True))(g, bk),
                         r=[XB, B("St_bf")], w=[pbB[bk]])
                for q in range(2):
                    S.op("vector", (lambda q: lambda e: e.tensor_tensor(out=t1[:, q * 512:(q + 1) * 512].rearrange("p (h d) -> p h d", h=8),
                                                                        in0=pb[6 + q][:, :].rearrange("p (h d) -> p h d", h=8),
                                                                        in1=AP(tms, 16 + 8 * q, [[64, 128], [1, 8], [0, 64]]), op=ALU.mult))(q),
                         r=[pbB[6 + q], B("edacs")], w=[EB])
                    S.op("vector", (lambda q: lambda e: e.tensor_tensor(out=t1[:, q * 512:(q + 1) * 512], in0=t1[:, q * 512:(q + 1) * 512],
                                                                        in1=pb[4 + q][:, :], op=ALU.add))(q),
                         r=[pbB[4 + q], EB], w=[EB])
                for g in range(4):
                    bk = 6 + g // 2
                    S.op("tensor", (lambda g, bk: lambda e: e.matmul(out=pb[bk][:, (g % 2) * 256:(g % 2 + 1) * 256],
                                                                     lhsT=B_tm[:, g * 128:(g + 1) * 128],
                                                                     rhs=xdte_bf[:, g * 256:(g + 1) * 256], start=True, stop=True))(g, bk),
                         r=[B("B_tm"), B("xdte_bf")], w=[pbB[bk]])
                S.op("gpsimd", lambda e: e.tensor_tensor(out=t2[:, :].rearrange("p (h d) -> p h d", h=16),
                                                         in0=xs_tm[:, :].rearrange("p (h d) -> p h d", h=16),
                                                         in1=AP(d_bc, 0, [[16, 128], [1, 16], [0, 64]]), op=ALU.mult),
                     r=[B("xs_tm"), B("d_bc"), XB], w=[XB])
                S.op("gpsimd", lambda e: e.tensor_tensor(out=t1[:, :], in0=t1[:, :], in1=t2[:, :], op=ALU.add), r=[EB, XB], w=[EB])
                S.op("vector", lambda e: e.tensor_tensor(out=t1[:, :], in0=t1[:, :], in1=szc[:, :], op=ALU.mult), r=[EB, szB], w=[EB])
                S.op("vector", lambda e: e.tensor_tensor(out=St[:, :].rearrange("p (h d) -> p h d", h=16),
                                                         in0=St[:, :].rearrange("p (h d) -> p h d", h=16),
                                                         in1=AP(tms, 48, [[64, 128], [1, 16], [0, 64]]), op=ALU.mult),
                     r=[B("St"), B("cd_bc")], w=[B("St")])
                for q in range(2):
                    S.op("vector", (lambda q: lambda e: e.tensor_tensor(out=St[:, q * 512:(q + 1) * 512], in0=St[:, q * 512:(q + 1) * 512],
                                                                        in1=pb[6 + q][:, :], op=ALU.add))(q),
                         r=[B("St"), pbB[6 + q]], w=[B("St")])
                S.op("scalar", lambda e: e.copy(out=St_bf[:, :], in_=St[:, :]), r=[B("St")], w=[B("St_bf")])

            def B6():
                for g in range(4):
                    S.op("scalar", (lambda g: lambda e: e.activation(out=xdt_bf[:, g * 256:(g + 1) * 256], in_=t1[:, g * 256:(g + 1) * 256],
                                                                     func=AF.Square, accum_out=stc(SSQB + g)))(g),
                         r=[EB], w=[B("xdt_bf"), B("ssqb")])
                S.op("scalar", lambda e: e.copy(out=stc(SCR0), in_=stc(SCR1)), r=[B("st_scr")], w=[B("ssqb")])
                rsqrt(SSQB, RSTDB, 4, 1.0 / 256, B("ssqb"), B("rstdb"), MSB)
                S.op("vector", lambda e: e.tensor_tensor(out=xs_tm[:, :].rearrange("p (g d) -> p g d", g=4),
                                                         in0=t1[:, :].rearrange("p (g d) -> p g d", g=4),
                                                         in1=AP(st, RSTDB, [[64, 128], [1, 4], [0, 256]]), op=ALU.mult),
                     r=[EB, B("rstdb")], w=[B("xs_tm")])
                for h in range(8):
                    S.op("tensor", (lambda h: lambda e: e.transpose(out=pbf(5)[:, h * 128:(h + 1) * 128],
                                                                   in_=xs_tm[:, h * 128:(h + 1) * 128], identity=ident_bf[:, :]))(h),
                         r=[B("xs_tm"), B("ident_bf")], w=[pbB[5]])
                S.op("scalar", lambda e: e.copy(out=yT[:, 8:16, :], in_=pbf(5).rearrange("p (k t) -> p k t", k=8)),
                     r=[pbB[5]], w=[B("yTb")])

            def B7():
                S.dma("sync", lambda e: e.dma_start(out=t2[:, :], in_=src_d[r0:r0 + T, :]), "xr", r=[dramB], w=[XB])
                for b in range(2):
                    for k in range(16):
                        S.op("tensor", (lambda b, k: lambda e: e.matmul(out=pb[4 + b][:, :], lhsT=yT[:, k, :],
                                                                        rhs=wout[:, k, b * 512:(b + 1) * 512], start=(k == 0), stop=(k == 15)))(b, k),
                             r=[B("yTa"), B("yTb"), B("wout")], w=[pbB[4 + b]])
                for b in range(2):
                    S.op("scalar", (lambda b: lambda e: e.activation(out=xdt_bf[:, b * 512:(b + 1) * 512], in_=pb[4 + b][:, :],
                                                                     func=AF.Square, accum_out=stc(SSQM + b)))(b),
                         r=[pbB[4 + b]], w=[B("xdt_bf"), B("ssqm")])
                S.op("scalar", lambda e: e.copy(out=stc(SCR0), in_=stc(SCR1)), r=[B("st_scr")], w=[B("ssqm")])
                S.op("vector", lambda e: e.tensor_tensor(out=stc(SSQM), in0=stc(SSQM), in1=stc(SSQM + 1), op=ALU.add),
                     r=[B("ssqm")], w=[B("ssqm")])
                rsqrt(SSQM, RSTDM, 1, 1.0 / D, B("ssqm"), B("rstdm"), MSM)
                for b in range(2):
                    S.op("vector", (lambda b: lambda e: e.scalar_tensor_tensor(out=t1[:, b * 512:(b + 1) * 512], in0=pb[4 + b][:, :],
                                                                               scalar=stc(RSTDM), in1=post_bc[:, b * 512:(b + 1) * 512],
                                                                               op0=ALU.mult, op1=ALU.mult))(b),
                         r=[pbB[4 + b], B("rstdm"), B("post_bc")], w=[EB])
                S.op("gpsimd", lambda e: e.tensor_tensor(out=t1[:, :], in0=t1[:, :], in1=t2[:, :], op=ALU.add), r=[EB, XB], w=[EB])
                S.dma("sync", lambda e: e.dma_start(out=dst_d[r0:r0 + T, :], in_=t1[:, :]), "xs", r=[EB], w=[dramB])
            return [B2, B3, B4, B5, B6, B7]

        def Aprime(As):
            b1 = As["B1"]
            return S.record([As["A1"], lambda: b1(0, 4), As["A2"], lambda: b1(4, 8), As["A3"], lambda: b1(8, 12), As["A5"],
                             lambda: b1(12, 16), As["A4a"]])

        As = stageA(0)
        for o in Aprime(As):
            S.play(o)
        for c in range(NCH):
            Bs = stageB(c)
            X = S.record(Bs[:5])
            xbar = len(X)
            X += S.record(Bs[5:])
            Y = S.record([As["A4b"]])
            ybar = len(Y)
            An = stageA(c + 1) if c + 1 < NCH else None
            if An is not None:
                Y += Aprime(An)
            elif l + 1 < NL:
                load_win(l + 1)
            S.merge(X, Y, ybar, xbar)
            As = An

    for l_ in range(NL):
        do_layer(l_, x_d if l_ == 0 else xbuf_d, out_d if l_ == NL - 1 else xbuf_d)
    global LAST_S
    LAST_S = S
    S.emit(nc, es)
    es.close()
    return nc


_CONSTS = None


def _consts():
    global _CONSTS
    if _CONSTS is None:
        ident = np.eye(T, dtype=np.float32)
        tril = np.tril(np.ones((T, T), np.float32))
        sidx = np.arange(T)[:, None]
        lidx = np.arange(T)[None, :]
        negm1 = np.where(sidx <= lidx, 0.0, NEG).astype(np.float32)
        negm = np.tile(negm1, (1, 4))
        sel = np.zeros((T, 16, T), np.float32)
        for h in range(16):
            sel[h, h, :] = 1.0
        one16 = np.zeros((T, T), np.float32)
        one16[0:16, :] = 1.0
        _CONSTS = dict(c_ident=ident, c_tril=tril, c_negm=negm, c_sel=sel.reshape(T, 16 * T), c_one16=one16)
    return _CONSTS


_PROG = {}


def _get_prog(NL, NCH):
    key = (NL, NCH)
    if key not in _PROG:
        _PROG[key] = build_program(NL, NCH)
    return _PROG[key]


PARAMS = ["pre_norm_w", "w_in", "gmlp_v_norm_w", "gmlp_v_norm_b", "gmlp_ws", "gmlp_bs", "gmlp_norm_w",
          "conv_w", "conv_b", "dt_bias", "a_log", "d_skip", "ssd_norm_w", "w_out", "post_norm_w"]

FUSED = True


def kernel(**inputs):
    x = np.ascontiguousarray(np.asarray(inputs["x"], dtype=np.float32))
    Bn, Ls, _ = x.shape
    NCH = Ls // T
    depth = inputs["w_in"].shape[0]
    par = {k: np.ascontiguousarray(np.asarray(inputs[k], dtype=np.float32)) for k in PARAMS}
    cs = _consts()
    if FUSED:
        nc = _get_prog(depth, NCH)
        in_maps = [dict(x=x[b], **par, **cs) for b in range(Bn)]
        res = run_bass_kernel_spmd(nc, in_maps, core_ids=list(range(Bn)))
        return np.stack([np.asarray(r["out"]) for r in res.results], axis=0).astype(np.float32)
    cur = [x[b] for b in range(Bn)]
    nc = _get_prog(1, NCH)
    for l in range(depth):
        pl = {k: np.ascontiguousarray(v[l:l + 1]) for k, v in par.items()}
        in_maps = [dict(x=cur[b], **pl, **cs) for b in range(Bn)]
        res = run_bass_kernel_spmd(nc, in_maps, core_ids=list(range(Bn)))
        cur = [np.ascontiguousarray(np.asarray(r["out"], dtype=np.float32)) for r in res.results]
    return np.stack(cur, axis=0).astype(np.float32)
```

```python
import numpy as np
from contextlib import ExitStack
import concourse.bass as bass
import concourse.mybir as mybir
from concourse.bass_utils import run_bass_kernel_spmd

F32 = mybir.dt.float32
BF16 = mybir.dt.bfloat16
AF = mybir.ActivationFunctionType
ALU = mybir.AluOpType
AX = mybir.AxisListType

D = 1024
T = 128
INC = 6160
EPS = 1e-6
NEG = -60000.0
import os
DBG = int(os.environ.get("KDBG", "99"))
ENGS = ("tensor", "vector", "scalar", "gpsimd", "sync")


class _Fake:
    def __init__(self):
        self.name = None
        self.kw = {}

    def __getattr__(self, name):
        def f(*a, **kw):
            self.name = name
            self.kw = kw
            return None
        return f


def _fsize(ap):
    n = 1
    for d in list(ap.shape)[1:]:
        n *= int(d)
    return n


def op_cost(eng, fn, kind):
    if kind == "dma":
        return 2.5
    fk = _Fake()
    try:
        fn(fk)
    except Exception:
        return 0.5
    kw = fk.kw
    n = 128
    for key in ("in_", "in0", "rhs", "out", "data1"):
        if key in kw and hasattr(kw[key], "shape"):
            n = _fsize(kw[key])
            break
    if eng == "tensor":
        if fk.name == "transpose":
            return 0.12
        return 0.07 + n / 2400.0 + 0.05
    if eng == "vector":
        return 0.06 + (150 + n) / 960.0
    if eng == "scalar":
        return 0.17 + n / 1200.0 + (0.15 if "accum_out" in kw else 0.0)
    if eng == "gpsimd":
        return 0.8 + n / 800.0
    return 0.1


class Buf:
    __slots__ = ("name", "writer", "readers", "excl", "wreal")

    def __init__(self, name, excl=False):
        self.name = name
        self.writer = None
        self.readers = []
        self.wreal = True
        self.excl = excl


class Item:
    __slots__ = ("fn", "waits", "kind", "idx", "dkey")

    def __init__(self, fn, waits, kind, idx, dkey=None):
        self.fn, self.waits, self.kind, self.idx, self.dkey = fn, waits, kind, idx, dkey


class Sched:
    def __init__(self):
        self.items = {e: [] for e in ENGS}
        self.cnt = {e: 0 for e in ENGS}
        self.seen = {e: {} for e in ENGS}
        self.dma_cnt = {}
        self.signal = {e: set() for e in ENGS}
        self.rec = None
        self.efree = {e: 0.0 for e in ENGS}
        self.done = {}

    def peek(self, reads, writes):
        deps = []
        for b in reads:
            if b.writer is not None:
                deps.append(b.writer)
            if b.excl:
                deps.extend(b.readers)
        for b in writes:
            if b.writer is not None:
                deps.append(b.writer)
            deps.extend(b.readers)
        return deps

    def est_start(self, eng, reads, writes):
        t = self.efree[eng]
        for tok in self.peek(reads, writes):
            d = self.done.get(tok, 0.0)
            if not (tok[0] == "e" and tok[1] == eng):
                d += 0.15
            t = max(t, d)
        return t

    def record(self, fns):
        self.rec = []
        for fn in fns:
            fn()
        ops, self.rec = self.rec, None
        return ops

    def play(self, o):
        kind, eng, fn, key, r, w = o
        st = self.est_start(eng, r, w)
        c = op_cost(eng, fn, kind)
        if kind == "op":
            tok = self.op(eng, fn, r, w)
            self.efree[eng] = st + c
            self.done[tok] = st + c
        else:
            tok = self.dma(eng, fn, key, r, w)
            self.efree[eng] = st + 0.1
            self.done[tok] = st + c
        return tok

    def merge(self, X, Y, ybar, xbar):
        i = j = 0
        while i < len(X) or j < len(Y):
            if i >= len(X):
                self.play(Y[j]); j += 1
                continue
            if j >= len(Y):
                self.play(X[i]); i += 1
                continue
            if i >= xbar and j < ybar:
                self.play(Y[j]); j += 1
                continue
            sx = self.est_start(X[i][1], X[i][4], X[i][5])
            sy = self.est_start(Y[j][1], Y[j][4], Y[j][5])
            if sx <= sy:
                self.play(X[i]); i += 1
            else:
                self.play(Y[j]); j += 1

    def _collect(self, eng, reads, writes, is_dma):
        deps = []
        for b in reads:
            if b.writer is not None:
                deps.append(b.writer)
            if b.excl:
                deps.extend(b.readers)
        for b in writes:
            if b.writer is not None:
                deps.append(b.writer)
            deps.extend(b.readers)
        waits = []
        for tok in deps:
            if tok[0] == "e":
                _, src, idx = tok
                if src == eng and not is_dma:
                    if eng == "tensor" or not any((b.writer == tok and b.wreal) for b in reads):
                        continue
                if self.seen[eng].get(src, -1) >= idx:
                    continue
                self.seen[eng][src] = idx
                self.signal[src].add(idx)
                waits.append(tok)
            else:
                _, key, val = tok
                if self.seen[eng].get(("d", key), 0) >= val:
                    continue
                self.seen[eng][("d", key)] = val
                waits.append(tok)
        best = {}
        for tok in waits:
            k = tok[1]
            if k not in best or tok[2] > best[k][2]:
                best[k] = tok
        return list(best.values())

    def op(self, eng, fn, r=(), w=()):
        if self.rec is not None:
            self.rec.append(("op", eng, fn, None, tuple(r), tuple(w)))
            return None
        waits = self._collect(eng, r, w, False)
        idx = self.cnt[eng]
        self.cnt[eng] += 1
        tok = ("e", eng, idx)
        self.items[eng].append(Item(fn, waits, "c", idx))
        for b in r:
            if b.excl:
                b.writer = tok
                b.wreal = False
                b.readers = []
            else:
                b.readers.append(tok)
        for b in w:
            b.writer = tok
            b.wreal = True
            b.readers = []
        return tok

    def dma(self, eng, fn, key, r=(), w=()):
        if self.rec is not None:
            self.rec.append(("dma", eng, fn, key, tuple(r), tuple(w)))
            return None
        waits = self._collect(eng, r, w, True)
        n = self.dma_cnt.get(key, 0) + 1
        self.dma_cnt[key] = n
        tok = ("d", key, 16 * n)
        self.items[eng].append(Item(fn, waits, "d", None, key))
        for b in r:
            b.readers.append(tok)
        for b in w:
            b.writer = tok
            b.readers = []
        return tok

    def emit(self, nc, es):
        esem = {e: es.enter_context(nc.semaphore("es_" + e)) for e in ENGS if e != "sync"}
        dsem = {k: es.enter_context(nc.semaphore("ds_" + str(k))) for k in self.dma_cnt}
        rank = {}
        for e in ENGS:
            for i, idx in enumerate(sorted(self.signal[e])):
                rank[(e, idx)] = i + 1
        final_dma = [(dsem[k], 16 * n) for k, n in self.dma_cnt.items()]

        def run(ename, e):
            for it in self.items[ename]:
                for tok in it.waits:
                    if tok[0] == "e":
                        e.wait_ge(esem[tok[1]], rank[(tok[1], tok[2])])
                    else:
                        e.wait_ge(dsem[tok[1]], tok[2])
                ins = it.fn(e)
                if it.kind == "c":
                    if it.idx in self.signal[ename]:
                        ins.then_inc(esem[ename], 1)
                else:
                    ins.then_inc(dsem[it.dkey], 16)
            if ename == "sync":
                for s, v in final_dma:
                    e.wait_ge(s, v)

        with nc.Block() as block:
            @block.tensor
            def _(e):
                run("tensor", e)

            @block.vector
            def _(e):
                run("vector", e)

            @block.scalar
            def _(e):
                run("scalar", e)

            @block.gpsimd
            def _(e):
                run("gpsimd", e)

            @block.sync
            def _(e):
                run("sync", e)


def AP(t, off, dims):
    return bass.AP(t, off, [list(d) for d in dims])


def build_program(NL, NCH):
    nc = bass.Bass("TRN2", target_bir_lowering=False)
    L = NCH * T

    def din(name, shape):
        return nc.dram_tensor(name, shape, F32, kind="ExternalInput").ap()

    x_d = din("x", [L, D])
    pre_d = din("pre_norm_w", [NL, D])
    win_d = din("w_in", [NL, D, INC])
    vw_d = din("gmlp_v_norm_w", [NL, D])
    vb_d = din("gmlp_v_norm_b", [NL, D])
    ws_d = din("gmlp_ws", [NL, 8, T, T])
    bs_d = din("gmlp_bs", [NL, 8, T])
    gnw_d = din("gmlp_norm_w", [NL, D])
    cw_d = din("conv_w", [NL, 4, 2048])
    cb_d = din("conv_b", [NL, 2048])
    dtb_d = din("dt_bias", [NL, 16])
    alog_d = din("a_log", [NL, 16])
    dsk_d = din("d_skip", [NL, 16])
    snw_d = din("ssd_norm_w", [NL, D])
    wout_d = din("w_out", [NL, 2 * D, D])
    post_d = din("post_norm_w", [NL, D])
    c_ident = din("c_ident", [T, T])
    c_tril = din("c_tril", [T, T])
    c_negm = din("c_negm", [T, 512])
    c_sel = din("c_sel", [T, 16 * T])
    c_one16 = din("c_one16", [T, T])
    out_d = nc.dram_tensor("out", [L, D], F32, kind="ExternalOutput").ap()
    xbuf_d = nc.dram_tensor("xbuf", [L, D], F32, kind="Internal").ap() if NL > 1 else None

    es = ExitStack()
    S = Sched()
    bufs = {}

    def B(name):
        if name not in bufs:
            bufs[name] = Buf(name)
        return bufs[name]

    def sb(name, shape, dt):
        return es.enter_context(nc.sbuf_tensor(name, shape, dt))

    win = sb("win", [128, 8, INC], BF16)
    wout = sb("wout", [128, 16, D], BF16)
    ident_bf = sb("ident_bf", [128, 128], BF16)
    ident_f = sb("ident_f", [128, 16], F32)
    negm = sb("negm", [128, 128], BF16)
    one16 = sb("one16", [128, 1], F32)
    mhalf = sb("mhalf", [128, 8], F32)
    post_bc = sb("post_bc", [128, D], F32)
    vw_bc = sb("vw_bc", [128, D], F32)
    bias_t = sb("bias_t", [128, D], F32)
    wsT = sb("wsT", [128, 8, T], BF16)
    prm = sb("prm", [128, 112], F32)
    dtb = sb("dtb", [128, 1], F32)
    alog = sb("alog", [128, 1], F32)
    aneg = sb("aneg", [128, 1], F32)
    d_bc = sb("d_bc", [128, 16], F32)
    rs = sb("rs", [128, 8], F32)

    x_sb = sb("x_sb", [128, D], F32)
    xbf = sb("xbf", [128, D], BF16)
    hT = [sb("hT%d" % i, [128, 8, 131], BF16) for i in range(2)]
    halo = sb("halo", [128, 8, 3], BF16)
    g_sb = sb("g_sb", [128, D], F32)
    sz = [sb("sz%d" % i, [128, D], F32) for i in range(2)]
    dtx = [sb("dtx%d" % i, [128, T], F32) for i in range(2)]
    y_sb = sb("y_sb", [128, D], F32)
    yn_bf = sb("yn_bf", [128, D], BF16)
    vn_bf = yn_bf
    yT = sb("yT", [128, 16, T], BF16)
    acc = [sb("acc%d" % i, [128, T], F32) for i in range(2)]
    xbc_fs = [sb("xbc_f%d" % i, [128, D], F32) for i in range(2)]
    xs_tm = sb("xs_tm", [128, D], BF16)
    B_tm = sb("B_tm", [128, 512], BF16)
    fm = sb("fm", [128, 5, T], F32)
    dacs_hi = sb("dacs_hi", [128, T], BF16)
    dacs_lo = sb("dacs_lo", [128, T], BF16)
    tm48 = sb("tm48", [128, 48], F32)
    tms = sb("tms", [128, 4, 16], F32)
    diagcd = sb("diagcd", [128, 16], F32)
    cbT_bf = sb("cbT_bf", [128, 512], BF16)
    E_f = sb("E_f", [128, D], F32)
    xdt_bf = sb("xdt_bf", [128, D], BF16)
    xdte_bf = sb("xdte_bf", [128, D], BF16)
    St = sb("St", [128, D], F32)
    St_bf = sb("St_bf", [128, D], BF16)
    st = sb("st", [128, 64], F32)
    bnst = sb("bnst", [128, 2, 6], F32)

    pb = [es.enter_context(nc.psum_tensor("pb%d" % i, [128, 512], F32)) for i in range(8)]
    pbB = [B("pb%d" % i) for i in range(8)]
    for b_ in pbB:
        b_.excl = True

    def pbf(i):
        return pb[i][:, :].bitcast(BF16)

    SSQ, MS, RSTD, RSTDV, NMR, MSV, RSTDM, MSM, SCR0, SCR1 = range(10)
    SSQG, MSG, RSTDG = 16, 24, 32
    SSQB, MSB, RSTDB = 40, 44, 48
    SSQM = 52
    MV = 56

    def stc(c, n=1):
        return st[:, c:c + n]

    S.dma("sync", lambda e: e.dma_start(out=ident_f[:, :], in_=c_ident[:, 0:16]), "c0", w=[B("ident_f")])
    S.op("vector", lambda e: e.memset(one16[:, :], 1.0), w=[B("one16")])
    S.dma("gpsimd", lambda e: e.dma_start(out=ident_bf[:, :], in_=c_ident), "c3", w=[B("ident_bf")])
    S.dma("gpsimd", lambda e: e.dma_start(out=negm[:, :], in_=c_negm[:, 0:128]), "c4", w=[B("negm")])
    S.op("vector", lambda e: e.memset(mhalf[:, :], -0.5), w=[B("mhalf")])
    S.op("vector", lambda e: e.memset(dacs_hi[:, :], 0.0), w=[B("dacs_hi")])
    S.op("vector", lambda e: e.memset(dacs_lo[:, :], 0.0), w=[B("dacs_lo")])
    S.op("vector", lambda e: e.memset(diagcd[:, :], 0.0), w=[B("diagcd")])
    S.op("vector", lambda e: e.memset(st[:, SCR0:SCR1 + 1], 0.0), w=[B("st_scr")])

    def rsqrt(src_c, dst_c, n, scale, rbuf, wbuf, tmp_c):
        S.op("vector", lambda e: e.tensor_scalar(out=stc(tmp_c, n), in0=stc(src_c, n), scalar1=scale,
                                                 scalar2=EPS, op0=ALU.mult, op1=ALU.add),
             r=[rbuf], w=[B("tmp%d" % tmp_c)])
        S.op("gpsimd", lambda e: e.tensor_tensor(out=stc(dst_c, n), in0=stc(tmp_c, n), in1=mhalf[:, 0:n],
                                                 op=ALU.pow),
             r=[B("tmp%d" % tmp_c), B("mhalf")], w=[wbuf])

    def do_layer(l, src_d, dst_d):
        winB = [B("win_k%d" % k) for k in range(8)]

        def load_win(ll):
            for k in range(8):
                S.dma("gpsimd", (lambda k: lambda e: e.dma_start(out=win[:, k, :], in_=win_d[ll, k * 128:(k + 1) * 128, :]))(k),
                      "win%d" % k, w=[winB[k]])
        if l == 0:
            load_win(0)
        if True:
            stgp = g_sb[0:16, 0:896].rearrange("p (g t) -> p g t", g=7)
            for kk in range(4):
                S.dma("sync", (lambda kk: lambda e: e.dma_start(out=stgp[:, kk, :], in_=cw_d[l, kk].rearrange("(j p) -> j p", p=128)))(kk),
                      "q%d" % kk, w=[B("g_sb")])
            S.dma("sync", lambda e: e.dma_start(out=stgp[:, 4, :], in_=cb_d[l].rearrange("(j p) -> j p", p=128)), "q4", w=[B("g_sb")])
            S.dma("sync", lambda e: e.dma_start(out=stgp[0:8, 5, :], in_=pre_d[l].rearrange("(j p) -> j p", p=128)), "q5", w=[B("g_sb")])
            S.dma("sync", lambda e: e.dma_start(out=g_sb[8:16, 640:768], in_=gnw_d[l].rearrange("(j p) -> j p", p=128)), "q6", w=[B("g_sb")])
            S.dma("sync", lambda e: e.dma_start(out=stgp[0:8, 6, :], in_=snw_d[l].rearrange("(j p) -> j p", p=128)), "q7", w=[B("g_sb")])
            S.dma("sync", lambda e: e.dma_start(out=g_sb[8:16, 768:896], in_=bs_d[l]), "q8", w=[B("g_sb")])
            for gi in range(7):
                S.op("tensor", (lambda gi: lambda e: e.transpose(out=pb[6][:, gi * 16:(gi + 1) * 16], in_=g_sb[0:16, gi * 128:(gi + 1) * 128],
                                                                identity=ident_f[0:16, 0:16]))(gi),
                     r=[B("g_sb"), B("ident_f")], w=[pbB[6]])
            S.op("vector", lambda e: e.tensor_copy(out=prm[:, :], in_=pb[6][:, 0:112]), r=[pbB[6]], w=[B("prm")])
            S.dma("sync", lambda e: e.dma_start(out=dtb[0:16, :], in_=dtb_d[l].rearrange("(h o) -> h o", o=1), allow_slow_non_contiguous=True), "p6", w=[B("dtb")])
            S.dma("sync", lambda e: e.dma_start(out=alog[0:16, :], in_=alog_d[l].rearrange("(h o) -> h o", o=1), allow_slow_non_contiguous=True), "p7", w=[B("alog")])
            S.dma("sync", lambda e: e.dma_start(out=d_bc[:, :], in_=dsk_d[l].partition_broadcast(128), allow_slow_non_contiguous=True), "p8", w=[B("d_bc")])
            S.dma("sync", lambda e: e.dma_start(out=post_bc[:, :], in_=post_d[l].partition_broadcast(128), allow_slow_non_contiguous=True), "p9", w=[B("post_bc")])
            S.dma("sync", lambda e: e.dma_start(out=vw_bc[:, :], in_=vw_d[l].partition_broadcast(128), allow_slow_non_contiguous=True), "p10", w=[B("vw_bc")])
            S.dma("sync", lambda e: e.dma_start(out=sz[0][:, :], in_=vb_d[l].partition_broadcast(128), allow_slow_non_contiguous=True), "p11", w=[B("sz0")])
            S.dma("sync", lambda e: e.dma_start(out=y_sb[:, :].rearrange("t (h s) -> t h s", h=8),
                                               in_=ws_d[l].rearrange("h t s -> t h s"), allow_slow_non_contiguous=True), "p12", w=[B("y_sb")])
        S.op("scalar", lambda e: e.activation(out=aneg[0:16, :], in_=alog[0:16, :], func=AF.Exp), r=[B("alog")], w=[B("aneg")])
        S.op("vector", lambda e: e.tensor_scalar(out=aneg[0:16, :], in0=aneg[0:16, :], scalar1=-1.0, scalar2=None, op0=ALU.mult),
             r=[B("aneg")], w=[B("aneg")])
        S.dma("sync", lambda e: e.dma_start(out=St[:, 0:128], in_=c_tril), "c1", w=[B("St")])
        ws3 = y_sb[:, :].rearrange("t (h s) -> t h s", h=8)
        S.op("vector", lambda e: e.tensor_tensor(out=ws3, in0=ws3, in1=AP(St, 0, [[1024, 128], [0, 8], [1, 128]]), op=ALU.mult),
             r=[B("y_sb"), B("St")], w=[B("y_sb")])
        S.op("vector", lambda e: e.tensor_reduce(out=rs[:, :], in_=ws3, axis=AX.X, op=ALU.add), r=[B("y_sb")], w=[B("rs")])
        S.op("vector", lambda e: e.tensor_copy(out=yn_bf[:, :], in_=y_sb[:, :]), r=[B("y_sb")], w=[B("yn_bf")])
        for h in range(8):
            bk = 4 + h // 4
            S.op("tensor", (lambda h, bk: lambda e: e.transpose(out=pbf(bk)[:, (h % 4) * 128:(h % 4 + 1) * 128],
                                                               in_=yn_bf[:, h * 128:(h + 1) * 128], identity=ident_bf[:, :]))(h, bk),
                 r=[B("yn_bf"), B("ident_bf")], w=[pbB[bk]])
        for q in range(2):
            S.op("vector", (lambda q: lambda e: e.tensor_copy(out=wsT[:, 4 * q:4 * q + 4, :],
                                                              in_=pbf(4 + q)[:, 0:512].rearrange("p (h t) -> p h t", h=4)))(q),
                 r=[pbB[4 + q]], w=[B("wsT")])
        for h in range(8):
            S.op("vector", (lambda h: lambda e: e.tensor_scalar(out=bias_t[:, h * 128:(h + 1) * 128], in0=sz[0][:, h * 128:(h + 1) * 128],
                                                                scalar1=rs[:, h:h + 1], scalar2=prm[:, 104 + h:105 + h],
                                                                op0=ALU.mult, op1=ALU.add))(h),
                 r=[B("sz0"), B("rs"), B("prm")], w=[B("bias_t")])
        stg = [(g_sb, B("g_sb")), (y_sb, B("y_sb")), (sz[1], B("sz1")), (xbc_fs[0], B("xbc_f0"))]
        for k in range(16):
            tt, tb = stg[k % 4]
            S.dma("sync", (lambda k, tt: lambda e: e.dma_start(out=tt[:, :], in_=wout_d[l, k * 128:(k + 1) * 128, :]))(k, tt),
                  "wo%d" % (k % 4), w=[tb])
            if k % 2 == 0:
                S.op("vector", (lambda k, tt: lambda e: e.tensor_scalar(out=wout[:, k, :], in0=tt[:, :], scalar1=prm[:, 88 + k:89 + k],
                                                                        scalar2=None, op0=ALU.mult))(k, tt),
                     r=[tb, B("prm")], w=[B("wout")])
            else:
                S.op("scalar", (lambda k, tt: lambda e: e.activation(out=wout[:, k, :], in_=tt[:, :], func=AF.Copy,
                                                                     scale=prm[:, 88 + k:89 + k]))(k, tt),
                     r=[tb, B("prm")], w=[B("wout")])
        S.op("gpsimd", lambda e: e.memset(St[:, :], 0.0), w=[B("St")])
        S.op("gpsimd", lambda e: e.memset(St_bf[:, :], 0.0), w=[B("St_bf")])
        S.op("gpsimd", lambda e: e.memset(halo[:, :, :], 0.0), w=[B("halo")])

        XR_, AB, E1, L1, DT, DACS = None, 0, 1, 2, 3, 4
        DTA = AB
        Ebf = E_f[:, :].bitcast(BF16).rearrange("p (h l) -> p h l", h=16)
        EB = B("E_f")

        def f(i):
            return fm[0:16, i, :]

        def stageA(c):
            par = c % 2
            hTc, hB = hT[par], B("hT%d" % par)
            szc, szB = sz[par], B("sz%d" % par)
            dtxc, dtxB = dtx[par], B("dtx%d" % par)
            dramB = B("dram%d" % c)
            r0 = c * T
            xB = B("x_sb")
            xbc_f = xbc_fs[par]
            xbcT = xbc_f[:, :].bitcast(BF16).rearrange("p (j t) -> p j t", j=16)
            XB = B("xbc_f%d" % par)

            def A1():
                S.dma("sync", lambda e: e.dma_start(out=x_sb[:, :], in_=src_d[r0:r0 + T, :]), "xl", r=[dramB], w=[xB])
                S.op("scalar", lambda e: e.activation(out=xbf[:, :], in_=x_sb[:, :], func=AF.Square, accum_out=stc(SSQ)),
                     r=[xB], w=[B("xbf"), B("ssq")])
                S.op("scalar", lambda e: e.copy(out=stc(SCR0), in_=stc(SCR1)), r=[B("st_scr")], w=[B("ssq")])
                rsqrt(SSQ, RSTD, 1, 1.0 / D, B("ssq"), B("rstd"), MS)
                S.op("vector", lambda e: e.tensor_scalar(out=xbf[:, :], in0=x_sb[:, :], scalar1=stc(RSTD), scalar2=None, op0=ALU.mult),
                     r=[xB, B("rstd")], w=[B("xbf")])
                for k in range(8):
                    S.op("tensor", (lambda k: lambda e: e.transpose(out=pbf(0)[:, k * 128:(k + 1) * 128],
                                                                   in_=xbf[:, k * 128:(k + 1) * 128], identity=ident_bf[:, :]))(k),
                         r=[B("xbf"), B("ident_bf")], w=[pbB[0]])
                S.op("gpsimd", lambda e: e.tensor_copy(out=hTc[:, :, 0:3], in_=halo[:, :, :]), r=[B("halo")], w=[hB])
                S.op("vector", lambda e: e.tensor_tensor(out=hTc[:, :, 3:131], in0=pbf(0).rearrange("p (k t) -> p k t", k=8),
                                                         in1=AP(prm, 80, [[112, 128], [1, 8], [0, 128]]), op=ALU.mult),
                     r=[pbB[0], B("prm")], w=[hB])
                S.op("gpsimd", lambda e: e.tensor_copy(out=halo[:, :, :], in_=hTc[:, :, 128:131]), r=[hB], w=[B("halo")])

            def proj(bank, col0):
                for k in range(8):
                    S.op("tensor", (lambda k: lambda e: e.matmul(out=pb[bank][:, :], lhsT=hTc[:, k, 3:131],
                                                                 rhs=win[:, k, col0:col0 + 512], start=(k == 0), stop=(k == 7)))(k),
                         r=[hB, winB[k]], w=[pbB[bank]])

            def A2():
                proj(2, 1024)
                proj(3, 1536)
                S.op("vector", lambda e: e.bn_stats(out=bnst[:, 0, :], in_=pb[2][:, :]), r=[pbB[2]], w=[B("bnst")])
                S.op("vector", lambda e: e.bn_stats(out=bnst[:, 1, :], in_=pb[3][:, :]), r=[pbB[3]], w=[B("bnst")])
                S.op("vector", lambda e: e.bn_aggr(out=stc(MV, 2), in_=bnst[:, :, :].rearrange("p a b -> p (a b)")),
                     r=[B("bnst")], w=[B("mv")])
                S.op("vector", lambda e: e.tensor_scalar(out=stc(MSV), in0=stc(MV + 1), scalar1=EPS, scalar2=None, op0=ALU.add),
                     r=[B("mv")], w=[B("msv")])
                S.op("gpsimd", lambda e: e.tensor_tensor(out=stc(RSTDV), in0=stc(MSV), in1=mhalf[:, 0:1], op=ALU.pow),
                     r=[B("msv"), B("mhalf")], w=[B("rstdv")])
                S.op("vector", lambda e: e.scalar_tensor_tensor(out=stc(NMR), in0=stc(MV), scalar=-1.0, in1=stc(RSTDV),
                                                                op0=ALU.mult, op1=ALU.mult),
                     r=[B("mv"), B("rstdv")], w=[B("nmr")])
                for i, bank in enumerate((2, 3)):
                    S.op("scalar", (lambda i, bank: lambda e: e.activation(out=y_sb[:, i * 512:(i + 1) * 512], in_=pb[bank][:, :],
                                                                           func=AF.Identity, bias=stc(NMR), scale=stc(RSTDV)))(i, bank),
                         r=[pbB[bank], B("nmr"), B("rstdv")], w=[B("y_sb")])
                S.op("gpsimd", lambda e: e.tensor_tensor(out=vn_bf[:, :], in0=y_sb[:, :], in1=vw_bc[:, :], op=ALU.mult),
                     r=[B("y_sb"), B("vw_bc")], w=[B("yn_bf")])

            def A3():
                proj(2, 2048)
                proj(3, 2560)
                for i, bank in enumerate((2, 3)):
                    S.op("scalar", (lambda i, bank: lambda e: e.activation(out=g_sb[:, i * 512:(i + 1) * 512], in_=pb[bank][:, :],
                                                                           func=AF.Silu))(i, bank),
                         r=[pbB[bank]], w=[B("g_sb")])
                proj(2, 0)
                proj(3, 512)
                for i, bank in enumerate((2, 3)):
                    S.op("vector", (lambda i, bank: lambda e: e.tensor_tensor(out=g_sb[:, i * 512:(i + 1) * 512], in0=g_sb[:, i * 512:(i + 1) * 512],
                                                                              in1=pb[bank][:, :], op=ALU.mult))(i, bank),
                         r=[pbB[bank], B("g_sb")], w=[B("g_sb")])

            def A4():
                for h in range(8):
                    bk = 2 + h // 4
                    S.op("tensor", (lambda h, bk: lambda e: e.matmul(out=pb[bk][:, (h % 4) * 128:(h % 4 + 1) * 128], lhsT=wsT[:, h, :],
                                                                     rhs=vn_bf[:, h * 128:(h + 1) * 128], start=True, stop=True))(h, bk),
                         r=[B("wsT"), B("yn_bf")], w=[pbB[bk]])
                for q in range(2):
                    S.op("vector", (lambda q: lambda e: e.tensor_tensor(out=y_sb[:, q * 512:(q + 1) * 512], in0=pb[2 + q][:, :],
                                                                        in1=bias_t[:, q * 512:(q + 1) * 512], op=ALU.add))(q),
                         r=[pbB[2 + q], B("bias_t")], w=[B("y_sb")])
                S.op("vector", lambda e: e.tensor_tensor(out=y_sb[:, :], in0=y_sb[:, :], in1=g_sb[:, :], op=ALU.mult),
                     r=[B("y_sb"), B("g_sb")], w=[B("y_sb")])
                for h in range(8):
                    S.op("scalar", (lambda h: lambda e: e.activation(out=xbf[:, h * 128:(h + 1) * 128], in_=y_sb[:, h * 128:(h + 1) * 128],
                                                                     func=AF.Square, accum_out=stc(SSQG + h)))(h),
                         r=[B("y_sb")], w=[B("xbf"), B("ssqg")])
                S.op("scalar", lambda e: e.copy(out=stc(SCR0), in_=stc(SCR1)), r=[B("st_scr")], w=[B("ssqg")])
                rsqrt(SSQG, RSTDG, 8, 1.0 / 128, B("ssqg"), B("rstdg"), MSG)
                S.op("vector", lambda e: e.tensor_tensor(out=yn_bf[:, :].rearrange("p (h d) -> p h d", h=8),
                                                         in0=y_sb[:, :].rearrange("p (h d) -> p h d", h=8),
                                                         in1=AP(st, RSTDG, [[64, 128], [1, 8], [0, 128]]), op=ALU.mult),
                     r=[B("y_sb"), B("rstdg")], w=[B("yn_bf")])

            def A4b():
                for h in range(8):
                    S.op("tensor", (lambda h: lambda e: e.transpose(out=pbf(0)[:, h * 128:(h + 1) * 128],
                                                                   in_=yn_bf[:, h * 128:(h + 1) * 128], identity=ident_bf[:, :]))(h),
                         r=[B("yn_bf"), B("ident_bf")], w=[pbB[0]])
                S.op("scalar", lambda e: e.copy(out=yT[:, 0:8, :], in_=pbf(0).rearrange("p (k t) -> p k t", k=8)),
                     r=[pbB[0]], w=[B("yTa")])

            def A5():
                for k in range(8):
                    S.op("tensor", (lambda k: lambda e: e.matmul(out=pb[0][0:16, 0:128], lhsT=win[:, k, 6144:6160],
                                                                 rhs=hTc[:, k, 3:131], start=(k == 0), stop=(k == 7)))(k),
                         r=[hB, winB[k]], w=[pbB[0]])
                S.op("vector", lambda e: e.tensor_scalar(out=dtxc[0:16, :], in0=pb[0][0:16, 0:128], scalar1=dtb[0:16, :], scalar2=None,
                                                         op0=ALU.add),
                     r=[pbB[0], B("dtb")], w=[dtxB])
                proj(2, 3072)
                proj(3, 3584)
                for i, bank in enumerate((2, 3)):
                    S.op("scalar", (lambda i, bank: lambda e: e.activation(out=szc[:, i * 512:(i + 1) * 512], in_=pb[bank][:, :],
                                                                           func=AF.Silu))(i, bank),
                         r=[pbB[bank]], w=[szB])
            def B1(j0, j1):
                for j in range(j0, j1):
                    bank = 2 + (j % 2)
                    a_t, aB = acc[j % 2], B("acc%d" % (j % 2))
                    c0 = 4096 + j * 128
                    for k in range(8):
                        S.op("tensor", (lambda k, bank, c0: lambda e: e.matmul(out=pb[bank][:, 0:131], lhsT=win[:, k, c0:c0 + 128],
                                                                             rhs=hTc[:, k, 0:131], start=(k == 0), stop=(k == 7)))(k, bank, c0),
                             r=[hB, winB[k]], w=[pbB[bank]])
                    rb = j % 2
                    S.op("scalar", (lambda j, bank, a_t: lambda e: e.activation(out=a_t[:, :], in_=pb[bank][:, 3:131], func=AF.Identity,
                                                                              scale=prm[:, 48 + j:49 + j], bias=prm[:, 64 + j:65 + j]))(j, bank, a_t),
                         r=[pbB[bank], B("prm")], w=[aB])
                    S.op("scalar", (lambda bank, rb: lambda e: e.copy(out=pb[rb][:, 256:387], in_=pb[bank][:, 0:131]))(bank, rb),
                         r=[pbB[bank]], w=[pbB[rb]])
                    for kk in (2, 1, 0):
                        S.op("vector", (lambda j, rb, a_t, kk: lambda e: e.scalar_tensor_tensor(
                            out=a_t[:, :], in0=pb[rb][:, 256 + kk:256 + kk + 128], scalar=prm[:, 16 * kk + j:16 * kk + j + 1], in1=a_t[:, :],
                            op0=ALU.mult, op1=ALU.add))(j, rb, a_t, kk),
                             r=[pbB[rb], B("prm"), aB], w=[aB])
                    S.op("scalar", (lambda j, a_t: lambda e: e.activation(out=xbcT[:, j, :], in_=a_t[:, :], func=AF.Silu))(j, a_t),
                         r=[aB], w=[XB])

            return dict(A1=A1, A2=A2, A3=A3, A4a=A4, A5=A5, B1=B1, A4b=A4b)

        def stageB(c):
            par = c % 2
            hTc, hB = hT[par], B("hT%d" % par)
            szc, szB = sz[par], B("sz%d" % par)
            dtxc, dtxB = dtx[par], B("dtx%d" % par)
            dramB = B("dram%d" % c)
            r0 = c * T
            XR = dtxc[0:16, :]
            DTE = XR
            M1 = pbB[7]
            xbc_f = xbc_fs[par]
            xbcT = xbc_f[:, :].bitcast(BF16).rearrange("p (j t) -> p j t", j=16)
            XB = B("xbc_f%d" % par)
            t1 = E_f
            t2 = xbc_f

            def B2():
                S.op("vector", lambda e: e.scalar_tensor_tensor(out=f(AB), in0=XR, scalar=-1.0, in1=XR, op0=ALU.mult, op1=ALU.max),
                     r=[dtxB], w=[B("fAB")])
                S.op("scalar", lambda e: e.activation(out=f(E1), in_=f(AB), func=AF.Exp, scale=-1.0), r=[B("fAB")], w=[B("fE1")])
                S.op("scalar", lambda e: e.activation(out=f(L1), in_=f(E1), func=AF.Ln, bias=1.0), r=[B("fE1")], w=[B("fL1")])
                S.op("vector", lambda e: e.scalar_tensor_tensor(out=f(DT), in0=XR, scalar=0.0, in1=f(L1), op0=ALU.max, op1=ALU.add),
                     r=[dtxB, B("fL1")], w=[B("fDT")])
                S.op("vector", lambda e: e.tensor_scalar(out=f(DTA), in0=f(DT), scalar1=aneg[0:16, :], scalar2=None, op0=ALU.mult),
                     r=[B("fDT"), B("aneg")], w=[B("fAB")])
                S.op("vector", lambda e: e.memset(f(E1), 1.0), r=[B("fE1")], w=[B("fE1")])
                S.op("vector", lambda e: e.tensor_tensor_scan(out=f(DACS), data0=f(E1), data1=f(DTA), initial=0.0,
                                                              op0=ALU.mult, op1=ALU.add),
                     r=[B("fE1"), B("fAB")], w=[B("fDACS")])
                S.op("scalar", lambda e: e.activation(out=DTE, in_=f(DACS), func=AF.Exp, scale=-1.0, bias=fm[0:16, DACS, 127:128]),
                     r=[B("fDACS")], w=[dtxB])
                S.op("vector", lambda e: e.tensor_copy(out=dacs_hi[0:16, :], in_=f(DACS)), r=[B("fDACS")], w=[B("dacs_hi")])
                S.op("vector", lambda e: e.tensor_tensor(out=dacs_lo[0:16, :], in0=f(DACS), in1=dacs_hi[0:16, :], op=ALU.subtract),
                     r=[B("fDACS"), B("dacs_hi")], w=[B("dacs_lo")])
                S.op("scalar", lambda e: e.activation(out=fm[0:16, L1, 0:1], in_=fm[0:16, DACS, 127:128], func=AF.Exp),
                     r=[B("fDACS"), B("fL1")], w=[B("fL1")])
                S.op("vector", lambda e: e.tensor_scalar(out=diagcd[0:16, :], in0=ident_f[0:16, 0:16], scalar1=fm[0:16, L1, 0:1],
                                                         scalar2=None, op0=ALU.mult),
                     r=[B("fL1"), B("ident_f")], w=[B("diagcd")])
                srcs = (f(DT), f(DACS), DTE)
                for i in range(3):
                    S.op("tensor", (lambda i: lambda e: e.transpose(out=pb[7][:, 128 + 16 * i:128 + 16 * (i + 1)], in_=srcs[i],
                                                                   identity=ident_f[0:16, 0:16]))(i),
                         r=[B("fDT"), B("fDACS"), dtxB, B("ident_f")], w=[M1])
                S.op("tensor", lambda e: e.matmul(out=pb[7][:, 176:192], lhsT=AP(one16, 0, [[1, 128], [0, 128]]), rhs=diagcd[:, :], start=True, stop=True),
                     r=[B("one16"), B("diagcd")], w=[M1])
                S.op("vector", lambda e: e.tensor_copy(out=tm48[:, :], in_=pb[7][:, 128:176]), r=[M1], w=[B("tm48")])
                S.op("vector", lambda e: e.tensor_copy(out=tms[:, 3, :], in_=pb[7][:, 176:192]), r=[M1], w=[B("cd_bc")])
                S.op("vector", lambda e: e.tensor_scalar(out=tms[:, 0, :], in0=tm48[:, 16:32], scalar1=-1.0, scalar2=None, op0=ALU.mult),
                     r=[B("tm48")], w=[B("negdacs")])
                S.op("scalar", lambda e: e.activation(out=tms[:, 1, :], in_=tm48[:, 16:32], func=AF.Exp), r=[B("tm48")], w=[B("edacs")])
                S.op("vector", lambda e: e.tensor_tensor(out=tms[:, 2, :], in0=tm48[:, 0:16], in1=tm48[:, 32:48], op=ALU.mult),
                     r=[B("tm48")], w=[B("wdte")])

            def B3():
                for j in range(8):
                    S.op("tensor", (lambda j: lambda e: e.transpose(out=pbf(4)[:, j * 128:(j + 1) * 128], in_=xbcT[:, j, :],
                                                                   identity=ident_bf[:, :]))(j),
                         r=[XB, B("ident_bf")], w=[pbB[4]])
                S.op("vector", lambda e: e.tensor_copy(out=xs_tm[:, :], in_=pbf(4)), r=[pbB[4]], w=[B("xs_tm")])
                for g in range(4):
                    S.op("tensor", (lambda g: lambda e: e.transpose(out=pbf(7)[:, 512 + g * 128:512 + (g + 1) * 128], in_=xbcT[:, 8 + g, :],
                                                                   identity=ident_bf[:, :]))(g),
                         r=[XB, B("ident_bf")], w=[M1])
                S.op("scalar", lambda e: e.copy(out=B_tm[:, :], in_=pbf(7)[:, 512:1024]), r=[M1], w=[B("B_tm")])
                S.op("gpsimd", lambda e: e.tensor_tensor(out=xdt_bf[:, :].rearrange("p (h d) -> p h d", h=16),
                                                         in0=xs_tm[:, :].rearrange("p (h d) -> p h d", h=16),
                                                         in1=AP(tm48, 0, [[48, 128], [1, 16], [0, 64]]), op=ALU.mult),
                     r=[B("xs_tm"), B("tm48")], w=[B("xdt_bf")])
                S.op("gpsimd", lambda e: e.tensor_tensor(out=xdte_bf[:, :].rearrange("p (h d) -> p h d", h=16),
                                                         in0=xs_tm[:, :].rearrange("p (h d) -> p h d", h=16),
                                                         in1=AP(tms, 32, [[64, 128], [1, 16], [0, 64]]), op=ALU.mult),
                     r=[B("xs_tm"), B("wdte")], w=[B("xdte_bf")])

            def B4():
                for g in range(4):
                    S.op("tensor", (lambda g: lambda e: e.matmul(out=pb[6][:, g * 128:(g + 1) * 128], lhsT=xbcT[:, 8 + g, :],
                                                                 rhs=xbcT[:, 12 + g, :], start=True, stop=True))(g),
                         r=[XB], w=[pbB[6]])
                S.op("scalar", lambda e: e.copy(out=cbT_bf[:, :], in_=pb[6][:, :]), r=[pbB[6]], w=[B("cbT_bf")])
                for q in range(4):
                    bk = 4 + q % 2
                    S.op("tensor", (lambda bk: lambda e: e.matmul(out=pb[bk][:, :], lhsT=ident_bf[:, :], rhs=AP(negm, 0, [[128, 128], [0, 4], [1, 128]]),
                                                                  start=True, stop=False))(bk),
                         r=[B("ident_bf"), B("negm")], w=[pbB[bk]])
                    for r_ in range(4):
                        h = 4 * q + r_
                        for nm, src in (("dacs_hi", dacs_hi), ("dacs_lo", dacs_lo)):
                            S.op("tensor", (lambda bk, r_, h, src, nm: lambda e: e.matmul(
                                out=pb[bk][:, r_ * 128:(r_ + 1) * 128], lhsT=AP(ident_bf, h, [[128, 128], [0, 128]]), rhs=src[:, :],
                                start=False, stop=(nm == "dacs_lo" and r_ == 3)))(bk, r_, h, src, nm),
                                 r=[B("ident_bf"), B(nm)], w=[pbB[bk]])
                    for r_ in range(4):
                        h = 4 * q + r_
                        S.op("scalar", (lambda bk, r_, h: lambda e: e.activation(out=Ebf[:, h, :], in_=pb[bk][:, r_ * 128:(r_ + 1) * 128],
                                                                                func=AF.Exp, bias=tms[:, 0, h:h + 1]))(bk, r_, h),
                             r=[pbB[bk], B("negdacs")], w=[EB])
                S.op("vector", lambda e: e.tensor_tensor(out=Ebf.rearrange("p (g r) l -> p g r l", g=4),
                                                         in0=Ebf.rearrange("p (g r) l -> p g r l", g=4),
                                                         in1=AP(cbT_bf, 0, [[512, 128], [128, 4], [0, 4], [1, 128]]), op=ALU.mult),
                     r=[EB, B("cbT_bf")], w=[EB])

            def B5():
                for h in range(16):
                    bk = 4 + h // 8
                    S.op("tensor", (lambda h, bk: lambda e: e.matmul(out=pb[bk][:, (h % 8) * 64:(h % 8 + 1) * 64], lhsT=Ebf[:, h, :],
                                                                     rhs=xdt_bf[:, h * 64:(h + 1) * 64], start=True, stop=True))(h, bk),
                         r=[EB, B("xdt_bf")], w=[pbB[bk]])
                for g in range(4):
                    bk = 6 + g // 2
                    S.op("tensor", (lambda g, bk: lambda e: e.matmul(out=pb[bk][:, (g % 2) * 256:(g % 2 + 1) * 256], lhsT=xbcT[:, 12 + g, :],
                                                                     rhs=St_bf[:, g * 256:(g + 1) * 256], start=True, stop=True))(g, bk),
                         r=[XB, B("St_bf")], w=[pbB[bk]])
                for q in range(2):
                    S.op("vector", (lambda q: lambda e: e.tensor_tensor(out=t1[:, q * 512:(q + 1) * 512].rearrange("p (h d) -> p h d", h=8),
                                                                        in0=pb[6 + q][:, :].rearrange("p (h d) -> p h d", h=8),
                                                                        in1=AP(tms, 16 + 8 * q, [[64, 128], [1, 8], [0, 64]]), op=ALU.mult))(q),
                         r=[pbB[6 + q], B("edacs")], w=[EB])
                    S.op("vector", (lambda q: lambda e: e.tensor_tensor(out=t1[:, q * 512:(q + 1) * 512], in0=t1[:, q * 512:(q + 1) * 512],
                                                                        in1=pb[4 + q][:, :], op=ALU.add))(q),
                         r=[pbB[4 + q], EB], w=[EB])
                for g in range(4):
                    bk = 6 + g // 2
                    S.op("tensor", (lambda g, bk: lambda e: e.matmul(out=pb[bk][:, (g % 2) * 256:(g % 2 + 1) * 256],
                                                                     lhsT=B_tm[:, g * 128:(g + 1) * 128],
                                                                     rhs=xdte_bf[:, g * 256:(g + 1) * 256], start=True, stop=True))(g, bk),
                         r=[B("B_tm"), B("xdte_bf")], w=[pbB[bk]])
                S.op("gpsimd", lambda e: e.tensor_tensor(out=t2[:, :].rearrange("p (h d) -> p h d", h=16),
                                                         in0=xs_tm[:, :].rearrange("p (h d) -> p h d", h=16),
                                                         in1=AP(d_bc, 0, [[16, 128], [1, 16], [0, 64]]), op=ALU.mult),
                     r=[B("xs_tm"), B("d_bc"), XB], w=[XB])
                S.op("gpsimd", lambda e: e.tensor_tensor(out=t1[:, :], in0=t1[:, :], in1=t2[:, :], op=ALU.add), r=[EB, XB], w=[EB])
                S.op("vector", lambda e: e.tensor_tensor(out=t1[:, :], in0=t1[:, :], in1=szc[:, :], op=ALU.mult), r=[EB, szB], w=[EB])
                S.op("vector", lambda e: e.tensor_tensor(out=St[:, :].rearrange("p (h d) -> p h d", h=16),
                                                         in0=St[:, :].rearrange("p (h d) -> p h d", h=16),
                                                         in1=AP(tms, 48, [[64, 128], [1, 16], [0, 64]]), op=ALU.mult),
                     r=[B("St"), B("cd_bc")], w=[B("St")])
                for q in range(2):
                    S.op("vector", (lambda q: lambda e: e.tensor_tensor(out=St[:, q * 512:(q + 1) * 512], in0=St[:, q * 512:(q + 1) * 512],
                                                                        in1=pb[6 + q][:, :], op=ALU.add))(q),
                         r=[B("St"), pbB[6 + q]], w=[B("St")])
                S.op("scalar", lambda e: e.copy(out=St_bf[:, :], in_=St[:, :]), r=[B("St")], w=[B("St_bf")])

            def B6():
                for g in range(4):
                    S.op("scalar", (lambda g: lambda e: e.activation(out=xdt_bf[:, g * 256:(g + 1) * 256], in_=t1[:, g * 256:(g + 1) * 256],
                                                                     func=AF.Square, accum_out=stc(SSQB + g)))(g),
                         r=[EB], w=[B("xdt_bf"), B("ssqb")])
                S.op("scalar", lambda e: e.copy(out=stc(SCR0), in_=stc(SCR1)), r=[B("st_scr")], w=[B("ssqb")])
                rsqrt(SSQB, RSTDB, 4, 1.0 / 256, B("ssqb"), B("rstdb"), MSB)
                S.op("vector", lambda e: e.tensor_tensor(out=xs_tm[:, :].rearrange("p (g d) -> p g d", g=4),
                                                         in0=t1[:, :].rearrange("p (g d) -> p g d", g=4),
                                                         in1=AP(st, RSTDB, [[64, 128], [1, 4], [0, 256]]), op=ALU.mult),
                     r=[EB, B("rstdb")], w=[B("xs_tm")])
                for h in range(8):
                    S.op("tensor", (lambda h: lambda e: e.transpose(out=pbf(5)[:, h * 128:(h + 1) * 128],
                                                                   in_=xs_tm[:, h * 128:(h + 1) * 128], identity=ident_bf[:, :]))(h),
                         r=[B("xs_tm"), B("ident_bf")], w=[pbB[5]])
                S.op("scalar", lambda e: e.copy(out=yT[:, 8:16, :], in_=pbf(5).rearrange("p (k t) -> p k t", k=8)),
                     r=[pbB[5]], w=[B("yTb")])

            def B7():
                S.dma("sync", lambda e: e.dma_start(out=t2[:, :], in_=src_d[r0:r0 + T, :]), "xr", r=[dramB], w=[XB])
                for b in range(2):
                    for k in range(16):
                        S.op("tensor", (lambda b, k: lambda e: e.matmul(out=pb[4 + b][:, :], lhsT=yT[:, k, :],
                                                                        rhs=wout[:, k, b * 512:(b + 1) * 512], start=(k == 0), stop=(k == 15)))(b, k),
                             r=[B("yTa"), B("yTb"), B("wout")], w=[pbB[4 + b]])
                for b in range(2):
                    S.op("scalar", (lambda b: lambda e: e.activation(out=xdt_bf[:, b * 512:(b + 1) * 512], in_=pb[4 + b][:, :],
                                                                     func=AF.Square, accum_out=stc(SSQM + b)))(b),
                         r=[pbB[4 + b]], w=[B("xdt_bf"), B("ssqm")])
                S.op("scalar", lambda e: e.copy(out=stc(SCR0), in_=stc(SCR1)), r=[B("st_scr")], w=[B("ssqm")])
                S.op("vector", lambda e: e.tensor_tensor(out=stc(SSQM), in0=stc(SSQM), in1=stc(SSQM + 1), op=ALU.add),
                     r=[B("ssqm")], w=[B("ssqm")])
                rsqrt(SSQM, RSTDM, 1, 1.0 / D, B("ssqm"), B("rstdm"), MSM)
                for b in range(2):
                    S.op("vector", (lambda b: lambda e: e.scalar_tensor_tensor(out=t1[:, b * 512:(b + 1) * 512], in0=pb[4 + b][:, :],
                                                                               scalar=stc(RSTDM), in1=post_bc[:, b * 512:(b + 1) * 512],
                                                                               op0=ALU.mult, op1=ALU.mult))(b),
                         r=[pbB[4 + b], B("rstdm"), B("post_bc")], w=[EB])
                S.op("gpsimd", lambda e: e.tensor_tensor(out=t1[:, :], in0=t1[:, :], in1=t2[:, :], op=ALU.add), r=[EB, XB], w=[EB])
                S.dma("sync", lambda e: e.dma_start(out=dst_d[r0:r0 + T, :], in_=t1[:, :]), "xs", r=[EB], w=[dramB])
            return [B2, B3, B4, B5, B6, B7]

        def Aprime(As):
            b1 = As["B1"]
            return S.record([As["A1"], lambda: b1(0, 4), As["A2"], lambda: b1(4, 8), As["A3"], lambda: b1(8, 12), As["A5"],
                             lambda: b1(12, 16), As["A4a"]])

        As = stageA(0)
        for o in Aprime(As):
            S.play(o)
        for c in range(NCH):
            Bs = stageB(c)
            X = S.record(Bs[:5])
            xbar = len(X)
            X += S.record(Bs[5:])
            Y = S.record([As["A4b"]])
            ybar = len(Y)
            An = stageA(c + 1) if c + 1 < NCH else None
            if An is not None:
                Y += Aprime(An)
            elif l + 1 < NL:
                load_win(l + 1)
            S.merge(X, Y, ybar, xbar)
            As = An

    for l_ in range(NL):
        do_layer(l_, x_d if l_ == 0 else xbuf_d, out_d if l_ == NL - 1 else xbuf_d)
    global LAST_S
    LAST_S = S
    S.emit(nc, es)
    es.close()
    return nc


_CONSTS = None


def _consts():
    global _CONSTS
    if _CONSTS is None:
        ident = np.eye(T, dtype=np.float32)
        tril = np.tril(np.ones((T, T), np.float32))
        sidx = np.arange(T)[:, None]
        lidx = np.arange(T)[None, :]
        negm1 = np.where(sidx <= lidx, 0.0, NEG).astype(np.float32)
        negm = np.tile(negm1, (1, 4))
        sel = np.zeros((T, 16, T), np.float32)
        for h in range(16):
            sel[h, h, :] = 1.0
        one16 = np.zeros((T, T), np.float32)
        one16[0:16, :] = 1.0
        _CONSTS = dict(c_ident=ident, c_tril=tril, c_negm=negm, c_sel=sel.reshape(T, 16 * T), c_one16=one16)
    return _CONSTS


_PROG = {}


def _get_prog(NL, NCH):
    key = (NL, NCH)
    if key not in _PROG:
        _PROG[key] = build_program(NL, NCH)
    return _PROG[key]


PARAMS = ["pre_norm_w", "w_in", "gmlp_v_norm_w", "gmlp_v_norm_b", "gmlp_ws", "gmlp_bs", "gmlp_norm_w",
          "conv_w", "conv_b", "dt_bias", "a_log", "d_skip", "ssd_norm_w", "w_out", "post_norm_w"]

FUSED = True


def kernel(**inputs):
    x = np.ascontiguousarray(np.asarray(inputs["x"], dtype=np.float32))
    Bn, Ls, _ = x.shape
    NCH = Ls // T
    depth = inputs["w_in"].shape[0]
    par = {k: np.ascontiguousarray(np.asarray(inputs[k], dtype=np.float32)) for k in PARAMS}
    cs = _consts()
    if FUSED:
        nc = _get_prog(depth, NCH)
        in_maps = [dict(x=x[b], **par, **cs) for b in range(Bn)]
        res = run_bass_kernel_spmd(nc, in_maps, core_ids=list(range(Bn)))
        return np.stack([np.asarray(r["out"]) for r in res.results], axis=0).astype(np.float32)
    cur = [x[b] for b in range(Bn)]
    nc = _get_prog(1, NCH)
    for l in range(depth):
        pl = {k: np.ascontiguousarray(v[l:l + 1]) for k, v in par.items()}
        in_maps = [dict(x=cur[b], **pl, **cs) for b in range(Bn)]
        res = run_bass_kernel_spmd(nc, in_maps, core_ids=list(range(Bn)))
        cur = [np.ascontiguousarray(np.asarray(r["out"], dtype=np.float32)) for r in res.results]
    return np.stack(cur, axis=0).astype(np.float32)
```

```python
import numpy as np
from contextlib import ExitStack
import concourse.bass as bass
import concourse.mybir as mybir
from concourse.bass_utils import run_bass_kernel_spmd

F32 = mybir.dt.float32
BF16 = mybir.dt.bfloat16
AF = mybir.ActivationFunctionType
ALU = mybir.AluOpType
AX = mybir.AxisListType

D = 1024
T = 128
INC = 6160
EPS = 1e-6
NEG = -60000.0
import os
DBG = int(os.environ.get("KDBG", "99"))
ENGS = ("tensor", "vector", "scalar", "gpsimd", "sync")


class _Fake:
    def __init__(self):
        self.name = None
        self.kw = {}

    def __getattr__(self, name):
        def f(*a, **kw):
            self.name = name
            self.kw = kw
            return None
        return f


def _fsize(ap):
    n = 1
    for d in list(ap.shape)[1:]:
        n *= int(d)
    return n


def op_cost(eng, fn, kind):
    if kind == "dma":
        return 2.5
    fk = _Fake()
    try:
        fn(fk)
    except Exception:
        return 0.5
    kw = fk.kw
    n = 128
    for key in ("in_", "in0", "rhs", "out", "data1"):
        if key in kw and hasattr(kw[key], "shape"):
            n = _fsize(kw[key])
            break
    if eng == "tensor":
        if fk.name == "transpose":
            return 0.12
        return 0.07 + n / 2400.0 + 0.05
    if eng == "vector":
        return 0.06 + (150 + n) / 960.0
    if eng == "scalar":
        return 0.17 + n / 1200.0 + (0.15 if "accum_out" in kw else 0.0)
    if eng == "gpsimd":
        return 0.8 + n / 800.0
    return 0.1


class Buf:
    __slots__ = ("name", "writer", "readers", "excl", "wreal")

    def __init__(self, name, excl=False):
        self.name = name
        self.writer = None
        self.readers = []
        self.wreal = True
        self.excl = excl


class Item:
    __slots__ = ("fn", "waits", "kind", "idx", "dkey")

    def __init__(self, fn, waits, kind, idx, dkey=None):
        self.fn, self.waits, self.kind, self.idx, self.dkey = fn, waits, kind, idx, dkey


class Sched:
    def __init__(self):
        self.items = {e: [] for e in ENGS}
        self.cnt = {e: 0 for e in ENGS}
        self.seen = {e: {} for e in ENGS}
        self.dma_cnt = {}
        self.signal = {e: set() for e in ENGS}
        self.rec = None
        self.efree = {e: 0.0 for e in ENGS}
        self.done = {}

    def peek(self, reads, writes):
        deps = []
        for b in reads:
            if b.writer is not None:
                deps.append(b.writer)
            if b.excl:
                deps.extend(b.readers)
        for b in writes:
            if b.writer is not None:
                deps.append(b.writer)
            deps.extend(b.readers)
        return deps

    def est_start(self, eng, reads, writes):
        t = self.efree[eng]
        for tok in self.peek(reads, writes):
            d = self.done.get(tok, 0.0)
            if not (tok[0] == "e" and tok[1] == eng):
                d += 0.15
            t = max(t, d)
        return t

    def record(self, fns):
        self.rec = []
        for fn in fns:
            fn()
        ops, self.rec = self.rec, None
        return ops

    def play(self, o):
        kind, eng, fn, key, r, w = o
        st = self.est_start(eng, r, w)
        c = op_cost(eng, fn, kind)
        if kind == "op":
            tok = self.op(eng, fn, r, w)
            self.efree[eng] = st + c
            self.done[tok] = st + c
        else:
            tok = self.dma(eng, fn, key, r, w)
            self.efree[eng] = st + 0.1
            self.done[tok] = st + c
        return tok

    def merge(self, X, Y, ybar, xbar):
        i = j = 0
        while i < len(X) or j < len(Y):
            if i >= len(X):
                self.play(Y[j]); j += 1
                continue
            if j >= len(Y):
                self.play(X[i]); i += 1
                continue
            if i >= xbar and j < ybar:
                self.play(Y[j]); j += 1
                continue
            sx = self.est_start(X[i][1], X[i][4], X[i][5])
            sy = self.est_start(Y[j][1], Y[j][4], Y[j][5])
            if sx <= sy:
                self.play(X[i]); i += 1
            else:
                self.play(Y[j]); j += 1

    def _collect(self, eng, reads, writes, is_dma):
        deps = []
        for b in reads:
            if b.writer is not None:
                deps.append(b.writer)
            if b.excl:
                deps.extend(b.readers)
        for b in writes:
            if b.writer is not None:
                deps.append(b.writer)
            deps.extend(b.readers)
        waits = []
        for tok in deps:
            if tok[0] == "e":
                _, src, idx = tok
                if src == eng and not is_dma:
                    if eng == "tensor" or not any((b.writer == tok and b.wreal) for b in reads):
                        continue
                if self.seen[eng].get(src, -1) >= idx:
                    continue
                self.seen[eng][src] = idx
                self.signal[src].add(idx)
                waits.append(tok)
            else:
                _, key, val = tok
                if self.seen[eng].get(("d", key), 0) >= val:
                    continue
                self.seen[eng][("d", key)] = val
                waits.append(tok)
        best = {}
        for tok in waits:
            k = tok[1]
            if k not in best or tok[2] > best[k][2]:
                best[k] = tok
        return list(best.values())

    def op(self, eng, fn, r=(), w=()):
        if self.rec is not None:
            self.rec.append(("op", eng, fn, None, tuple(r), tuple(w)))
            return None
        waits = self._collect(eng, r, w, False)
        idx = self.cnt[eng]
        self.cnt[eng] += 1
        tok = ("e", eng, idx)
        self.items[eng].append(Item(fn, waits, "c", idx))
        for b in r:
            if b.excl:
                b.writer = tok
                b.wreal = False
                b.readers = []
            else:
                b.readers.append(tok)
        for b in w:
            b.writer = tok
            b.wreal = True
            b.readers = []
        return tok

    def dma(self, eng, fn, key, r=(), w=()):
        if self.rec is not None:
            self.rec.append(("dma", eng, fn, key, tuple(r), tuple(w)))
            return None
        waits = self._collect(eng, r, w, True)
        n = self.dma_cnt.get(key, 0) + 1
        self.dma_cnt[key] = n
        tok = ("d", key, 16 * n)
        self.items[eng].append(Item(fn, waits, "d", None, key))
        for b in r:
            b.readers.append(tok)
        for b in w:
            b.writer = tok
            b.readers = []
        return tok

    def emit(self, nc, es):
        esem = {e: es.enter_context(nc.semaphore("es_" + e)) for e in ENGS if e != "sync"}
        dsem = {k: es.enter_context(nc.semaphore("ds_" + str(k))) for k in self.dma_cnt}
        rank = {}
        for e in ENGS:
            for i, idx in enumerate(sorted(self.signal[e])):
                rank[(e, idx)] = i + 1
        final_dma = [(dsem[k], 16 * n) for k, n in self.dma_cnt.items()]

        def run(ename, e):
            for it in self.items[ename]:
                for tok in it.waits:
                    if tok[0] == "e":
                        e.wait_ge(esem[tok[1]], rank[(tok[1], tok[2])])
                    else:
                        e.wait_ge(dsem[tok[1]], tok[2])
                ins = it.fn(e)
                if it.kind == "c":
                    if it.idx in self.signal[ename]:
                        ins.then_inc(esem[ename], 1)
                else:
                    ins.then_inc(dsem[it.dkey], 16)
            if ename == "sync":
                for s, v in final_dma:
                    e.wait_ge(s, v)

        with nc.Block() as block:
            @block.tensor
            def _(e):
                run("tensor", e)

            @block.vector
            def _(e):
                run("vector", e)

            @block.scalar
            def _(e):
                run("scalar", e)

            @block.gpsimd
            def _(e):
                run("gpsimd", e)

            @block.sync
            def _(e):
                run("sync", e)


def AP(t, off, dims):
    return bass.AP(t, off, [list(d) for d in dims])


def build_program(NL, NCH):
    nc = bass.Bass("TRN2", target_bir_lowering=False)
    L = NCH * T

    def din(name, shape):
        return nc.dram_tensor(name, shape, F32, kind="ExternalInput").ap()

    x_d = din("x", [L, D])
    pre_d = din("pre_norm_w", [NL, D])
    win_d = din("w_in", [NL, D, INC])
    vw_d = din("gmlp_v_norm_w", [NL, D])
    vb_d = din("gmlp_v_norm_b", [NL, D])
    ws_d = din("gmlp_ws", [NL, 8, T, T])
    bs_d = din("gmlp_bs", [NL, 8, T])
    gnw_d = din("gmlp_norm_w", [NL, D])
    cw_d = din("conv_w", [NL, 4, 2048])
    cb_d = din("conv_b", [NL, 2048])
    dtb_d = din("dt_bias", [NL, 16])
    alog_d = din("a_log", [NL, 16])
    dsk_d = din("d_skip", [NL, 16])
    snw_d = din("ssd_norm_w", [NL, D])
    wout_d = din("w_out", [NL, 2 * D, D])
    post_d = din("post_norm_w", [NL, D])
    c_ident = din("c_ident", [T, T])
    c_tril = din("c_tril", [T, T])
    c_negm = din("c_negm", [T, 512])
    c_sel = din("c_sel", [T, 16 * T])
    c_one16 = din("c_one16", [T, T])
    out_d = nc.dram_tensor("out", [L, D], F32, kind="ExternalOutput").ap()
    xbuf_d = nc.dram_tensor("xbuf", [L, D], F32, kind="Internal").ap() if NL > 1 else None

    es = ExitStack()
    S = Sched()
    bufs = {}

    def B(name):
        if name not in bufs:
            bufs[name] = Buf(name)
        return bufs[name]

    def sb(name, shape, dt):
        return es.enter_context(nc.sbuf_tensor(name, shape, dt))

    win = sb("win", [128, 8, INC], BF16)
    wout = sb("wout", [128, 16, D], BF16)
    ident_bf = sb("ident_bf", [128, 128], BF16)
    ident_f = sb("ident_f", [128, 16], F32)
    negm = sb("negm", [128, 128], BF16)
    one16 = sb("one16", [128, 1], F32)
    mhalf = sb("mhalf", [128, 8], F32)
    post_bc = sb("post_bc", [128, D], F32)
    vw_bc = sb("vw_bc", [128, D], F32)
    bias_t = sb("bias_t", [128, D], F32)
    wsT = sb("wsT", [128, 8, T], BF16)
    prm = sb("prm", [128, 112], F32)
    dtb = sb("dtb", [128, 1], F32)
    alog = sb("alog", [128, 1], F32)
    aneg = sb("aneg", [128, 1], F32)
    d_bc = sb("d_bc", [128, 16], F32)
    rs = sb("rs", [128, 8], F32)

    x_sb = sb("x_sb", [128, D], F32)
    xbf = sb("xbf", [128, D], BF16)
    hT = [sb("hT%d" % i, [128, 8, 131], BF16) for i in range(2)]
    halo = sb("halo", [128, 8, 3], BF16)
    g_sb = sb("g_sb", [128, D], F32)
    sz = [sb("sz%d" % i, [128, D], F32) for i in range(2)]
    dtx = [sb("dtx%d" % i, [128, T], F32) for i in range(2)]
    y_sb = sb("y_sb", [128, D], F32)
    yn_bf = sb("yn_bf", [128, D], BF16)
    vn_bf = yn_bf
    yT = sb("yT", [128, 16, T], BF16)
    acc = [sb("acc%d" % i, [128, T], F32) for i in range(2)]
    xbc_fs = [sb("xbc_f%d" % i, [128, D], F32) for i in range(2)]
    xs_tm = sb("xs_tm", [128, D], BF16)
    B_tm = sb("B_tm", [128, 512], BF16)
    fm = sb("fm", [128, 5, T], F32)
    dacs_hi = sb("dacs_hi", [128, T], BF16)
    dacs_lo = sb("dacs_lo", [128, T], BF16)
    tm48 = sb("tm48", [128, 48], F32)
    tms = sb("tms", [128, 4, 16], F32)
    diagcd = sb("diagcd", [128, 16], F32)
    cbT_bf = sb("cbT_bf", [128, 512], BF16)
    E_f = sb("E_f", [128, D], F32)
    xdt_bf = sb("xdt_bf", [128, D], BF16)
    xdte_bf = sb("xdte_bf", [128, D], BF16)
    St = sb("St", [128, D], F32)
    St_bf = sb("St_bf", [128, D], BF16)
    st = sb("st", [128, 64], F32)
    bnst = sb("bnst", [128, 2, 6], F32)

    pb = [es.enter_context(nc.psum_tensor("pb%d" % i, [128, 512], F32)) for i in range(8)]
    pbB = [B("pb%d" % i) for i in range(8)]
    for b_ in pbB:
        b_.excl = True

    def pbf(i):
        return pb[i][:, :].bitcast(BF16)

    SSQ, MS, RSTD, RSTDV, NMR, MSV, RSTDM, MSM, SCR0, SCR1 = range(10)
    SSQG, MSG, RSTDG = 16, 24, 32
    SSQB, MSB, RSTDB = 40, 44, 48
    SSQM = 52
    MV = 56

    def stc(c, n=1):
        return st[:, c:c + n]

    S.dma("sync", lambda e: e.dma_start(out=ident_f[:, :], in_=c_ident[:, 0:16]), "c0", w=[B("ident_f")])
    S.op("vector", lambda e: e.memset(one16[:, :], 1.0), w=[B("one16")])
    S.dma("gpsimd", lambda e: e.dma_start(out=ident_bf[:, :], in_=c_ident), "c3", w=[B("ident_bf")])
    S.dma("gpsimd", lambda e: e.dma_start(out=negm[:, :], in_=c_negm[:, 0:128]), "c4", w=[B("negm")])
    S.op("vector", lambda e: e.memset(mhalf[:, :], -0.5), w=[B("mhalf")])
    S.op("vector", lambda e: e.memset(dacs_hi[:, :], 0.0), w=[B("dacs_hi")])
    S.op("vector", lambda e: e.memset(dacs_lo[:, :], 0.0), w=[B("dacs_lo")])
    S.op("vector", lambda e: e.memset(diagcd[:, :], 0.0), w=[B("diagcd")])
    S.op("vector", lambda e: e.memset(st[:, SCR0:SCR1 + 1], 0.0), w=[B("st_scr")])

    def rsqrt(src_c, dst_c, n, scale, rbuf, wbuf, tmp_c):
        S.op("vector", lambda e: e.tensor_scalar(out=stc(tmp_c, n), in0=stc(src_c, n), scalar1=scale,
                                                 scalar2=EPS, op0=ALU.mult, op1=ALU.add),
             r=[rbuf], w=[B("tmp%d" % tmp_c)])
        S.op("gpsimd", lambda e: e.tensor_tensor(out=stc(dst_c, n), in0=stc(tmp_c, n), in1=mhalf[:, 0:n],
                                                 op=ALU.pow),
             r=[B("tmp%d" % tmp_c), B("mhalf")], w=[wbuf])

    def do_layer(l, src_d, dst_d):
        winB = [B("win_k%d" % k) for k in range(8)]

        def load_win(ll):
            for k in range(8):
                S.dma("gpsimd", (lambda k: lambda e: e.dma_start(out=win[:, k, :], in_=win_d[ll, k * 128:(k + 1) * 128, :]))(k),
                      "win%d" % k, w=[winB[k]])
        if l == 0:
            load_win(0)
        if True:
            stgp = g_sb[0:16, 0:896].rearrange("p (g t) -> p g t", g=7)
            for kk in range(4):
                S.dma("sync", (lambda kk: lambda e: e.dma_start(out=stgp[:, kk, :], in_=cw_d[l, kk].rearrange("(j p) -> j p", p=128)))(kk),
                      "q%d" % kk, w=[B("g_sb")])
            S.dma("sync", lambda e: e.dma_start(out=stgp[:, 4, :], in_=cb_d[l].rearrange("(j p) -> j p", p=128)), "q4", w=[B("g_sb")])
            S.dma("sync", lambda e: e.dma_start(out=stgp[0:8, 5, :], in_=pre_d[l].rearrange("(j p) -> j p", p=128)), "q5", w=[B("g_sb")])
            S.dma("sync", lambda e: e.dma_start(out=g_sb[8:16, 640:768], in_=gnw_d[l].rearrange("(j p) -> j p", p=128)), "q6", w=[B("g_sb")])
            S.dma("sync", lambda e: e.dma_start(out=stgp[0:8, 6, :], in_=snw_d[l].rearrange("(j p) -> j p", p=128)), "q7", w=[B("g_sb")])
            S.dma("sync", lambda e: e.dma_start(out=g_sb[8:16, 768:896], in_=bs_d[l]), "q8", w=[B("g_sb")])
            for gi in range(7):
                S.op("tensor", (lambda gi: lambda e: e.transpose(out=pb[6][:, gi * 16:(gi + 1) * 16], in_=g_sb[0:16, gi * 128:(gi + 1) * 128],
                                                                identity=ident_f[0:16, 0:16]))(gi),
                     r=[B("g_sb"), B("ident_f")], w=[pbB[6]])
            S.op("vector", lambda e: e.tensor_copy(out=prm[:, :], in_=pb[6][:, 0:112]), r=[pbB[6]], w=[B("prm")])
            S.dma("sync", lambda e: e.dma_start(out=dtb[0:16, :], in_=dtb_d[l].rearrange("(h o) -> h o", o=1), allow_slow_non_contiguous=True), "p6", w=[B("dtb")])
            S.dma("sync", lambda e: e.dma_start(out=alog[0:16, :], in_=alog_d[l].rearrange("(h o) -> h o", o=1), allow_slow_non_contiguous=True), "p7", w=[B("alog")])
            S.dma("sync", lambda e: e.dma_start(out=d_bc[:, :], in_=dsk_d[l].partition_broadcast(128), allow_slow_non_contiguous=True), "p8", w=[B("d_bc")])
            S.dma("sync", lambda e: e.dma_start(out=post_bc[:, :], in_=post_d[l].partition_broadcast(128), allow_slow_non_contiguous=True), "p9", w=[B("post_bc")])
            S.dma("sync", lambda e: e.dma_start(out=vw_bc[:, :], in_=vw_d[l].partition_broadcast(128), allow_slow_non_contiguous=True), "p10", w=[B("vw_bc")])
            S.dma("sync", lambda e: e.dma_start(out=sz[0][:, :], in_=vb_d[l].partition_broadcast(128), allow_slow_non_contiguous=True), "p11", w=[B("sz0")])
            S.dma("sync", lambda e: e.dma_start(out=y_sb[:, :].rearrange("t (h s) -> t h s", h=8),
                                               in_=ws_d[l].rearrange("h t s -> t h s"), allow_slow_non_contiguous=True), "p12", w=[B("y_sb")])
        S.op("scalar", lambda e: e.activation(out=aneg[0:16, :], in_=alog[0:16, :], func=AF.Exp), r=[B("alog")], w=[B("aneg")])
        S.op("vector", lambda e: e.tensor_scalar(out=aneg[0:16, :], in0=aneg[0:16, :], scalar1=-1.0, scalar2=None, op0=ALU.mult),
             r=[B("aneg")], w=[B("aneg")])
        S.dma("sync", lambda e: e.dma_start(out=St[:, 0:128], in_=c_tril), "c1", w=[B("St")])
        ws3 = y_sb[:, :].rearrange("t (h s) -> t h s", h=8)
        S.op("vector", lambda e: e.tensor_tensor(out=ws3, in0=ws3, in1=AP(St, 0, [[1024, 128], [0, 8], [1, 128]]), op=ALU.mult),
             r=[B("y_sb"), B("St")], w=[B("y_sb")])
        S.op("vector", lambda e: e.tensor_reduce(out=rs[:, :], in_=ws3, axis=AX.X, op=ALU.add), r=[B("y_sb")], w=[B("rs")])
        S.op("vector", lambda e: e.tensor_copy(out=yn_bf[:, :], in_=y_sb[:, :]), r=[B("y_sb")], w=[B("yn_bf")])
        for h in range(8):
            bk = 4 + h // 4
            S.op("tensor", (lambda h, bk: lambda e: e.transpose(out=pbf(bk)[:, (h % 4) * 128:(h % 4 + 1) * 128],
                                                               in_=yn_bf[:, h * 128:(h + 1) * 128], identity=ident_bf[:, :]))(h, bk),
                 r=[B("yn_bf"), B("ident_bf")], w=[pbB[bk]])
        for q in range(2):
            S.op("vector", (lambda q: lambda e: e.tensor_copy(out=wsT[:, 4 * q:4 * q + 4, :],
                                                              in_=pbf(4 + q)[:, 0:512].rearrange("p (h t) -> p h t", h=4)))(q),
                 r=[pbB[4 + q]], w=[B("wsT")])
        for h in range(8):
            S.op("vector", (lambda h: lambda e: e.tensor_scalar(out=bias_t[:, h * 128:(h + 1) * 128], in0=sz[0][:, h * 128:(h + 1) * 128],
                                                                scalar1=rs[:, h:h + 1], scalar2=prm[:, 104 + h:105 + h],
                                                                op0=ALU.mult, op1=ALU.add))(h),
                 r=[B("sz0"), B("rs"), B("prm")], w=[B("bias_t")])
        stg = [(g_sb, B("g_sb")), (y_sb, B("y_sb")), (sz[1], B("sz1")), (xbc_fs[0], B("xbc_f0"))]
        for k in range(16):
            tt, tb = stg[k % 4]
            S.dma("sync", (lambda k, tt: lambda e: e.dma_start(out=tt[:, :], in_=wout_d[l, k * 128:(k + 1) * 128, :]))(k, tt),
                  "wo%d" % (k % 4), w=[tb])
            if k % 2 == 0:
                S.op("vector", (lambda k, tt: lambda e: e.tensor_scalar(out=wout[:, k, :], in0=tt[:, :], scalar1=prm[:, 88 + k:89 + k],
                                                                        scalar2=None, op0=ALU.mult))(k, tt),
                     r=[tb, B("prm")], w=[B("wout")])
            else:
                S.op("scalar", (lambda k, tt: lambda e: e.activation(out=wout[:, k, :], in_=tt[:, :], func=AF.Copy,
                                                                     scale=prm[:, 88 + k:89 + k]))(k, tt),
                     r=[tb, B("prm")], w=[B("wout")])
        S.op("gpsimd", lambda e: e.memset(St[:, :], 0.0), w=[B("St")])
        S.op("gpsimd", lambda e: e.memset(St_bf[:, :], 0.0), w=[B("St_bf")])
        S.op("gpsimd", lambda e: e.memset(halo[:, :, :], 0.0), w=[B("halo")])

        XR_, AB, E1, L1, DT, DACS = None, 0, 1, 2, 3, 4
        DTA = AB
        Ebf = E_f[:, :].bitcast(BF16).rearrange("p (h l) -> p h l", h=16)
        EB = B("E_f")

        def f(i):
            return fm[0:16, i, :]

        def stageA(c):
            par = c % 2
            hTc, hB = hT[par], B("hT%d" % par)
            szc, szB = sz[par], B("sz%d" % par)
            dtxc, dtxB = dtx[par], B("dtx%d" % par)
            dramB = B("dram%d" % c)
            r0 = c * T
            xB = B("x_sb")
            xbc_f = xbc_fs[par]
            xbcT = xbc_f[:, :].bitcast(BF16).rearrange("p (j t) -> p j t", j=16)
            XB = B("xbc_f%d" % par)

            def A1():
                S.dma("sync", lambda e: e.dma_start(out=x_sb[:, :], in_=src_d[r0:r0 + T, :]), "xl", r=[dramB], w=[xB])
                S.op("scalar", lambda e: e.activation(out=xbf[:, :], in_=x_sb[:, :], func=AF.Square, accum_out=stc(SSQ)),
                     r=[xB], w=[B("xbf"), B("ssq")])
                S.op("scalar", lambda e: e.copy(out=stc(SCR0), in_=stc(SCR1)), r=[B("st_scr")], w=[B("ssq")])
                rsqrt(SSQ, RSTD, 1, 1.0 / D, B("ssq"), B("rstd"), MS)
                S.op("vector", lambda e: e.tensor_scalar(out=xbf[:, :], in0=x_sb[:, :], scalar1=stc(RSTD), scalar2=None, op0=ALU.mult),
                     r=[xB, B("rstd")], w=[B("xbf")])
                for k in range(8):
                    S.op("tensor", (lambda k: lambda e: e.transpose(out=pbf(0)[:, k * 128:(k + 1) * 128],
                                                                   in_=xbf[:, k * 128:(k + 1) * 128], identity=ident_bf[:, :]))(k),
                         r=[B("xbf"), B("ident_bf")], w=[pbB[0]])
                S.op("gpsimd", lambda e: e.tensor_copy(out=hTc[:, :, 0:3], in_=halo[:, :, :]), r=[B("halo")], w=[hB])
                S.op("vector", lambda e: e.tensor_tensor(out=hTc[:, :, 3:131], in0=pbf(0).rearrange("p (k t) -> p k t", k=8),
                                                         in1=AP(prm, 80, [[112, 128], [1, 8], [0, 128]]), op=ALU.mult),
                     r=[pbB[0], B("prm")], w=[hB])
                S.op("gpsimd", lambda e: e.tensor_copy(out=halo[:, :, :], in_=hTc[:, :, 128:131]), r=[hB], w=[B("halo")])

            def proj(bank, col0):
                for k in range(8):
                    S.op("tensor", (lambda k: lambda e: e.matmul(out=pb[bank][:, :], lhsT=hTc[:, k, 3:131],
                                                                 rhs=win[:, k, col0:col0 + 512], start=(k == 0), stop=(k == 7)))(k),
                         r=[hB, winB[k]], w=[pbB[bank]])

            def A2():
                proj(2, 1024)
                proj(3, 1536)
                S.op("vector", lambda e: e.bn_stats(out=bnst[:, 0, :], in_=pb[2][:, :]), r=[pbB[2]], w=[B("bnst")])
                S.op("vector", lambda e: e.bn_stats(out=bnst[:, 1, :], in_=pb[3][:, :]), r=[pbB[3]], w=[B("bnst")])
                S.op("vector", lambda e: e.bn_aggr(out=stc(MV, 2), in_=bnst[:, :, :].rearrange("p a b -> p (a b)")),
                     r=[B("bnst")], w=[B("mv")])
                S.op("vector", lambda e: e.tensor_scalar(out=stc(MSV), in0=stc(MV + 1), scalar1=EPS, scalar2=None, op0=ALU.add),
                     r=[B("mv")], w=[B("msv")])
                S.op("gpsimd", lambda e: e.tensor_tensor(out=stc(RSTDV), in0=stc(MSV), in1=mhalf[:, 0:1], op=ALU.pow),
                     r=[B("msv"), B("mhalf")], w=[B("rstdv")])
                S.op("vector", lambda e: e.scalar_tensor_tensor(out=stc(NMR), in0=stc(MV), scalar=-1.0, in1=stc(RSTDV),
                                                                op0=ALU.mult, op1=ALU.mult),
                     r=[B("mv"), B("rstdv")], w=[B("nmr")])
                for i, bank in enumerate((2, 3)):
                    S.op("scalar", (lambda i, bank: lambda e: e.activation(out=y_sb[:, i * 512:(i + 1) * 512], in_=pb[bank][:, :],
                                                                           func=AF.Identity, bias=stc(NMR), scale=stc(RSTDV)))(i, bank),
                         r=[pbB[bank], B("nmr"), B("rstdv")], w=[B("y_sb")])
                S.op("gpsimd", lambda e: e.tensor_tensor(out=vn_bf[:, :], in0=y_sb[:, :], in1=vw_bc[:, :], op=ALU.mult),
                     r=[B("y_sb"), B("vw_bc")], w=[B("yn_bf")])

            def A3():
                proj(2, 2048)
                proj(3, 2560)
                for i, bank in enumerate((2, 3)):
                    S.op("scalar", (lambda i, bank: lambda e: e.activation(out=g_sb[:, i * 512:(i + 1) * 512], in_=pb[bank][:, :],
                                                                           func=AF.Silu))(i, bank),
                         r=[pbB[bank]], w=[B("g_sb")])
                proj(2, 0)
                proj(3, 512)
                for i, bank in enumerate((2, 3)):
                    S.op("vector", (lambda i, bank: lambda e: e.tensor_tensor(out=g_sb[:, i * 512:(i + 1) * 512], in0=g_sb[:, i * 512:(i + 1) * 512],
                                                                              in1=pb[bank][:, :], op=ALU.mult))(i, bank),
                         r=[pbB[bank], B("g_sb")], w=[B("g_sb")])

            def A4():
                for h in range(8):
                    bk = 2 + h // 4
                    S.op("tensor", (lambda h, bk: lambda e: e.matmul(out=pb[bk][:, (h % 4) * 128:(h % 4 + 1) * 128], lhsT=wsT[:, h, :],
                                                                     rhs=vn_bf[:, h * 128:(h + 1) * 128], start=True, stop=True))(h, bk),
                         r=[B("wsT"), B("yn_bf")], w=[pbB[bk]])
                for q in range(2):
                    S.op("vector", (lambda q: lambda e: e.tensor_tensor(out=y_sb[:, q * 512:(q + 1) * 512], in0=pb[2 + q][:, :],
                                                                        in1=bias_t[:, q * 512:(q + 1) * 512], op=ALU.add))(q),
                         r=[pbB[2 + q], B("bias_t")], w=[B("y_sb")])
                S.op("vector", lambda e: e.tensor_tensor(out=y_sb[:, :], in0=y_sb[:, :], in1=g_sb[:, :], op=ALU.mult),
                     r=[B("y_sb"), B("g_sb")], w=[B("y_sb")])
                for h in range(8):
                    S.op("scalar", (lambda h: lambda e: e.activation(out=xbf[:, h * 128:(h + 1) * 128], in_=y_sb[:, h * 128:(h + 1) * 128],
                                                                     func=AF.Square, accum_out=stc(SSQG + h)))(h),
                         r=[B("y_sb")], w=[B("xbf"), B("ssqg")])
                S.op("scalar", lambda e: e.copy(out=stc(SCR0), in_=stc(SCR1)), r=[B("st_scr")], w=[B("ssqg")])
                rsqrt(SSQG, RSTDG, 8, 1.0 / 128, B("ssqg"), B("rstdg"), MSG)
                S.op("vector", lambda e: e.tensor_tensor(out=yn_bf[:, :].rearrange("p (h d) -> p h d", h=8),
                                                         in0=y_sb[:, :].rearrange("p (h d) -> p h d", h=8),
                                                         in1=AP(st, RSTDG, [[64, 128], [1, 8], [0, 128]]), op=ALU.mult),
                     r=[B("y_sb"), B("rstdg")], w=[B("yn_bf")])

            def A4b():
                for h in range(8):
                    S.op("tensor", (lambda h: lambda e: e.transpose(out=pbf(0)[:, h * 128:(h + 1) * 128],
                                                                   in_=yn_bf[:, h * 128:(h + 1) * 128], identity=ident_bf[:, :]))(h),
                         r=[B("yn_bf"), B("ident_bf")], w=[pbB[0]])
                S.op("scalar", lambda e: e.copy(out=yT[:, 0:8, :], in_=pbf(0).rearrange("p (k t) -> p k t", k=8)),
                     r=[pbB[0]], w=[B("yTa")])

            def A5():
                for k in range(8):
                    S.op("tensor", (lambda k: lambda e: e.matmul(out=pb[0][0:16, 0:128], lhsT=win[:, k, 6144:6160],
                                                                 rhs=hTc[:, k, 3:131], start=(k == 0), stop=(k == 7)))(k),
                         r=[hB, winB[k]], w=[pbB[0]])
                S.op("vector", lambda e: e.tensor_scalar(out=dtxc[0:16, :], in0=pb[0][0:16, 0:128], scalar1=dtb[0:16, :], scalar2=None,
                                                         op0=ALU.add),
                     r=[pbB[0], B("dtb")], w=[dtxB])
                proj(2, 3072)
                proj(3, 3584)
                for i, bank in enumerate((2, 3)):
                    S.op("scalar", (lambda i, bank: lambda e: e.activation(out=szc[:, i * 512:(i + 1) * 512], in_=pb[bank][:, :],
                                                                           func=AF.Silu))(i, bank),
                         r=[pbB[bank]], w=[szB])
            def B1(j0, j1):
                for j in range(j0, j1):
                    bank = 2 + (j % 2)
                    a_t, aB = acc[j % 2], B("acc%d" % (j % 2))
                    c0 = 4096 + j * 128
                    for k in range(8):
                        S.op("tensor", (lambda k, bank, c0: lambda e: e.matmul(out=pb[bank][:, 0:131], lhsT=win[:, k, c0:c0 + 128],
                                                                             rhs=hTc[:, k, 0:131], start=(k == 0), stop=(k == 7)))(k, bank, c0),
                             r=[hB, winB[k]], w=[pbB[bank]])
                    rb = j % 2
                    S.op("scalar", (lambda j, bank, a_t: lambda e: e.activation(out=a_t[:, :], in_=pb[bank][:, 3:131], func=AF.Identity,
                                                                              scale=prm[:, 48 + j:49 + j], bias=prm[:, 64 + j:65 + j]))(j, bank, a_t),
                         r=[pbB[bank], B("prm")], w=[aB])
                    S.op("scalar", (lambda bank, rb: lambda e: e.copy(out=pb[rb][:, 256:387], in_=pb[bank][:, 0:131]))(bank, rb),
                         r=[pbB[bank]], w=[pbB[rb]])
                    for kk in (2, 1, 0):
                        S.op("vector", (lambda j, rb, a_t, kk: lambda e: e.scalar_tensor_tensor(
                            out=a_t[:, :], in0=pb[rb][:, 256 + kk:256 + kk + 128], scalar=prm[:, 16 * kk + j:16 * kk + j + 1], in1=a_t[:, :],
                            op0=ALU.mult, op1=ALU.add))(j, rb, a_t, kk),
                             r=[pbB[rb], B("prm"), aB], w=[aB])
                    S.op("scalar", (lambda j, a_t: lambda e: e.activation(out=xbcT[:, j, :], in_=a_t[:, :], func=AF.Silu))(j, a_t),
                         r=[aB], w=[XB])

            return dict(A1=A1, A2=A2, A3=A3, A4a=A4, A5=A5, B1=B1, A4b=A4b)

        def stageB(c):
            par = c % 2
            hTc, hB = hT[par], B("hT%d" % par)
            szc, szB = sz[par], B("sz%d" % par)
            dtxc, dtxB = dtx[par], B("dtx%d" % par)
            dramB = B("dram%d" % c)
            r0 = c * T
            XR = dtxc[0:16, :]
            DTE = XR
            M1 = pbB[7]
            xbc_f = xbc_fs[par]
            xbcT = xbc_f[:, :].bitcast(BF16).rearrange("p (j t) -> p j t", j=16)
            XB = B("xbc_f%d" % par)
            t1 = E_f
            t2 = xbc_f

            def B2():
                S.op("vector", lambda e: e.scalar_tensor_tensor(out=f(AB), in0=XR, scalar=-1.0, in1=XR, op0=ALU.mult, op1=ALU.max),
                     r=[dtxB], w=[B("fAB")])
                S.op("scalar", lambda e: e.activation(out=f(E1), in_=f(AB), func=AF.Exp, scale=-1.0), r=[B("fAB")], w=[B("fE1")])
                S.op("scalar", lambda e: e.activation(out=f(L1), in_=f(E1), func=AF.Ln, bias=1.0), r=[B("fE1")], w=[B("fL1")])
                S.op("vector", lambda e: e.scalar_tensor_tensor(out=f(DT), in0=XR, scalar=0.0, in1=f(L1), op0=ALU.max, op1=ALU.add),
                     r=[dtxB, B("fL1")], w=[B("fDT")])
                S.op("vector", lambda e: e.tensor_scalar(out=f(DTA), in0=f(DT), scalar1=aneg[0:16, :], scalar2=None, op0=ALU.mult),
                     r=[B("fDT"), B("aneg")], w=[B("fAB")])
                S.op("vector", lambda e: e.memset(f(E1), 1.0), r=[B("fE1")], w=[B("fE1")])
                S.op("vector", lambda e: e.tensor_tensor_scan(out=f(DACS), data0=f(E1), data1=f(DTA), initial=0.0,
                                                              op0=ALU.mult, op1=ALU.add),
                     r=[B("fE1"), B("fAB")], w=[B("fDACS")])
                S.op("scalar", lambda e: e.activation(out=DTE, in_=f(DACS), func=AF.Exp, scale=-1.0, bias=fm[0:16, DACS, 127:128]),
                     r=[B("fDACS")], w=[dtxB])
                S.op("vector", lambda e: e.tensor_copy(out=dacs_hi[0:16, :], in_=f(DACS)), r=[B("fDACS")], w=[B("dacs_hi")])
                S.op("vector", lambda e: e.tensor_tensor(out=dacs_lo[0:16, :], in0=f(DACS), in1=dacs_hi[0:16, :], op=ALU.subtract),
                     r=[B("fDACS"), B("dacs_hi")], w=[B("dacs_lo")])
                S.op("scalar", lambda e: e.activation(out=fm[0:16, L1, 0:1], in_=fm[0:16, DACS, 127:128], func=AF.Exp),
                     r=[B("fDACS"), B("fL1")], w=[B("fL1")])
                S.op("vector", lambda e: e.tensor_scalar(out=diagcd[0:16, :], in0=ident_f[0:16, 0:16], scalar1=fm[0:16, L1, 0:1],
                                                         scalar2=None, op0=ALU.mult),
                     r=[B("fL1"), B("ident_f")], w=[B("diagcd")])
                srcs = (f(DT), f(DACS), DTE)
                for i in range(3):
                    S.op("tensor", (lambda i: lambda e: e.transpose(out=pb[7][:, 128 + 16 * i:128 + 16 * (i + 1)], in_=srcs[i],
                                                                   identity=ident_f[0:16, 0:16]))(i),
                         r=[B("fDT"), B("fDACS"), dtxB, B("ident_f")], w=[M1])
                S.op("tensor", lambda e: e.matmul(out=pb[7][:, 176:192], lhsT=AP(one16, 0, [[1, 128], [0, 128]]), rhs=diagcd[:, :], start=True, stop=True),
                     r=[B("one16"), B("diagcd")], w=[M1])
                S.op("vector", lambda e: e.tensor_copy(out=tm48[:, :], in_=pb[7][:, 128:176]), r=[M1], w=[B("tm48")])
                S.op("vector", lambda e: e.tensor_copy(out=tms[:, 3, :], in_=pb[7][:, 176:192]), r=[M1], w=[B("cd_bc")])
                S.op("vector", lambda e: e.tensor_scalar(out=tms[:, 0, :], in0=tm48[:, 16:32], scalar1=-1.0, scalar2=None, op0=ALU.mult),
                     r=[B("tm48")], w=[B("negdacs")])
                S.op("scalar", lambda e: e.activation(out=tms[:, 1, :], in_=tm48[:, 16:32], func=AF.Exp), r=[B("tm48")], w=[B("edacs")])
                S.op("vector", lambda e: e.tensor_tensor(out=tms[:, 2, :], in0=tm48[:, 0:16], in1=tm48[:, 32:48], op=ALU.mult),
                     r=[B("tm48")], w=[B("wdte")])

            def B3():
                for j in range(8):
                    S.op("tensor", (lambda j: lambda e: e.transpose(out=pbf(4)[:, j * 128:(j + 1) * 128], in_=xbcT[:, j, :],
                                                                   identity=ident_bf[:, :]))(j),
                         r=[XB, B("ident_bf")], w=[pbB[4]])
                S.op("vector", lambda e: e.tensor_copy(out=xs_tm[:, :], in_=pbf(4)), r=[pbB[4]], w=[B("xs_tm")])
                for g in range(4):
                    S.op("tensor", (lambda g: lambda e: e.transpose(out=pbf(7)[:, 512 + g * 128:512 + (g + 1) * 128], in_=xbcT[:, 8 + g, :],
                                                                   identity=ident_bf[:, :]))(g),
                         r=[XB, B("ident_bf")], w=[M1])
                S.op("scalar", lambda e: e.copy(out=B_tm[:, :], in_=pbf(7)[:, 512:1024]), r=[M1], w=[B("B_tm")])
                S.op("gpsimd", lambda e: e.tensor_tensor(out=xdt_bf[:, :].rearrange("p (h d) -> p h d", h=16),
                                                         in0=xs_tm[:, :].rearrange("p (h d) -> p h d", h=16),
                                                         in1=AP(tm48, 0, [[48, 128], [1, 16], [0, 64]]), op=ALU.mult),
                     r=[B("xs_tm"), B("tm48")], w=[B("xdt_bf")])
                S.op("gpsimd", lambda e: e.tensor_tensor(out=xdte_bf[:, :].rearrange("p (h d) -> p h d", h=16),
                                                         in0=xs_tm[:, :].rearrange("p (h d) -> p h d", h=16),
                                                         in1=AP(tms, 32, [[64, 128], [1, 16], [0, 64]]), op=ALU.mult),
                     r=[B("xs_tm"), B("wdte")], w=[B("xdte_bf")])

            def B4():
                for g in range(4):
                    S.op("tensor", (lambda g: lambda e: e.matmul(out=pb[6][:, g * 128:(g + 1) * 128], lhsT=xbcT[:, 8 + g, :],
                                                                 rhs=xbcT[:, 12 + g, :], start=True, stop=True))(g),
                         r=[XB], w=[pbB[6]])
                S.op("scalar", lambda e: e.copy(out=cbT_bf[:, :], in_=pb[6][:, :]), r=[pbB[6]], w=[B("cbT_bf")])
                for q in range(4):
                    bk = 4 + q % 2
                    S.op("tensor", (lambda bk: lambda e: e.matmul(out=pb[bk][:, :], lhsT=ident_bf[:, :], rhs=AP(negm, 0, [[128, 128], [0, 4], [1, 128]]),
                                                                  start=True, stop=False))(bk),
                         r=[B("ident_bf"), B("negm")], w=[pbB[bk]])
                    for r_ in range(4):
                        h = 4 * q + r_
                        for nm, src in (("dacs_hi", dacs_hi), ("dacs_lo", dacs_lo)):
                            S.op("tensor", (lambda bk, r_, h, src, nm: lambda e: e.matmul(
                                out=pb[bk][:, r_ * 128:(r_ + 1) * 128], lhsT=AP(ident_bf, h, [[128, 128], [0, 128]]), rhs=src[:, :],
                                start=False, stop=(nm == "dacs_lo" and r_ == 3)))(bk, r_, h, src, nm),
                                 r=[B("ident_bf"), B(nm)], w=[pbB[bk]])
                    for r_ in range(4):
                        h = 4 * q + r_
                        S.op("scalar", (lambda bk, r_, h: lambda e: e.activation(out=Ebf[:, h, :], in_=pb[bk][:, r_ * 128:(r_ + 1) * 128],
                                                                                func=AF.Exp, bias=tms[:, 0, h:h + 1]))(bk, r_, h),
                             r=[pbB[bk], B("negdacs")], w=[EB])
                S.op("vector", lambda e: e.tensor_tensor(out=Ebf.rearrange("p (g r) l -> p g r l", g=4),
                                                         in0=Ebf.rearrange("p (g r) l -> p g r l", g=4),
                                                         in1=AP(cbT_bf, 0, [[512, 128], [128, 4], [0, 4], [1, 128]]), op=ALU.mult),
                     r=[EB, B("cbT_bf")], w=[EB])

            def B5():
                for h in range(16):
                    bk = 4 + h // 8
                    S.op("tensor", (lambda h, bk: lambda e: e.matmul(out=pb[bk][:, (h % 8) * 64:(h % 8 + 1) * 64], lhsT=Ebf[:, h, :],
                                                                     rhs=xdt_bf[:, h * 64:(h + 1) * 64], start=True, stop=True))(h, bk),
                         r=[EB, B("xdt_bf")], w=[pbB[bk]])
                for g in range(4):
                    bk = 6 + g // 2
                    S.op("tensor", (lambda g, bk: lambda e: e.matmul(out=pb[bk][:, (g % 2) * 256:(g % 2 + 1) * 256], lhsT=xbcT[:, 12 + g, :],
                                                                     rhs=St_bf[:, g * 256:(g + 1) * 256], start=True, stop=True))(g, bk),
                         r=[XB, B("St_bf")], w=[pbB[bk]])
                for q in range(2):
                    S.op("vector", (lambda q: lambda e: e.tensor_tensor(out=t1[:, q * 512:(q + 1) * 512].rearrange("p (h d) -> p h d", h=8),
                                                                        in0=pb[6 + q][:, :].rearrange("p (h d) -> p h d", h=8),
                                                                        in1=AP(tms, 16 + 8 * q, [[64, 128], [1, 8], [0, 64]]), op=ALU.mult))(q),
                         r=[pbB[6 + q], B("edacs")], w=[EB])
                    S.op("vector", (lambda q: lambda e: e.tensor_tensor(out=t1[:, q * 512:(q + 1) * 512], in0=t1[:, q * 512:(q + 1) * 512],
                                                                        in1=pb[4 + q][:, :], op=ALU.add))(q),
                         r=[pbB[4 + q], EB], w=[EB])
                for g in range(4):
                    bk = 6 + g // 2
                    S.op("tensor", (lambda g, bk: lambda e: e.matmul(out=pb[bk][:, (g % 2) * 256:(g % 2 + 1) * 256],
                                                                     lhsT=B_tm[:, g * 128:(g + 1) * 128],
                                                                     rhs=xdte_bf[:, g * 256:(g + 1) * 256], start=True, stop=True))(g, bk),
                         r=[B("B_tm"), B("xdte_bf")], w=[pbB[bk]])
                S.op("gpsimd", lambda e: e.tensor_tensor(out=t2[:, :].rearrange("p (h d) -> p h d", h=16),
                                                         in0=xs_tm[:, :].rearrange("p (h d) -> p h d", h=16),
                                                         in1=AP(d_bc, 0, [[16, 128], [1, 16], [0, 64]]), op=ALU.mult),
                     r=[B("xs_tm"), B("d_bc"), XB], w=[XB])
                S.op("gpsimd", lambda e: e.tensor_tensor(out=t1[:, :], in0=t1[:, :], in1=t2[:, :], op=ALU.add), r=[EB, XB], w=[EB])
                S.op("vector", lambda e: e.tensor_tensor(out=t1[:, :], in0=t1[:, :], in1=szc[:, :], op=ALU.mult), r=[EB, szB], w=[EB])
                S.op("vector", lambda e: e.tensor_tensor(out=St[:, :].rearrange("p (h d) -> p h d", h=16),
                                                         in0=St[:, :].rearrange("p (h d) -> p h d", h=16),
                                                         in1=AP(tms, 48, [[64, 128], [1, 16], [0, 64]]), op=ALU.mult),
                     r=[B("St"), B("cd_bc")], w=[B("St")])
                for q in range(2):
                    S.op("vector", (lambda q: lambda e: e.tensor_tensor(out=St[:, q * 512:(q + 1) * 512], in0=St[:, q * 512:(q + 1) * 512],
                                                                        in1=pb[6 + q][:, :], op=ALU.add))(q),
                         r=[B("St"), pbB[6 + q]], w=[B("St")])
                S.op("scalar", lambda e: e.copy(out=St_bf[:, :], in_=St[:, :]), r=[B("St")], w=[B("St_bf")])

            def B6():
                for g in range(4):
                    S.op("scalar", (lambda g: lambda e: e.activation(out=xdt_bf[:, g * 256:(g + 1) * 256], in_=t1[:, g * 256:(g + 1) * 256],
                                                                     func=AF.Square, accum_out=stc(SSQB + g)))(g),
                         r=[EB], w=[B("xdt_bf"), B("ssqb")])
                S.op("scalar", lambda e: e.copy(out=stc(SCR0), in_=stc(SCR1)), r=[B("st_scr")], w=[B("ssqb")])
                rsqrt(SSQB, RSTDB, 4, 1.0 / 256, B("ssqb"), B("rstdb"), MSB)
                S.op("vector", lambda e: e.tensor_tensor(out=xs_tm[:, :].rearrange("p (g d) -> p g d", g=4),
                                                         in0=t1[:, :].rearrange("p (g d) -> p g d", g=4),
                                                         in1=AP(st, RSTDB, [[64, 128], [1, 4], [0, 256]]), op=ALU.mult),
                     r=[EB, B("rstdb")], w=[B("xs_tm")])
                for h in range(8):
                    S.op("tensor", (lambda h: lambda e: e.transpose(out=pbf(5)[:, h * 128:(h + 1) * 128],
                                                                   in_=xs_tm[:, h * 128:(h + 1) * 128], identity=ident_bf[:, :]))(h),
                         r=[B("xs_tm"), B("ident_bf")], w=[pbB[5]])
                S.op("scalar", lambda e: e.copy(out=yT[:, 8:16, :], in_=pbf(5).rearrange("p (k t) -> p k t", k=8)),
                     r=[pbB[5]], w=[B("yTb")])

            def B7():
                S.dma("sync", lambda e: e.dma_start(out=t2[:, :], in_=src_d[r0:r0 + T, :]), "xr", r=[dramB], w=[XB])
                for b in range(2):
                    for k in range(16):
                        S.op("tensor", (lambda b, k: lambda e: e.matmul(out=pb[4 + b][:, :], lhsT=yT[:, k, :],
                                                                        rhs=wout[:, k, b * 512:(b + 1) * 512], start=(k == 0), stop=(k == 15)))(b, k),
                             r=[B("yTa"), B("yTb"), B("wout")], w=[pbB[4 + b]])
                for b in range(2):
                    S.op("scalar", (lambda b: lambda e: e.activation(out=xdt_bf[:, b * 512:(b + 1) * 512], in_=pb[4 + b][:, :],
                                                                     func=AF.Square, accum_out=stc(SSQM + b)))(b),
                         r=[pbB[4 + b]], w=[B("xdt_bf"), B("ssqm")])
                S.op("scalar", lambda e: e.copy(out=stc(SCR0), in_=stc(SCR1)), r=[B("st_scr")], w=[B("ssqm")])
                S.op("vector", lambda e: e.tensor_tensor(out=stc(SSQM), in0=stc(SSQM), in1=stc(SSQM + 1), op=ALU.add),
                     r=[B("ssqm")], w=[B("ssqm")])
                rsqrt(SSQM, RSTDM, 1, 1.0 / D, B("ssqm"), B("rstdm"), MSM)
                for b in range(2):
                    S.op("vector", (lambda b: lambda e: e.scalar_tensor_tensor(out=t1[:, b * 512:(b + 1) * 512], in0=pb[4 + b][:, :],
                                                                               scalar=stc(RSTDM), in1=post_bc[:, b * 512:(b + 1) * 512],
                                                                               op0=ALU.mult, op1=ALU.mult))(b),
                         r=[pbB[4 + b], B("rstdm"), B("post_bc")], w=[EB])
                S.op("gpsimd", lambda e: e.tensor_tensor(out=t1[:, :], in0=t1[:, :], in1=t2[:, :], op=ALU.add), r=[EB, XB], w=[EB])
                S.dma("gpsimd", lambda e: e.dma_start(out=dst_d[r0:r0 + T, :], in_=t1[:, :]), "xs", r=[EB], w=[dramB])
            return [B2, B3, B4, B5, B6, B7]

        def Aprime(As):
            b1 = As["B1"]
            return S.record([As["A1"], lambda: b1(0, 4), As["A2"], lambda: b1(4, 8), As["A3"], lambda: b1(8, 12), As["A5"],
                             lambda: b1(12, 16), As["A4a"]])

        As = stageA(0)
        for o in Aprime(As):
            S.play(o)
        for c in range(NCH):
            Bs = stageB(c)
            X = S.record(Bs[:5])
            xbar = len(X)
            X += S.record(Bs[5:])
            Y = S.record([As["A4b"]])
            ybar = len(Y)
            An = stageA(c + 1) if c + 1 < NCH else None
            if An is not None:
                Y += Aprime(An)
            elif l + 1 < NL:
                load_win(l + 1)
            S.merge(X, Y, ybar, xbar)
            As = An

    for l_ in range(NL):
        do_layer(l_, x_d if l_ == 0 else xbuf_d, out_d if l_ == NL - 1 else xbuf_d)
    global LAST_S
    LAST_S = S
    S.emit(nc, es)
    es.close()
    return nc


_CONSTS = None


def _consts():
    global _CONSTS
    if _CONSTS is None:
        ident = np.eye(T, dtype=np.float32)
        tril = np.tril(np.ones((T, T), np.float32))
        sidx = np.arange(T)[:, None]
        lidx = np.arange(T)[None, :]
        negm1 = np.where(sidx <= lidx, 0.0, NEG).astype(np.float32)
        negm = np.tile(negm1, (1, 4))
        sel = np.zeros((T, 16, T), np.float32)
        for h in range(16):
            sel[h, h, :] = 1.0
        one16 = np.zeros((T, T), np.float32)
        one16[0:16, :] = 1.0
        _CONSTS = dict(c_ident=ident, c_tril=tril, c_negm=negm, c_sel=sel.reshape(T, 16 * T), c_one16=one16)
    return _CONSTS


_PROG = {}


def _get_prog(NL, NCH):
    key = (NL, NCH)
    if key not in _PROG:
        _PROG[key] = build_program(NL, NCH)
    return _PROG[key]


PARAMS = ["pre_norm_w", "w_in", "gmlp_v_norm_w", "gmlp_v_norm_b", "gmlp_ws", "gmlp_bs", "gmlp_norm_w",
          "conv_w", "conv_b", "dt_bias", "a_log", "d_skip", "ssd_norm_w", "w_out", "post_norm_w"]

FUSED = True


def kernel(**inputs):
    x = np.ascontiguousarray(np.asarray(inputs["x"], dtype=np.float32))
    Bn, Ls, _ = x.shape
    NCH = Ls // T
    depth = inputs["w_in"].shape[0]
    par = {k: np.ascontiguousarray(np.asarray(inputs[k], dtype=np.float32)) for k in PARAMS}
    cs = _consts()
    if FUSED:
        nc = _get_prog(depth, NCH)
        in_maps = [dict(x=x[b], **par, **cs) for b in range(Bn)]
        res = run_bass_kernel_spmd(nc, in_maps, core_ids=list(range(Bn)))
        return np.stack([np.asarray(r["out"]) for r in res.results], axis=0).astype(np.float32)
    cur = [x[b] for b in range(Bn)]
    nc = _get_prog(1, NCH)
    for l in range(depth):
        pl = {k: np.ascontiguousarray(v[l:l + 1]) for k, v in par.items()}
        in_maps = [dict(x=cur[b], **pl, **cs) for b in range(Bn)]
        res = run_bass_kernel_spmd(nc, in_maps, core_ids=list(range(Bn)))
        cur = [np.ascontiguousarray(np.asarray(r["out"], dtype=np.float32)) for r in res.results]
    return np.stack(cur, axis=0).astype(np.float32)
```
